# Optimizing a Trainium2 kernel written in Bass

```python
import jax, jax.numpy as jnp
from jax import lax
import numpy as np

D_MODEL = 1024
BATCH = 32
SEQ = 2048
DEPTH = 1
DEC_BATCH = 16
DEC_SEQ = 32
PAST_LEN = 1024

CHUNK = 64
CONV_W = 4
D_LRU = D_MODEL
LRU_HEADS = 8
LRU_BLOCK = D_LRU // LRU_HEADS
LRU_C = 8.0
D_SSD = D_MODEL
SSD_HEADS = 16
SSD_HEAD_DIM = D_SSD // SSD_HEADS
SSD_GROUPS = 2
SSD_HPG = SSD_HEADS // SSD_GROUPS
SSD_STATE = 128
D_XBC = D_SSD + 2 * SSD_GROUPS * SSD_STATE
D_MIX = D_LRU + D_SSD
IN_COLS = 2 * D_LRU + D_XBC + D_SSD + SSD_HEADS
EPS = 1e-6

kernel_name = "hawk_ssd_parallel_streaming_encoder_step"


def rmsnorm(x, g):
    xf = x.astype(jnp.float32)
    y = xf * lax.rsqrt(jnp.mean(xf * xf, axis=-1, keepdims=True) + EPS)
    return (y * g.astype(jnp.float32)).astype(x.dtype)


def causal_conv(u, buf, w, b):
    L = u.shape[1]
    ext = jnp.concatenate([buf.astype(u.dtype), u], axis=1)
    out = b + sum(ext[:, k:k + L] * w[k] for k in range(CONV_W))
    return out, ext[:, L:]


def block_diag(u, w, b):
    Bsz, L, _ = u.shape
    uh = u.reshape(Bsz, L, LRU_HEADS, LRU_BLOCK)
    return jnp.einsum('blhi,hij->blhj', uh, w).reshape(Bsz, L, D_LRU) + b


def rg_lru(u, h0, w_a, b_a, w_x, b_x, lam, start):
    L = u.shape[1]
    uf = u.astype(jnp.float32)
    r = jax.nn.sigmoid(block_diag(uf, w_a.astype(jnp.float32), b_a.astype(jnp.float32)))
    i = jax.nn.sigmoid(block_diag(uf, w_x.astype(jnp.float32), b_x.astype(jnp.float32)))
    log_a = -LRU_C * r * jax.nn.softplus(-lam.astype(jnp.float32))
    a = jnp.exp(log_a)
    mult = jnp.sqrt(-jnp.expm1(2.0 * log_a))
    reset = (start + jnp.arange(L)) == 0
    mult = jnp.where(reset[None, :, None], 1.0, mult)
    bv = mult * (i * uf)
    bv = bv.at[:, 0].add(a[:, 0] * h0.astype(jnp.float32))

    def combine(p, q):
        return p[0] * q[0], q[0] * p[1] + q[1]

    _, h = lax.associative_scan(combine, (a, bv), axis=1)
    return h, h[:, -1]


def segsum(x):
    T = x.shape[-1]
    xr = jnp.broadcast_to(x[..., :, None], x.shape + (T,))
    xr = jnp.where(jnp.tril(jnp.ones((T, T), bool), -1), xr, 0.0)
    s = jnp.cumsum(xr, axis=-2)
    return jnp.where(jnp.tril(jnp.ones((T, T), bool)), s, -jnp.inf)


def ssd_scan(x, dA, Bm, Cm, s0, q):
    b_, l = x.shape[:2]
    c = l // q
    x = x.reshape(b_, c, q, SSD_GROUPS, SSD_HPG, SSD_HEAD_DIM)
    Bm = Bm.reshape(b_, c, q, SSD_GROUPS, SSD_STATE)
    Cm = Cm.reshape(b_, c, q, SSD_GROUPS, SSD_STATE)
    A = dA.reshape(b_, c, q, SSD_GROUPS, SSD_HPG).transpose(0, 3, 4, 1, 2)
    A_cs = jnp.cumsum(A, axis=-1)
    Lm = jnp.exp(segsum(A))
    CB = jnp.einsum('bclgn,bcsgn->bcgls', Cm, Bm)
    y_diag = jnp.einsum('bcgls,bgkcls,bcsgkp->bclgkp', CB, Lm, x)
    decay_states = jnp.exp(A_cs[..., -1:] - A_cs)
    states = jnp.einsum('bcsgn,bgkcs,bcsgkp->bcgkpn', Bm, decay_states, x)
    states = jnp.concatenate([s0[:, None], states], axis=1)
    chunk_A = jnp.pad(A_cs[..., -1], ((0, 0), (0, 0), (0, 0), (1, 0)))
    decay_chunk = jnp.exp(segsum(chunk_A))
    new_states = jnp.einsum('bgkzc,bcgkpn->bzgkpn', decay_chunk, states)
    start_states, final = new_states[:, :-1], new_states[:, -1]
    y_off = jnp.einsum('bclgn,bcgkpn,bgkcl->bclgkp', Cm, start_states, jnp.exp(A_cs))
    y = (y_diag + y_off).reshape(b_, l, SSD_GROUPS, SSD_HPG, SSD_HEAD_DIM)
    return y, final


def mixer_layer(x, c, buf_lru, h_lru, buf_ssd, s_ssd, start, q, p):
    Bsz, L, _ = x.shape
    mod = jax.nn.silu(c) @ p['w_ada'] + p['b_ada']
    shift, scale, gate = jnp.split(mod, 3, axis=-1)
    hn = rmsnorm(x, p['norm_g']) * (1.0 + scale[:, None]) + shift[:, None]
    proj = hn @ p['w_in']
    lru_x, lru_g, xbc, z, dt_raw = jnp.split(
        proj, [D_LRU, 2 * D_LRU, 2 * D_LRU + D_XBC, 2 * D_LRU + D_XBC + D_SSD], axis=-1)

    u, new_buf_lru = causal_conv(lru_x, buf_lru, p['lru_conv_w'], p['lru_conv_b'])
    h, new_h = rg_lru(u, h_lru, p['lru_w_a'], p['lru_b_a'], p['lru_w_x'], p['lru_b_x'],
                      p['lru_lambda'], start)
    y_lru = h.astype(x.dtype) * jax.nn.silu(lru_g)

    xbc_c, new_buf_ssd = causal_conv(xbc, buf_ssd, p['ssd_conv_w'], p['ssd_conv_b'])
    xbc_c = jax.nn.silu(xbc_c).astype(jnp.float32)
    xs, Bm, Cm = jnp.split(xbc_c, [D_SSD, D_SSD + SSD_GROUPS * SSD_STATE], axis=-1)
    xs = xs.reshape(Bsz, L, SSD_GROUPS, SSD_HPG, SSD_HEAD_DIM)
    Bm = Bm.reshape(Bsz, L, SSD_GROUPS, SSD_STATE)
    Cm = Cm.reshape(Bsz, L, SSD_GROUPS, SSD_STATE)
    dt = jax.nn.softplus(dt_raw.astype(jnp.float32) + p['ssd_dt_bias'].astype(jnp.float32))
    dt = dt.reshape(Bsz, L, SSD_GROUPS, SSD_HPG)
    A = -jnp.exp(p['ssd_a_log'].astype(jnp.float32)).reshape(SSD_GROUPS, SSD_HPG)
    y, new_s = ssd_scan(xs * dt[..., None], dt * A, Bm, Cm, s_ssd.astype(jnp.float32), q)
    y = y + p['ssd_d'].astype(jnp.float32).reshape(SSD_GROUPS, SSD_HPG)[:, :, None] * xs
    yz = y.reshape(Bsz, L, D_SSD) * jax.nn.silu(z.astype(jnp.float32))
    yz = yz.reshape(Bsz, L, SSD_GROUPS, D_SSD // SSD_GROUPS)
    y_ssd = rmsnorm(yz, p['ssd_norm_g'].reshape(SSD_GROUPS, D_SSD // SSD_GROUPS))
    y_ssd = y_ssd.reshape(Bsz, L, D_SSD).astype(x.dtype)

    mix = jnp.concatenate([y_lru, y_ssd], axis=-1) @ p['w_out']
    out = x + gate[:, None] * mix
    return (out, new_buf_lru, new_h.astype(x.dtype), new_buf_ssd, new_s.astype(x.dtype))


def setup_inputs(seed: int = 0) -> dict:
    key = jax.random.key(seed)
    ks = jax.random.split(key, 32)
    nrm = lambda k, s, sc: jax.random.normal(k, s, jnp.float32) * sc
    a0 = jax.random.uniform(ks[20], (DEPTH, D_LRU), jnp.float32, 0.9, 0.999)
    dt0 = jnp.exp(jax.random.uniform(ks[21], (DEPTH, SSD_HEADS), jnp.float32,
                                     np.log(1e-3), np.log(1e-1)))
    return {
        'x_prompt': nrm(ks[0], (BATCH, SEQ, D_MODEL), 1.0),
        'x_sample': nrm(ks[1], (DEC_BATCH, DEC_SEQ, D_MODEL), 1.0),
        'c_prompt': nrm(ks[2], (BATCH, D_MODEL), 1.0),
        'c_sample': nrm(ks[3], (DEC_BATCH, D_MODEL), 1.0),
        'state_lru_conv': nrm(ks[4], (DEPTH, DEC_BATCH, CONV_W - 1, D_LRU), 1.0),
        'state_lru_h': nrm(ks[5], (DEPTH, DEC_BATCH, D_LRU), 1.0),
        'state_ssd_conv': nrm(ks[6], (DEPTH, DEC_BATCH, CONV_W - 1, D_XBC), 1.0),
        'state_ssd': nrm(ks[7], (DEPTH, DEC_BATCH, SSD_GROUPS, SSD_HPG, SSD_HEAD_DIM, SSD_STATE), 0.1),
        'norm_g': 1.0 + nrm(ks[8], (DEPTH, D_MODEL), 0.02),
        'w_ada': nrm(ks[9], (DEPTH, D_MODEL, 3 * D_MODEL), 0.5 * D_MODEL ** -0.5),
        'b_ada': nrm(ks[10], (DEPTH, 3 * D_MODEL), 0.01),
        'w_in': nrm(ks[11], (DEPTH, D_MODEL, IN_COLS), D_MODEL ** -0.5),
        'lru_conv_w': nrm(ks[12], (DEPTH, CONV_W, D_LRU), CONV_W ** -0.5),
        'lru_conv_b': nrm(ks[13], (DEPTH, D_LRU), 0.01),
        'lru_w_a': nrm(ks[14], (DEPTH, LRU_HEADS, LRU_BLOCK, LRU_BLOCK), LRU_BLOCK ** -0.5),
        'lru_b_a': nrm(ks[15], (DEPTH, D_LRU), 0.01),
        'lru_w_x': nrm(ks[16], (DEPTH, LRU_HEADS, LRU_BLOCK, LRU_BLOCK), LRU_BLOCK ** -0.5),
        'lru_b_x': nrm(ks[17], (DEPTH, D_LRU), 0.01),
        'lru_lambda': jnp.log(a0) - jnp.log1p(-a0),
        'ssd_conv_w': nrm(ks[18], (DEPTH, CONV_W, D_XBC), CONV_W ** -0.5),
        'ssd_conv_b': nrm(ks[19], (DEPTH, D_XBC), 0.01),
        'ssd_dt_bias': dt0 + jnp.log(-jnp.expm1(-dt0)),
        'ssd_a_log': jnp.log(jax.random.uniform(ks[22], (DEPTH, SSD_HEADS), jnp.float32, 1.0, 16.0)),
        'ssd_d': 1.0 + nrm(ks[23], (DEPTH, SSD_HEADS), 0.1),
        'ssd_norm_g': 1.0 + nrm(ks[24], (DEPTH, D_SSD), 0.02),
        'w_out': nrm(ks[25], (DEPTH, D_MIX, D_MODEL), D_MIX ** -0.5),
        'final_norm_g': 1.0 + nrm(ks[26], (D_MODEL,), 0.02),
    }


def reference(x_prompt, x_sample, c_prompt, c_sample, state_lru_conv, state_lru_h,
              state_ssd_conv, state_ssd, norm_g, w_ada, b_ada, w_in, lru_conv_w, lru_conv_b,
              lru_w_a, lru_b_a, lru_w_x, lru_b_x, lru_lambda, ssd_conv_w, ssd_conv_b,
              ssd_dt_bias, ssd_a_log, ssd_d, ssd_norm_g, w_out, final_norm_g):
    xp, xs = x_prompt, x_sample
    bp = xp.shape[0]
    lc_p, lh_p, sc_p, ss_p = [], [], [], []
    lc_s, lh_s, sc_s, ss_s = [], [], [], []
    for l in range(DEPTH):
        p = {'norm_g': norm_g[l], 'w_ada': w_ada[l], 'b_ada': b_ada[l], 'w_in': w_in[l],
             'lru_conv_w': lru_conv_w[l], 'lru_conv_b': lru_conv_b[l],
             'lru_w_a': lru_w_a[l], 'lru_b_a': lru_b_a[l], 'lru_w_x': lru_w_x[l],
             'lru_b_x': lru_b_x[l], 'lru_lambda': lru_lambda[l],
             'ssd_conv_w': ssd_conv_w[l], 'ssd_conv_b': ssd_conv_b[l],
             'ssd_dt_bias': ssd_dt_bias[l], 'ssd_a_log': ssd_a_log[l], 'ssd_d': ssd_d[l],
             'ssd_norm_g': ssd_norm_g[l], 'w_out': w_out[l]}
        xp, a1, a2, a3, a4 = mixer_layer(
            xp, c_prompt,
            jnp.zeros((bp, CONV_W - 1, D_LRU), xp.dtype), jnp.zeros((bp, D_LRU), xp.dtype),
            jnp.zeros((bp, CONV_W - 1, D_XBC), xp.dtype),
            jnp.zeros((bp, SSD_GROUPS, SSD_HPG, SSD_HEAD_DIM, SSD_STATE), xp.dtype),
            0, CHUNK, p)
        xs, b1, b2, b3, b4 = mixer_layer(
            xs, c_sample, state_lru_conv[l], state_lru_h[l], state_ssd_conv[l], state_ssd[l],
            PAST_LEN, xs.shape[1], p)
        lc_p.append(a1); lh_p.append(a2); sc_p.append(a3); ss_p.append(a4)
        lc_s.append(b1); lh_s.append(b2); sc_s.append(b3); ss_s.append(b4)
    y_prompt = rmsnorm(xp, final_norm_g)
    y_sample = rmsnorm(xs, final_norm_g)
    return (y_prompt, y_sample,
            jnp.stack(lc_p), jnp.stack(lh_p), jnp.stack(sc_p), jnp.stack(ss_p),
            jnp.stack(lc_s), jnp.stack(lh_s), jnp.stack(sc_s), jnp.stack(ss_s))
```

```python
import math
from contextlib import ExitStack

import numpy as np
import concourse.bass as bass
import concourse.mybir as mybir
from concourse.bass_utils import run_bass_kernel_spmd

F32 = mybir.dt.float32
BF16 = mybir.dt.bfloat16
ALU = mybir.AluOpType
AF = mybir.ActivationFunctionType

P = 128
D = 1024
KC = 8
NCORES = 8
D_XBC = 1536
IN_COLS = 4624
EPS = 1e-6
OFF_LX, OFF_LG, OFF_XBC, OFF_Z, OFF_DT = 0, 1024, 2048, 3584, 4608

ENGS = ("pe", "act", "dve", "pool", "sp")


class Buf:
    __slots__ = ("name", "w", "r", "dsem", "dcnt", "excl")

    def __init__(self, name, excl=False):
        self.name = name
        self.excl = excl
        self.w = None
        self.r = {}
        self.dsem = None
        self.dcnt = 0


class Sched:
    def __init__(self, nc, stack):
        self.nc = nc
        self.stack = stack
        self.q = {e: [] for e in ENGS}
        self.esem = {}
        for e in ENGS:
            if e != "sp":
                self.esem[e] = stack.enter_context(nc.semaphore("es_" + e))
        self.ecnt = {e: 0 for e in ENGS}
        self.known = {e: {} for e in ENGS}
        self.nd = 0

    def _waits(self, eng, toks):
        need = {}
        for t in toks:
            if t is None:
                continue
            sem, val = t
            k = id(sem)
            if self.known[eng].get(k, 0) >= val:
                continue
            if k not in need or need[k][1] < val:
                need[k] = (sem, val)
        for k, (sem, val) in need.items():
            self.known[eng][k] = val
        return list(need.values())

    @staticmethod
    def _deps(reads, writes):
        toks = []
        for b in reads:
            toks.append(b.w)
        for b in writes:
            toks.append(b.w)
            toks.extend(b.r.values())
        return toks

    @staticmethod
    def _commit(tok, reads, writes):
        k = id(tok[0])
        for b in reads:
            b.r[k] = tok
        for b in writes:
            b.w = tok
            b.r = {}

    def op(self, eng, emit, reads=(), writes=()):
        if any(x.excl for x in reads):
            writes = list(writes) + [x for x in reads if x.excl and x not in writes]
            reads = [x for x in reads if not x.excl]
        waits = self._waits(eng, self._deps(reads, writes))
        self.ecnt[eng] += 1
        tok = (self.esem[eng], self.ecnt[eng])
        self.q[eng].append((waits, emit, self.esem[eng]))
        self._commit(tok, reads, writes)
        return tok

    def dma(self, eng, parts, reads=(), writes=(), owner=None, **kw):
        if owner is None:
            owner = writes[0] if writes else reads[0]
        if owner.dsem is None:
            self.nd += 1
            owner.dsem = self.stack.enter_context(self.nc.semaphore("ds%d" % self.nd))
        waits = self._waits(eng, self._deps(reads, writes))
        sem = owner.dsem
        owner.dcnt += 16 * len(parts)
        tok = (sem, owner.dcnt)

        def emit(e, parts=parts, sem=sem, kw=kw):
            for (o, i) in parts:
                e.dma_start(out=o, in_=i, **kw).then_inc(sem, 16)
            return None
        self.q[eng].append((waits, emit, None))
        self._commit(tok, reads, writes)
        return tok

    def barrier(self, extra=()):
        toks = [(self.esem[x], self.ecnt[x]) for x in self.esem if self.ecnt[x] > 0] + list(extra)
        for e in ENGS:
            w = self._waits(e, toks)
            if w:
                self.q[e].append((w, None, None))

    def finish(self, eng, toks):
        self.q[eng].append((self._waits(eng, toks), None, None))

    def run(self):
        nc, q = self.nc, self.q

        def play(e, items):
            for waits, emit, inc in items:
                for sem, val in waits:
                    e.wait_ge(sem, val)
                if emit is None:
                    continue
                ins = emit(e)
                if inc is not None:
                    ins.then_inc(inc, 1)

        with nc.Block() as block:
            @block.tensor
            def _(e):
                play(e, q["pe"])

            @block.scalar
            def _(e):
                play(e, q["act"])

            @block.vector
            def _(e):
                play(e, q["dve"])

            @block.gpsimd
            def _(e):
                play(e, q["pool"])

            @block.sync
            def _(e):
                play(e, q["sp"])


class _Stop(Exception):
    pass


DBG_STOP = None
DBG_REV = False
DBG_BARRIER = False
DBG_H = 0
DBG_TI = 0


_MARKS = {}


def mark(n):
    if DBG_STOP is None:
        return
    _MARKS[n] = _MARKS.get(n, 0) + 1
    if DBG_STOP == n or DBG_STOP == "%s:%d" % (n, _MARKS[n]):
        raise _Stop()


def build_program(NP, LP, NS, LS, T=256):
    NSEQ = NP + NS
    nc = bass.Bass("TRN2", target_bir_lowering=False)

    def din(name, shape):
        return nc.dram_tensor(name, list(shape), F32, kind="ExternalInput").ap()

    def dout(name, shape):
        return nc.dram_tensor(name, list(shape), F32, kind="ExternalOutput").ap()

    xp = din("xp", [NP, LP, D])
    xs_in = din("xs", [NS, LS, D])
    cc = din("cc", [NSEQ, D])
    st_lc = din("st_lc", [NS, 3, D])
    st_lh = din("st_lh", [NS, D])
    st_sc = din("st_sc", [NS, 3, D_XBC])
    st_ss = din("st_ss", [NS, D, P])
    norm_g = din("norm_g", [D])
    w_ada = din("w_ada", [D, 3 * D])
    b_ada = din("b_ada", [3 * D])
    w_in = din("w_in", [D, IN_COLS])
    lru_conv_w = din("lru_conv_w", [4, D])
    lru_conv_b = din("lru_conv_b", [D])
    lru_w_a = din("lru_w_a", [8, P, P])
    lru_b_a = din("lru_b_a", [D])
    lru_w_x = din("lru_w_x", [8, P, P])
    lru_b_x = din("lru_b_x", [D])
    lru_lambda = din("lru_lambda", [D])
    ssd_conv_w = din("ssd_conv_w", [4, D_XBC])
    ssd_conv_b = din("ssd_conv_b", [D_XBC])
    ssd_dt_bias = din("ssd_dt_bias", [16])
    ssd_a_log = din("ssd_a_log", [16])
    ssd_d = din("ssd_d", [16])
    ssd_norm_g = din("ssd_norm_g", [D])
    w_out = din("w_out", [2 * D, D])
    final_norm_g = din("final_norm_g", [D])
    consts = din("consts", [P, 4 * P])

    yp = dout("yp", [NP, LP, D])
    ys = dout("ys", [NS, LS, D])
    o_lc = [dout("o_lc_p", [NP, 3, D]), dout("o_lc_s", [NS, 3, D])]
    o_lh = [dout("o_lh_p", [NP, D]), dout("o_lh_s", [NS, D])]
    o_sc = [dout("o_sc_p", [NP, 3, D_XBC]), dout("o_sc_s", [NS, 3, D_XBC])]
    o_ss = [dout("o_ss_p", [NP, D, P]), dout("o_ss_s", [NS, D, P])]

    with ExitStack() as st:
        S = Sched(nc, st)
        stores = []

        def sb(name, shape, dt=F32):
            return st.enter_context(nc.sbuf_tensor(name, list(shape), dt))

        Win = sb("Win", [P, KC, IN_COLS], BF16)
        Wout = sb("Wout", [P, 16, D], BF16)
        Wa = sb("Wa", [P, 8, P], BF16)
        Wx = sb("Wx", [P, 8, P], BF16)
        cst = sb("cst", [P, 4 * P])
        identb = sb("identb", [P, P], BF16)
        NCOL = 8 + 32 + 8 + 8 + 8 + 8 + 48 + 12 + 8 + 16
        colp = sb("colp", [P, NCOL])
        dcol = sb("dcol", [P, 64])
        rows = sb("rows", [P, 64])
        fng = sb("fng", [P, D])
        gate = sb("gate", [P, D])
        modg = sb("modg", [8, D])
        cT = sb("cT", [P, KC, NSEQ])
        cTb = sb("cTb", [P, KC, NSEQ], BF16)
        modc = sb("modc", [P, 16, NSEQ])
        gsc = sb("gsc", [P, KC, NSEQ])

        xin = [sb("xin0", [P, D]), sb("xin1", [P, D])]
        junk = sb("junk", [P, D], BF16)
        ssq = sb("ssq", [P, 16])
        xn = sb("xn", [P, D], BF16)
        hnT = sb("hnT", [P, KC, T], BF16)
        lx = sb("lx", [P, T + 3])
        u = sb("u", [P, T])
        ub = sb("ub", [P, T], BF16)
        ga = sb("ga", [P, T])
        aa = sb("aa", [P, T])
        gm = sb("gm", [P, T])
        gx = sb("gx", [P, T])
        gb = sb("gb", [P, T])
        hh = sb("hh", [P, T])
        gg = sb("gg", [P, T])
        ylru = sb("ylru", [P, 8, T], BF16)
        xsb = sb("xsb", [P, 12, T], BF16)
        yssd = sb("yssd", [P, 8, T], BF16)
        lhalo = sb("lhalo", [P, 8, 3])
        shalo = sb("shalo", [P, 12, 3])
        hprev = sb("hprev", [P, 8])
        Sst = sb("Sst", [P, D])
        Sb = sb("Sb", [P, D], BF16)
        dtv = sb("dtv", [P, 16])
        dtt = sb("dtt", [P, 16])
        dA = sb("dA", [P, 16])
        sml = sb("sml", [P, 48])
        Ubig = sb("Ubig", [P, 8, P])
        Lm = sb("Lm", [P, 8, P])
        MT = sb("MT", [P, 8, P], BF16)
        CBm = sb("CBm", [P, P])
        xdt = sb("xdt", [P, D], BF16)
        xdd = sb("xdd", [P, D], BF16)
        xD = sb("xD", [P, 512])
        Btm = sb("Btm", [P, 256], BF16)
        yo = sb("yo", [P, 512])
        tz = sb("tz", [P, 512])
        yz = sb("yz", [P, 512])
        yn = sb("yn", [P, 512], BF16)
        gss = sb("gss", [P, 8])
        osb = sb("osb", [P, D])
        sto = sb("sto", [P, P])
        hT = sb("hT", [8, P])
        stg = xin[1]
        badag = gate
        rsel = osb
        wadab = osb.bitcast(BF16)

        ps = [st.enter_context(nc.psum_tensor("ps%d" % i, [P, 512], F32)) for i in range(8)]
        PA = [ps[0], ps[1]]
        PG, PT, PSG0, PSG1, PSM, PY = ps[2], ps[3], ps[4], ps[5], ps[6], ps[7]
        PTb = PT.bitcast(BF16)

        B = {}

        PSUM_NAMES = ("PA0", "PA1", "PG", "PT", "PSG0", "PSG1", "PSM", "PY")

        def b(name):
            if name.startswith("PSM"):
                name = "PSM"
            if name not in B:
                B[name] = Buf(name, excl=name in PSUM_NAMES)
            return B[name]

        bPA = [b("PA0"), b("PA1")]

        ident = cst[:, 0:P]
        Umat = cst[:, P:2 * P]
        Vmat = cst[:, 2 * P:3 * P]
        ones = cst[:, 3 * P:4 * P]

        C_NG, C_LCW, C_LCB, C_LBA, C_LBX, C_LAM, C_SCW, C_SCB, C_GSSD, C_BADA = (
            0, 8, 40, 48, 56, 64, 72, 120, 132, 140)
        DC_NBA, DC_NBX, DC_M8, DC_M16 = 0, 8, 16, 24
        R_DTB, R_A, R_D = 0, 16, 32

        def e_ts(out, in0, s1, s2, op0, op1=None):
            if op1 is None:
                return lambda e: e.tensor_scalar(out, in0, s1, None, op0)
            return lambda e: e.tensor_scalar(out, in0, s1, s2, op0, op1)

        def e_tt(out, a, bb, op):
            return lambda e: e.tensor_tensor(out, a, bb, op)

        def e_stt(out, in0, sc, in1, op0, op1):
            return lambda e: e.scalar_tensor_tensor(out, in0, sc, in1, op0, op1)

        def e_cp(out, in_):
            return lambda e: e.tensor_copy(out, in_)

        def e_acp(out, in_):
            return lambda e: e.copy(out, in_)

        def e_act(out, in_, func, bias=0.0, scale=1.0, accum=None):
            if accum is None:
                return lambda e: e.activation(out, in_, func, bias=bias, scale=scale)
            return lambda e: e.activation(out, in_, func, bias=bias, scale=scale, accum_out=accum)

        def e_mm(items):
            def emit(e):
                ins = None
                for (o, l, r, s0, s1) in items:
                    ins = e.matmul(o, l, r, start=s0, stop=s1)
                return ins
            return emit

        def e_tr(items):
            def emit(e):
                ins = None
                for (o, i, idn) in items:
                    ins = e.transpose(o, i, idn)
                return ins
            return emit

        def e_memset(ap, v):
            return lambda e: e.memset(ap, v)

        try:
            S.dma("sp", [(cst[:], consts)], writes=[b("cst")])
            S.op("dve", e_cp(identb[:], ident), reads=[b("cst")], writes=[b("identb")])

            def colload(off, n, src):
                S.dma("sp", [(colp[:, off:off + n], src.rearrange("(c p) -> p c", p=P))],
                      writes=[b("colp")], owner=b("colp_d%d" % off), allow_slow_non_contiguous=True)

            colload(C_NG, 8, norm_g)
            colload(C_LCB, 8, lru_conv_b)
            colload(C_LBA, 8, lru_b_a)
            colload(C_LBX, 8, lru_b_x)
            colload(C_LAM, 8, lru_lambda)
            colload(C_SCB, 12, ssd_conv_b)
            colload(C_GSSD, 8, ssd_norm_g)
            colload(C_BADA, 16, b_ada[0:2 * D])
            for k in range(4):
                colload(C_LCW + 8 * k, 8, lru_conv_w[k])
                colload(C_SCW + 12 * k, 12, ssd_conv_w[k])
            S.dma("sp", [(rows[:, R_DTB:R_DTB + 16], ssd_dt_bias.partition_broadcast(P)),
                         (rows[:, R_A:R_A + 16], ssd_a_log.partition_broadcast(P)),
                         (rows[:, R_D:R_D + 16], ssd_d.partition_broadcast(P))], writes=[b("rows")])
            S.dma("sp", [(fng[:], final_norm_g.partition_broadcast(P))], writes=[b("fng")])
            S.dma("sp", [(badag[0:NSEQ, :], b_ada[2 * D:3 * D].partition_broadcast(NSEQ))], writes=[b("gate")])
            S.dma("sp", [(cT[:, :, r], cc[r].rearrange("(k p) -> p k", p=P)) for r in range(NSEQ)], writes=[b("cT")],
                  allow_slow_non_contiguous=True)

            S.op("dve", e_ts(dcol[:, DC_NBA:DC_NBA + 16], colp[:, C_LBA:C_LBA + 16], -1.0, None, ALU.mult),
                 reads=[b("colp")], writes=[b("dcol")])
            S.op("act", e_act(dcol[:, 32:40], colp[:, C_LAM:C_LAM + 8], AF.Exp, scale=-1.0),
                 reads=[b("colp")], writes=[b("dcol")])
            S.op("act", e_act(dcol[:, 32:40], dcol[:, 32:40], AF.Ln, bias=1.0), reads=[b("dcol")], writes=[b("dcol")])
            S.op("dve", e_ts(dcol[:, DC_M8:DC_M8 + 8], dcol[:, 32:40], -8.0, None, ALU.mult),
                 reads=[b("dcol")], writes=[b("dcol")])
            S.op("dve", e_ts(dcol[:, DC_M16:DC_M16 + 8], dcol[:, 32:40], -16.0, None, ALU.mult),
                 reads=[b("dcol")], writes=[b("dcol")])
            S.op("act", e_act(rows[:, R_A:R_A + 16], rows[:, R_A:R_A + 16], AF.Exp), reads=[b("rows")], writes=[b("rows")])
            S.op("dve", e_ts(rows[:, R_A:R_A + 16], rows[:, R_A:R_A + 16], -1.0, None, ALU.mult),
                 reads=[b("rows")], writes=[b("rows")])

            w_in_v = w_in.rearrange("(k p) n -> p k n", p=P)
            for k in range(KC):
                S.dma("pool", [(Win[:, k, :], w_in_v[:, k, :])], writes=[b("Win")], owner=b("Win_d%d" % k))
            S.dma("pool", [(Wa[:], lru_w_a.rearrange("h i j -> i h j")),
                           (Wx[:], lru_w_x.rearrange("h i j -> i h j"))], writes=[b("Wg")])
            w_out_v = w_out.rearrange("(k p) n -> p k n", p=P)
            for k in range(8):
                S.dma("pool", [(Wout[:, k, :], w_out_v[:, k, :])], writes=[b("Wout")], owner=b("Wout_d%d" % k))
            for k in range(8, 16):
                S.dma("sp", [(stg[:], w_out_v[:, k, :])], writes=[b("xin1")])
                S.op("dve", e_ts(Wout[:, k, :], stg[:], colp[:, C_GSSD + k - 8:C_GSSD + k - 7], None, ALU.mult),
                     reads=[b("xin1"), b("colp")], writes=[b("Wout")])

            cTf = cT[:].rearrange("p k r -> p (k r)")
            S.op("act", e_act(modc[:].rearrange("p a r -> p (a r)")[:, 0:KC * NSEQ], cTf, AF.Exp, scale=-1.0),
                 reads=[b("cT")], writes=[b("modc")])
            mtmp = modc[:].rearrange("p a r -> p (a r)")[:, 0:KC * NSEQ]
            S.op("act", e_act(mtmp, mtmp, AF.Ln, bias=1.0), reads=[b("modc")], writes=[b("modc")])
            S.op("act", e_act(mtmp, mtmp, AF.Exp, scale=-1.0), reads=[b("modc")], writes=[b("modc")])
            S.op("dve", e_tt(cTb[:].rearrange("p k r -> p (k r)"), cTf, mtmp, ALU.mult),
                 reads=[b("cT"), b("modc")], writes=[b("cTb")])
            w_ada_v = w_ada.rearrange("(k p) n -> p k n", p=P)
            wadab3 = wadab[:, 0:KC * 256].rearrange("p (k n) -> p k n", k=KC)
            PSMc = PSM[:, 0:16 * NSEQ].rearrange("p (a r) -> p a r", a=16)
            for blk in range(12):
                S.dma("pool", [(wadab3, w_ada_v[:, :, blk * 256:(blk + 1) * 256])], writes=[b("osb")])
                if blk < 8:
                    for half in range(2):
                        cb = blk * 2 + half
                        S.op("pe", e_mm([(PSMc[:, cb, :], wadab3[:, k, half * P:(half + 1) * P], cTb[:, k, :],
                                          k == 0, k == KC - 1) for k in range(KC)]),
                             reads=[b("osb"), b("cTb")], writes=[b("PSM")])
                else:
                    gb_ = blk - 8
                    bank = PA[gb_ // 2]
                    S.op("pe", e_mm([(bank[0:NSEQ, (gb_ % 2) * 256:(gb_ % 2) * 256 + 256], cTb[:, k, :],
                                      wadab3[:, k, :], k == 0, k == KC - 1) for k in range(KC)]),
                         reads=[b("osb"), b("cTb")], writes=[bPA[gb_ // 2]])
            S.op("dve", e_tt(modc[:], PSMc, colp[:, C_BADA:C_BADA + 16].unsqueeze(2).to_broadcast([P, 16, NSEQ]), ALU.add),
                 reads=[b("PSM"), b("colp")], writes=[b("modc")])
            S.op("dve", e_ts(gsc[:], modc[:, 8:16, :], 1.0, None, ALU.add), reads=[b("modc")], writes=[b("gsc")])
            S.op("dve", e_tt(gsc[:], gsc[:], colp[:, C_NG:C_NG + 8].unsqueeze(2).to_broadcast([P, KC, NSEQ]), ALU.mult),
                 reads=[b("gsc"), b("colp")], writes=[b("gsc")])
            for h2 in range(2):
                S.op("dve", e_tt(modg[0:NSEQ, h2 * 512:(h2 + 1) * 512], PA[h2][0:NSEQ, :],
                                 badag[0:NSEQ, h2 * 512:(h2 + 1) * 512], ALU.add),
                     reads=[bPA[h2], b("gate")], writes=[b("modg")])

            mark("setup")
            def seq_views(si):
                if si < NP:
                    return xp[si], yp[si], 0, si, LP
                return xs_in[si - NP], ys[si - NP], 1, si - NP, LS

            tile_ctr = [0]

            seq_order = list(range(NP, NSEQ)) + list(range(NP))
            for si in (seq_order if not DBG_REV else seq_order[::-1]):
                xsrc, ydst, grp, gi, L = seq_views(si)
                if DBG_BARRIER:
                    S.barrier(stores)
                is_prompt = grp == 0
                Tt = min(T, L)
                Q = min(P, Tt)
                NSUB = Tt // Q
                NT = L // Tt
                assert L % Tt == 0 and Tt % Q == 0

                if is_prompt:
                    S.op("pool", e_memset(hprev[:], 0.0), writes=[b("hprev")])
                    S.op("pool", e_memset(lhalo[:], 0.0), writes=[b("lhalo")])
                    S.op("pool", e_memset(shalo[:], 0.0), writes=[b("shalo")])
                    S.op("pool", e_memset(Sst[:], 0.0), writes=[b("Sst")])
                    S.op("pool", e_memset(Sb[:], 0.0), writes=[b("Sb")])
                else:
                    S.dma("sp", [(xin[0][0:3, :], st_lc[gi]), (xin[1][0:3, :], st_sc[gi][:, 0:1024]),
                                 (osb[0:3, 0:512], st_sc[gi][:, 1024:1536])],
                          writes=[b("xin0"), b("xin1"), b("osb")], owner=b("stld"))
                    S.dma("sp", [(osb[32:33, :], st_lh[gi].rearrange("(o d) -> o d", o=1))], writes=[b("osb")], owner=b("stld2"))
                    PSs = PSG1[:, 0:64]
                    S.op("pe", e_tr([(PSs[:, c * 3:(c + 1) * 3], xin[0][0:3, c * P:(c + 1) * P], ident[0:3, 0:3]) for c in range(8)]
                                    + [(PSs[:, 24 + c * 3:24 + (c + 1) * 3], (xin[1][0:3, c * P:(c + 1) * P] if c < 8 else
                                                                          osb[0:3, (c - 8) * P:(c - 7) * P]), ident[0:3, 0:3]) for c in range(12)]),
                         reads=[b("xin0"), b("xin1"), b("osb"), b("cst")], writes=[b("PSG1")])
                    S.op("act", e_acp(lhalo[:].rearrange("p c j -> p (c j)"), PSs[:, 0:24]), reads=[b("PSG1")], writes=[b("lhalo")])
                    S.op("act", e_acp(shalo[:].rearrange("p c j -> p (c j)"), PSs[:, 24:60]), reads=[b("PSG1")], writes=[b("shalo")])
                    S.op("pe", e_tr([(PSG1[:, 64 + c:65 + c], osb[32:33, c * P:(c + 1) * P], ident[32:33, 32:33]) for c in range(8)]),
                         reads=[b("osb"), b("cst")], writes=[b("PSG1")])
                    S.op("act", e_acp(hprev[:], PSG1[:, 64:72]), reads=[b("PSG1")], writes=[b("hprev")])
                    for c4 in range(2):
                        S.dma("sp", [(osb[:, c4 * 512:(c4 + 1) * 512].rearrange("p (c n) -> p c n", c=4),
                                      st_ss[gi, c4 * 512:(c4 + 1) * 512, :].rearrange("(c p) n -> p c n", p=P))],
                              writes=[b("osb")])
                        S.op("pe", e_tr([(PSG0[:, c * P:(c + 1) * P], osb[:, c4 * 512 + c * P:c4 * 512 + (c + 1) * P], ident)
                                         for c in range(4)]),
                             reads=[b("osb"), b("cst")], writes=[b("PSG0")])
                        S.op("act", e_acp(Sst[:, c4 * 512:(c4 + 1) * 512], PSG0[:, :]), reads=[b("PSG0")], writes=[b("Sst")])
                    S.op("pool", e_cp(Sb[:], Sst[:]), reads=[b("Sst")], writes=[b("Sb")])
                S.op("dve", e_ts(rsel[0:NSEQ, :], modg[0:NSEQ, :], ident[0:NSEQ, si:si + 1], None, ALU.mult),
                     reads=[b("modg"), b("cst")], writes=[b("osb")])
                for h2 in range(2):
                    bank, bk = (PSG0, b("PSG0")) if h2 == 0 else (PSG1, b("PSG1"))
                    S.op("pe", e_mm([(bank[:, :], ones[0:NSEQ, :], rsel[0:NSEQ, h2 * 512:(h2 + 1) * 512], True, True)]),
                         reads=[b("osb"), b("cst")], writes=[bk])
                    S.op("act", e_acp(gate[:, h2 * 512:(h2 + 1) * 512], bank[:, :]), reads=[bk], writes=[b("gate")])

                if DBG_STOP == "dump_gate":
                    dbg = dout("dbg", [P, D])
                    stores.append(S.dma("sp", [(dbg, gate[:])], reads=[b("gate")], owner=b("dbg0")))
                    raise _Stop()
                mark("seqinit")
                for ti in range(NT):
                    t0 = ti * Tt
                    for j in range(NSUB):
                        xb_ = xin[j % 2]
                        bxin = b("xin%d" % (j % 2))
                        S.dma("sp", [(xb_[0:Q, :], xsrc[t0 + j * Q:t0 + (j + 1) * Q, :])], writes=[bxin])
                        S.op("act", e_act(junk[0:Q, :], xb_[0:Q, :], AF.Square, accum=ssq[0:Q, 0:1]),
                             reads=[bxin], writes=[b("junk"), b("ssq")])
                        S.op("act", e_act(ssq[0:Q, 1:2], ssq[0:Q, 0:1], AF.Ln, bias=EPS, scale=1.0 / D),
                             reads=[b("ssq")], writes=[b("ssq")])
                        S.op("act", e_act(ssq[0:Q, 2:3], ssq[0:Q, 1:2], AF.Exp, scale=-0.5),
                             reads=[b("ssq")], writes=[b("ssq")])
                        S.op("pool", e_ts(xn[0:Q, :], xb_[0:Q, :], ssq[0:Q, 2:3], None, ALU.mult),
                             reads=[bxin, b("ssq")], writes=[b("xn")])
                        PT3 = PTb[:, 0:KC * P].rearrange("p (k q) -> p k q", k=KC)
                        S.op("pe", e_tr([(PT3[:, k, 0:Q], xn[0:Q, k * P:(k + 1) * P], identb[0:Q, 0:Q]) for k in range(KC)]),
                             reads=[b("xn"), b("identb")], writes=[b("PT")])
                        for k in range(KC):
                            if k % 2 == 0:
                                S.op("act", e_act(hnT[:, k, j * Q:(j + 1) * Q], PT3[:, k, 0:Q], AF.Identity,
                                                  bias=modc[:, k, si:si + 1], scale=gsc[:, k, si:si + 1]),
                                     reads=[b("PT"), b("modc"), b("gsc")], writes=[b("hnT")])
                            else:
                                S.op("dve", e_ts(hnT[:, k, j * Q:(j + 1) * Q], PT3[:, k, 0:Q], gsc[:, k, si:si + 1],
                                                 modc[:, k, si:si + 1], ALU.mult, ALU.add),
                                     reads=[b("PT"), b("modc"), b("gsc")], writes=[b("hnT")])

                    mark("stage1")
                    def proj_fm(col0, bank_i):
                        S.op("pe", e_mm([(PA[bank_i][:, 0:Tt], Win[:, k, col0:col0 + P], hnT[:, k, 0:Tt],
                                          k == 0, k == KC - 1) for k in range(KC)]),
                             reads=[b("Win"), b("hnT")], writes=[bPA[bank_i]])

                    def sigmoid_act(dst, src, src_reads, nbias):
                        S.op("act", e_act(dst, src, AF.Exp, bias=nbias, scale=-1.0), reads=src_reads, writes=[b(dst.tensor.name)])
                        S.op("act", e_act(dst, dst, AF.Ln, bias=1.0), reads=[b(dst.tensor.name)], writes=[b(dst.tensor.name)])
                        S.op("act", e_act(dst, dst, AF.Exp, scale=-1.0), reads=[b(dst.tensor.name)], writes=[b(dst.tensor.name)])

                    def conv4(dst, src, wcol, bcol, nchunks, c):
                        S.op("dve", e_ts(dst[:, 0:Tt], src[:, 0:Tt], colp[:, wcol + c:wcol + c + 1],
                                         colp[:, bcol + c:bcol + c + 1], ALU.mult, ALU.add),
                             reads=[b("lx"), b("colp")], writes=[b("u")])
                        for k in range(1, 4):
                            S.op("dve", e_stt(dst[:, 0:Tt], src[:, k:k + Tt],
                                              colp[:, wcol + k * nchunks + c:wcol + k * nchunks + c + 1],
                                              dst[:, 0:Tt], ALU.mult, ALU.add),
                                 reads=[b("lx"), b("u"), b("colp")], writes=[b("u")])

                    pa_i = 0
                    for h in range(8):
                        S.op("pool", e_cp(lx[:, 0:3], lhalo[:, h, :]), reads=[b("lhalo")], writes=[b("lx")])
                        proj_fm(OFF_LX + h * P, pa_i)
                        S.op("act", e_acp(lx[:, 3:3 + Tt], PA[pa_i][:, 0:Tt]), reads=[bPA[pa_i]], writes=[b("lx")])
                        pa_i ^= 1
                        S.op("pool", e_cp(lhalo[:, h, :], lx[:, Tt:Tt + 3]), reads=[b("lx")], writes=[b("lhalo")])
                        conv4(u, lx, C_LCW, C_LCB, 8, h)
                        S.op("pool", e_cp(ub[:, 0:Tt], u[:, 0:Tt]), reads=[b("u")], writes=[b("ub")])
                        PG2 = PG[:, 0:2 * Tt].rearrange("p (a t) -> p a t", a=2)
                        S.op("pe", e_mm([(PG2[:, 0, :], Wa[:, h, :], ub[:, 0:Tt], True, True),
                                         (PG2[:, 1, :], Wx[:, h, :], ub[:, 0:Tt], True, True)]),
                             reads=[b("Wg"), b("ub")], writes=[b("PG")])
                        sigmoid_act(ga[:, 0:Tt], PG2[:, 0, :], [b("PG"), b("dcol")], dcol[:, DC_NBA + h:DC_NBA + h + 1])
                        S.op("act", e_act(aa[:, 0:Tt], ga[:, 0:Tt], AF.Exp, scale=dcol[:, DC_M8 + h:DC_M8 + h + 1]),
                             reads=[b("ga"), b("dcol")], writes=[b("aa")])
                        S.op("act", e_act(gm[:, 0:Tt], ga[:, 0:Tt], AF.Exp, scale=dcol[:, DC_M16 + h:DC_M16 + h + 1]),
                             reads=[b("ga"), b("dcol")], writes=[b("gm")])
                        S.op("act", e_act(gm[:, 0:Tt], gm[:, 0:Tt], AF.Ln, bias=1.0, scale=-1.0), reads=[b("gm")], writes=[b("gm")])
                        S.op("act", e_act(gm[:, 0:Tt], gm[:, 0:Tt], AF.Exp, scale=0.5), reads=[b("gm")], writes=[b("gm")])
                        sigmoid_act(gx[:, 0:Tt], PG2[:, 1, :], [b("PG"), b("dcol")], dcol[:, DC_NBX + h:DC_NBX + h + 1])
                        S.op("dve", e_tt(gb[:, 0:Tt], gx[:, 0:Tt], u[:, 0:Tt], ALU.mult), reads=[b("gx"), b("u")], writes=[b("gb")])
                        if is_prompt and ti == 0:
                            S.op("pool", e_memset(gm[:, 0:1], 1.0), reads=[], writes=[b("gm")])
                        S.op("dve", e_tt(gb[:, 0:Tt], gb[:, 0:Tt], gm[:, 0:Tt], ALU.mult), reads=[b("gb"), b("gm")], writes=[b("gb")])
                        S.op("dve", (lambda h=h: (lambda e: e.tensor_tensor_scan(hh[:, 0:Tt], aa[:, 0:Tt], gb[:, 0:Tt],
                                                                                   hprev[:, h:h + 1], ALU.mult, ALU.add)))(),
                             reads=[b("aa"), b("gb"), b("hprev")], writes=[b("hh")])
                        S.op("pool", e_cp(hprev[:, h:h + 1], hh[:, Tt - 1:Tt]), reads=[b("hh")], writes=[b("hprev")])
                        if DBG_STOP == "dump_lru" and h == DBG_H and ti == DBG_TI:
                            dbg = dout("dbg", [P, 8 * 256])
                            for i_, (t_, nm_) in enumerate([(lx, "lx"), (u, "u"), (ga, "ga"), (aa, "aa"), (gm, "gm"), (gx, "gx"), (gb, "gb"), (hh, "hh")]):
                                stores.append(S.dma("sp", [(dbg[:, i_ * 256:(i_ + 1) * 256], t_[:, 0:256])], reads=[b(nm_)], owner=b("dbg%d" % i_)))
                            raise _Stop()
                        proj_fm(OFF_LG + h * P, pa_i)
                        sigmoid_act(gg[:, 0:Tt], PA[pa_i][:, 0:Tt], [bPA[pa_i]], 0.0)
                        S.op("dve", e_tt(hh[:, 0:Tt], hh[:, 0:Tt], PA[pa_i][:, 0:Tt], ALU.mult),
                             reads=[b("hh"), bPA[pa_i]], writes=[b("hh")])
                        S.op("dve", e_tt(ylru[:, h, 0:Tt], hh[:, 0:Tt], gg[:, 0:Tt], ALU.mult),
                             reads=[b("hh"), b("gg")], writes=[b("ylru")])
                        pa_i ^= 1

                    mark("lru")
                    for c in range(12):
                        S.op("pool", e_cp(lx[:, 0:3], shalo[:, c, :]), reads=[b("shalo")], writes=[b("lx")])
                        proj_fm(OFF_XBC + c * P, pa_i)
                        S.op("act", e_acp(lx[:, 3:3 + Tt], PA[pa_i][:, 0:Tt]), reads=[bPA[pa_i]], writes=[b("lx")])
                        pa_i ^= 1
                        S.op("pool", e_cp(shalo[:, c, :], lx[:, Tt:Tt + 3]), reads=[b("lx")], writes=[b("shalo")])
                        conv4(u, lx, C_SCW, C_SCB, 12, c)
                        sigmoid_act(ga[:, 0:Tt], u[:, 0:Tt], [b("u")], 0.0)
                        S.op("dve", e_tt(xsb[:, c, 0:Tt], u[:, 0:Tt], ga[:, 0:Tt], ALU.mult),
                             reads=[b("u"), b("ga")], writes=[b("xsb")])

                    mark("xbc")
                    for j in range(NSUB):
                        c0 = j * Q
                        PSM_dt = PSM[0:Q, 0:16]
                        S.op("pe", e_mm([(PSM_dt, hnT[:, k, c0:c0 + Q], Win[:, k, OFF_DT:OFF_DT + 16], k == 0, k == KC - 1)
                                         for k in range(KC)]),
                             reads=[b("hnT"), b("Win")], writes=[b("PSM_dt")])
                        S.op("dve", e_tt(dtv[0:Q, :], PSM_dt, rows[0:Q, R_DTB:R_DTB + 16], ALU.add),
                             reads=[b("PSM_dt"), b("rows")], writes=[b("dtv")])
                        S.op("dve", e_ts(dtv[0:Q, :], dtv[0:Q, :], 30.0, None, ALU.min), reads=[b("dtv")], writes=[b("dtv")])
                        S.op("act", e_act(dtv[0:Q, :], dtv[0:Q, :], AF.Exp), reads=[b("dtv")], writes=[b("dtv")])
                        S.op("act", e_act(dtt[0:Q, :], dtv[0:Q, :], AF.Ln, bias=1.0), reads=[b("dtv")], writes=[b("dtt")])
                        S.op("dve", e_tt(dA[0:Q, :], dtt[0:Q, :], rows[0:Q, R_A:R_A + 16], ALU.mult),
                             reads=[b("dtt"), b("rows")], writes=[b("dA")])
                        mark("ssd_dt")
                        S.op("pe", e_mm([(PSM[0:Q, 64:80], Vmat[0:Q, 0:Q], dA[0:Q, :], True, True),
                                         (PSM[0:Q, 80:96], Umat[0:Q, 0:Q], dA[0:Q, :], True, True),
                                         (PSM[:, 96:112], ones[0:Q, :], dA[0:Q, :], True, True)]),
                             reads=[b("dA"), b("cst")], writes=[b("PSM_sm")])
                        S.op("act", e_act(sml[0:Q, 0:32], PSM[0:Q, 64:96], AF.Exp), reads=[b("PSM_sm")], writes=[b("sml")])
                        S.op("act", e_act(sml[:, 32:48], PSM[:, 96:112], AF.Exp), reads=[b("PSM_sm")], writes=[b("sml")])
                        mark("ssd_dec")
                        S.op("pe", e_tr([(PTb[0:Q, c * P:(c + 1) * P], xsb[:, c, c0:c0 + Q], identb[:, :]) for c in range(8)]),
                             reads=[b("xsb"), b("identb")], writes=[b("PT")])
                        PSMb = PSM.bitcast(BF16)
                        S.op("pe", e_tr([(PSMb[0:Q, 512 + g * P:512 + (g + 1) * P], xsb[:, 8 + g, c0:c0 + Q], identb[:, :])
                                         for g in range(2)]),
                             reads=[b("xsb"), b("identb")], writes=[b("PSM_bt")])
                        S.op("act", e_acp(Btm[0:Q, :], PSMb[0:Q, 512:768]), reads=[b("PSM_bt")], writes=[b("Btm")])
                        PT4 = PTb[0:Q, :].rearrange("p (k d) -> p k d", k=16)
                        dt_b = dtt[0:Q, :].unsqueeze(2).to_broadcast([Q, 16, 64])
                        S.op("dve", e_tt(xdt[0:Q, :].rearrange("p (k d) -> p k d", k=16), PT4, dt_b, ALU.mult),
                             reads=[b("PT"), b("dtt")], writes=[b("xdt")])
                        dec_b = sml[0:Q, 16:32].unsqueeze(2).to_broadcast([Q, 16, 64])
                        S.op("pool", e_tt(xdd[0:Q, :].rearrange("p (k d) -> p k d", k=16),
                                          xdt[0:Q, :].rearrange("p (k d) -> p k d", k=16), dec_b, ALU.mult),
                             reads=[b("xdt"), b("sml")], writes=[b("xdd")])

                        mark("ssd_tr")
                        for g in range(2):
                            S.op("pe", e_mm([(PSM[0:Q, 128:128 + Q], xsb[:, 8 + g, c0:c0 + Q], xsb[:, 10 + g, c0:c0 + Q], True, True)]),
                                 reads=[b("xsb")], writes=[b("PSM_cb")])
                            mark("ssd_cbmm")
                            if DBG_STOP == "exp1" and g == 1:
                                S.op("dve", e_memset(CBm[0:Q, 0:Q], 0.0), reads=[b("PSM_cb")], writes=[b("CBm")])
                                raise _Stop()
                            if DBG_STOP == "exp2" and g == 1:
                                S.op("dve", e_tt(CBm[0:Q, 0:Q], PSM[0:Q, 128:128 + Q], Vmat[0:Q, 0:Q], ALU.mult), reads=[b("cst")], writes=[b("CBm")])
                                raise _Stop()
                            S.op("dve", e_tt(CBm[0:Q, 0:Q], PSM[0:Q, 128:128 + Q], Vmat[0:Q, 0:Q], ALU.mult),
                                 reads=[b("PSM_cb"), b("cst")], writes=[b("CBm")])
                            mark("ssd_cb")
                            dA_b = dA[0:Q, g * 8:(g + 1) * 8].unsqueeze(2).to_broadcast([Q, 8, Q])
                            U_b = Umat[0:Q, 0:Q].unsqueeze(1).to_broadcast([Q, 8, Q])
                            S.op("pool", e_tt(Ubig[0:Q, :, 0:Q], U_b, dA_b, ALU.mult),
                                 reads=[b("dA"), b("cst")], writes=[b("Ubig")])
                            for hf in range(2):
                                bank, bk = (PSG0, b("PSG0")) if hf == 0 else (PSG1, b("PSG1"))
                                bank3 = bank[0:Q, :].rearrange("p (k l) -> p k l", k=4)
                                S.op("pe", e_mm([(bank3[:, k, 0:Q], Ubig[0:Q, hf * 4 + k, 0:Q], Vmat[0:Q, 0:Q], True, True)
                                                 for k in range(4)]),
                                     reads=[b("Ubig"), b("cst")], writes=[bk])
                                S.op("act", e_act(Lm[0:Q, hf * 4:(hf + 1) * 4, 0:Q], bank3[:, :, 0:Q], AF.Exp),
                                     reads=[bk], writes=[b("Lm")])
                            CB_b = CBm[0:Q, 0:Q].unsqueeze(1).to_broadcast([Q, 8, Q])
                            S.op("dve", e_tt(MT[0:Q, :, 0:Q], Lm[0:Q, :, 0:Q], CB_b, ALU.mult),
                                 reads=[b("Lm"), b("CBm")], writes=[b("MT")])
                            mark("ssd_L")
                            S.op("pe", e_mm([(PY[0:Q, k * 64:(k + 1) * 64], MT[0:Q, k, 0:Q],
                                              xdt[0:Q, g * 512 + k * 64:g * 512 + (k + 1) * 64], True, True) for k in range(8)]),
                                 reads=[b("MT"), b("xdt")], writes=[b("PY")])
                            S.op("pe", e_mm([(PA[0][0:Q, :], xsb[:, 10 + g, c0:c0 + Q], Sb[:, g * 512:(g + 1) * 512], True, True)]),
                                 reads=[b("xsb"), b("Sb")], writes=[bPA[0]])
                            S.op("pe", e_mm([(PA[1][:, :], Btm[0:Q, g * P:(g + 1) * P], xdd[0:Q, g * 512:(g + 1) * 512], True, True)]),
                                 reads=[b("Btm"), b("xdd")], writes=[bPA[1]])
                            mark("ssd_mm")
                            eA_b = sml[0:Q, g * 8:(g + 1) * 8].unsqueeze(2).to_broadcast([Q, 8, 64])
                            D_b = rows[0:Q, R_D + g * 8:R_D + (g + 1) * 8].unsqueeze(2).to_broadcast([Q, 8, 64])
                            S.op("dve", e_tt(yo[0:Q, :].rearrange("p (k d) -> p k d", k=8),
                                             PA[0][0:Q, :].rearrange("p (k d) -> p k d", k=8), eA_b, ALU.mult),
                                 reads=[bPA[0], b("sml")], writes=[b("yo")])
                            S.op("dve", e_tt(xD[0:Q, :].rearrange("p (k d) -> p k d", k=8),
                                             PT4[:, g * 8:(g + 1) * 8, :], D_b, ALU.mult),
                                 reads=[b("PT"), b("rows")], writes=[b("xD")])
                            S.op("pool", e_tt(yo[0:Q, :], yo[0:Q, :], xD[0:Q, :], ALU.add), reads=[b("yo"), b("xD")], writes=[b("yo")])
                            S.op("dve", e_tt(yo[0:Q, :], yo[0:Q, :], PY[0:Q, :], ALU.add), reads=[b("yo"), b("PY")], writes=[b("yo")])
                            mark("ssd_y")
                            ed_b = sml[:, 32 + g * 8:32 + (g + 1) * 8].unsqueeze(2).to_broadcast([P, 8, 64])
                            Sg = Sst[:, g * 512:(g + 1) * 512]
                            S.op("pool", e_tt(Sg.rearrange("p (k d) -> p k d", k=8), Sg.rearrange("p (k d) -> p k d", k=8),
                                              ed_b, ALU.mult), reads=[b("Sst"), b("sml")], writes=[b("Sst")])
                            S.op("dve", e_tt(Sg, Sg, PA[1][:, :], ALU.add), reads=[b("Sst"), bPA[1]], writes=[b("Sst")])
                            S.op("act", e_acp(Sb[:, g * 512:(g + 1) * 512], Sg), reads=[b("Sst")], writes=[b("Sb")])
                            mark("ssd_st")
                            S.op("pe", e_mm([(PG[0:Q, :], hnT[:, k, c0:c0 + Q], Win[:, k, OFF_Z + g * 512:OFF_Z + (g + 1) * 512],
                                              k == 0, k == KC - 1) for k in range(KC)]),
                                 reads=[b("hnT"), b("Win")], writes=[b("PG")])
                            sigmoid_act(tz[0:Q, :], PG[0:Q, :], [b("PG")], 0.0)
                            S.op("dve", e_tt(yz[0:Q, :], yo[0:Q, :], PG[0:Q, :], ALU.mult), reads=[b("yo"), b("PG")], writes=[b("yz")])
                            S.op("pool", e_tt(yz[0:Q, :], yz[0:Q, :], tz[0:Q, :], ALU.mult), reads=[b("yz"), b("tz")], writes=[b("yz")])
                            mark("ssd_z")
                            S.op("act", e_act(junk[0:Q, 0:512], yz[0:Q, :], AF.Square, accum=gss[0:Q, 0:1]),
                                 reads=[b("yz")], writes=[b("junk"), b("gss")])
                            S.op("act", e_act(gss[0:Q, 1:2], gss[0:Q, 0:1], AF.Ln, bias=EPS, scale=1.0 / 512),
                                 reads=[b("gss")], writes=[b("gss")])
                            S.op("act", e_act(gss[0:Q, 2:3], gss[0:Q, 1:2], AF.Exp, scale=-0.5), reads=[b("gss")], writes=[b("gss")])
                            S.op("pool", e_ts(yn[0:Q, :], yz[0:Q, :], gss[0:Q, 2:3], None, ALU.mult),
                                 reads=[b("yz"), b("gss")], writes=[b("yn")])
                            mark("ssd_n")
                            PSMy = PSMb[:, 512:512 + 4 * Q].rearrange("p (c q) -> p c q", c=4)
                            S.op("pe", e_tr([(PSMy[:, c, :], yn[0:Q, c * P:(c + 1) * P], identb[0:Q, 0:Q]) for c in range(4)]),
                                 reads=[b("yn"), b("identb")], writes=[b("PSM_bt")])
                            mark("ssd_ytr")
                            S.op("act", e_acp(yssd[:, g * 4:(g + 1) * 4, c0:c0 + Q], PSMy), reads=[b("PSM_bt")], writes=[b("yssd")])
                            mark("ssd_yev")

                    if DBG_STOP == "dump_y" and ti == DBG_TI:
                        dbg = dout("dbg", [P, 2, 8, Tt])
                        stores.append(S.dma("pool", [(dbg[:, 0], ylru[:, :, 0:Tt])], reads=[b("ylru")], owner=b("dbg0")))
                        stores.append(S.dma("pool", [(dbg[:, 1], yssd[:, :, 0:Tt])], reads=[b("yssd")], owner=b("dbg1")))
                        raise _Stop()
                    mark("ssd")
                    for j in range(NSUB):
                        c0 = j * Q
                        xr = xin[j % 2]
                        bxr = b("xin%d" % (j % 2))
                        S.dma("sp", [(xr[0:Q, :], xsrc[t0 + c0:t0 + c0 + Q, :])], writes=[bxr])
                        for h2 in range(2):
                            items = []
                            for k in range(16):
                                lhs = ylru[:, k, c0:c0 + Q] if k < 8 else yssd[:, k - 8, c0:c0 + Q]
                                items.append((PA[h2][0:Q, :], lhs, Wout[:, k, h2 * 512:(h2 + 1) * 512], k == 0, k == 15))
                            S.op("pe", e_mm(items), reads=[b("ylru"), b("yssd"), b("Wout")], writes=[bPA[h2]])
                            S.op("dve", e_tt(osb[0:Q, h2 * 512:(h2 + 1) * 512], PA[h2][0:Q, :], gate[0:Q, h2 * 512:(h2 + 1) * 512], ALU.mult),
                                 reads=[bPA[h2], b("gate")], writes=[b("osb")])
                        if DBG_STOP == "dump_o1" and ti == DBG_TI and j == 0:
                            dbg = dout("dbg", [P, D])
                            stores.append(S.dma("sp", [(dbg, osb[:])], reads=[b("osb")], owner=b("dbg0")))
                            raise _Stop()
                        S.op("pool", e_tt(osb[0:Q, :], osb[0:Q, :], xr[0:Q, :], ALU.add), reads=[b("osb"), bxr], writes=[b("osb")])
                        if DBG_STOP == "dump_o2" and ti == DBG_TI and j == 0:
                            dbg = dout("dbg", [P, D])
                            stores.append(S.dma("sp", [(dbg, osb[:])], reads=[b("osb")], owner=b("dbg0")))
                            raise _Stop()
                        S.op("act", e_act(junk[0:Q, :], osb[0:Q, :], AF.Square, accum=ssq[0:Q, 4:5]),
                             reads=[b("osb")], writes=[b("junk"), b("ssq")])
                        S.op("act", e_act(ssq[0:Q, 5:6], ssq[0:Q, 4:5], AF.Ln, bias=EPS, scale=1.0 / D), reads=[b("ssq")], writes=[b("ssq")])
                        S.op("act", e_act(ssq[0:Q, 6:7], ssq[0:Q, 5:6], AF.Exp, scale=-0.5), reads=[b("ssq")], writes=[b("ssq")])
                        S.op("dve", e_stt(xr[0:Q, :], osb[0:Q, :], ssq[0:Q, 6:7], fng[0:Q, :], ALU.mult, ALU.mult),
                             reads=[b("osb"), b("ssq"), b("fng")], writes=[bxr])
                        if DBG_STOP == "dump_o3" and ti == DBG_TI and j == 0:
                            dbg = dout("dbg", [P, D + 16])
                            stores.append(S.dma("sp", [(dbg[:, 0:D], xr[:])], reads=[bxr], owner=b("dbg0")))
                            stores.append(S.dma("sp", [(dbg[:, D:D + 16], ssq[:])], reads=[b("ssq")], owner=b("dbg1")))
                            raise _Stop()
                        stores.append(S.dma("sp", [(ydst[t0 + c0:t0 + c0 + Q, :], xr[0:Q, :])], reads=[bxr],
                                            owner=b("xout%d" % (j % 2))))

                mark("out")
                for blk in range(5):
                    col0 = OFF_LX + blk * 512 if blk < 2 else OFF_XBC + (blk - 2) * 512
                    S.op("pe", e_mm([(PY[0:3, :], hnT[:, k, Tt - 3:Tt], Win[:, k, col0:col0 + 512], k == 0, k == KC - 1)
                                     for k in range(KC)]),
                         reads=[b("hnT"), b("Win")], writes=[b("PY")])
                    stgt, stgb = (yo, b("yo")) if blk % 2 == 0 else (xD, b("xD"))
                    S.op("act", e_acp(stgt[0:3, :], PY[0:3, :]), reads=[b("PY")], writes=[stgb])
                    dst = o_lc[grp][gi][:, blk * 512:(blk + 1) * 512] if blk < 2 else o_sc[grp][gi][:, (blk - 2) * 512:(blk - 1) * 512]
                    stores.append(S.dma("sp", [(dst, stgt[0:3, :])], reads=[stgb], owner=b("cso%d" % (blk % 2))))
                S.op("pe", e_tr([(PSG1[0:8, 0:P], hprev[:, 0:8], ident)]), reads=[b("hprev"), b("cst")], writes=[b("PSG1")])
                S.op("act", e_acp(gss[0:8, :].bitcast(F32) if False else hT[0:8, :], PSG1[0:8, 0:P]), reads=[b("PSG1")], writes=[b("hT")])
                stores.append(S.dma("sp", [(o_lh[grp][gi].rearrange("(c p) -> c p", p=P), hT[0:8, :])], reads=[b("hT")]))
                for c in range(8):
                    S.op("pe", e_tr([(PSG0[:, 0:P], Sst[:, c * P:(c + 1) * P], ident)]), reads=[b("Sst"), b("cst")], writes=[b("PSG0")])
                    S.op("act", e_acp(sto[:], PSG0[:, 0:P]), reads=[b("PSG0")], writes=[b("sto")])
                    stores.append(S.dma("sp", [(o_ss[grp][gi, c * P:(c + 1) * P, :], sto[:])], reads=[b("sto")]))
                mark("seqend")
        except _Stop:
            pass
        S.finish("sp", stores)
        S.run()
    return nc


def make_consts():
    c = np.zeros((P, 4 * P), np.float32)
    j = np.arange(P)[:, None]
    s = np.arange(P)[None, :]
    c[:, 0:P] = (j == s)
    c[:, P:2 * P] = (j > s)
    c[:, 2 * P:3 * P] = (j <= s)
    c[:, 3 * P:4 * P] = 1.0
    return c


def core_inputs(inp, pidx, sidx):
    f = lambda a: np.ascontiguousarray(np.asarray(a, dtype=np.float32))
    m = {
        "xp": f(inp["x_prompt"][pidx]),
        "xs": f(inp["x_sample"][sidx]),
        "cc": f(np.concatenate([inp["c_prompt"][pidx], inp["c_sample"][sidx]], axis=0)),
        "st_lc": f(inp["state_lru_conv"][0][sidx]),
        "st_lh": f(inp["state_lru_h"][0][sidx]),
        "st_sc": f(inp["state_ssd_conv"][0][sidx]),
        "st_ss": f(np.asarray(inp["state_ssd"][0][sidx]).reshape(len(sidx), D, P)),
        "consts": make_consts(),
    }
    for k in ("norm_g", "w_ada", "b_ada", "w_in", "lru_conv_w", "lru_conv_b", "lru_w_a", "lru_b_a", "lru_w_x",
              "lru_b_x", "lru_lambda", "ssd_conv_w", "ssd_conv_b", "ssd_dt_bias", "ssd_a_log", "ssd_d",
              "ssd_norm_g", "w_out"):
        m[k] = f(np.asarray(inp[k])[0])
    m["final_norm_g"] = f(inp["final_norm_g"])
    return m


_NC_CACHE = {}


def kernel(**inputs):
    B_, L_ = inputs["x_prompt"].shape[:2]
    DB, DL = inputs["x_sample"].shape[:2]
    n = NCORES
    NP, NS = B_ // n, DB // n
    key = (NP, L_, NS, DL)
    if key not in _NC_CACHE:
        _NC_CACHE[key] = build_program(NP, L_, NS, DL)
    nc = _NC_CACHE[key]
    in_maps = []
    for i in range(n):
        pidx = list(range(i * NP, (i + 1) * NP))
        sidx = list(range(i * NS, (i + 1) * NS))
        in_maps.append(core_inputs(inputs, pidx, sidx))
    res = run_bass_kernel_spmd(nc, in_maps, core_ids=list(range(n)))
    R = res.results
    cat = lambda name: np.concatenate([np.asarray(r[name], dtype=np.float32) for r in R], axis=0)
    y_p = cat("yp")
    y_s = cat("ys")
    outs = [y_p, y_s]
    for sfx, nb in (("p", B_), ("s", DB)):
        outs.append(cat("o_lc_" + sfx)[None])
        outs.append(cat("o_lh_" + sfx)[None])
        outs.append(cat("o_sc_" + sfx)[None])
        outs.append(cat("o_ss_" + sfx).reshape(1, nb, 2, 8, 64, 128))
    return tuple(outs)
```

```python
import math
from contextlib import ExitStack

import numpy as np
import concourse.bass as bass
import concourse.mybir as mybir
from concourse.bass_utils import run_bass_kernel_spmd

F32 = mybir.dt.float32
BF16 = mybir.dt.bfloat16
ALU = mybir.AluOpType
AF = mybir.ActivationFunctionType

P = 128
D = 1024
KC = 8
NCORES = 8
D_XBC = 1536
IN_COLS = 4624
EPS = 1e-6
OFF_LX, OFF_LG, OFF_XBC, OFF_Z, OFF_DT = 0, 1024, 2048, 3584, 4608

ENGS = ("pe", "act", "dve", "pool", "sp")


class Buf:
    __slots__ = ("name", "w", "r", "dsem", "dcnt", "excl")

    def __init__(self, name, excl=False):
        self.name = name
        self.excl = excl
        self.w = None
        self.r = {}
        self.dsem = None
        self.dcnt = 0


class Sched:
    def __init__(self, nc, stack):
        self.nc = nc
        self.stack = stack
        self.q = {e: [] for e in ENGS}
        self.esem = {}
        for e in ENGS:
            if e != "sp":
                self.esem[e] = stack.enter_context(nc.semaphore("es_" + e))
        self.ecnt = {e: 0 for e in ENGS}
        self.known = {e: {} for e in ENGS}
        self.nd = 0
        self.rec = None

    def begin(self):
        self.rec = []

    def end(self):
        r, self.rec = self.rec, None
        return r

    def replay(self, chains):
        chains = [c for c in chains if c]
        pos = [0] * len(chains)
        while True:
            best, bf = None, None
            for i, c in enumerate(chains):
                if pos[i] < len(c):
                    f = pos[i] / len(c)
                    if bf is None or f < bf:
                        best, bf = i, f
            if best is None:
                break
            kind, args, kw, ph = chains[best][pos[best]]
            pos[best] += 1
            tok = (self.op if kind == "op" else self.dma)(*args, **kw)
            ph[0] = tok

    @staticmethod
    def _tok(t):
        return t[0] if isinstance(t, list) else t

    def _waits(self, eng, toks):
        need = {}
        for t in toks:
            if t is None:
                continue
            sem, val = t
            k = id(sem)
            if self.known[eng].get(k, 0) >= val:
                continue
            if k not in need or need[k][1] < val:
                need[k] = (sem, val)
        for k, (sem, val) in need.items():
            self.known[eng][k] = val
        return list(need.values())

    @staticmethod
    def _deps(reads, writes):
        toks = []
        for b in reads:
            toks.append(b.w)
        for b in writes:
            toks.append(b.w)
            toks.extend(b.r.values())
        return toks

    @staticmethod
    def _commit(tok, reads, writes):
        k = id(tok[0])
        for b in reads:
            b.r[k] = tok
        for b in writes:
            b.w = tok
            b.r = {}

    def op(self, eng, emit, reads=(), writes=()):
        if self.rec is not None:
            ph = [None]
            self.rec.append(("op", (eng, emit, tuple(reads), tuple(writes)), {}, ph))
            return ph
        if any(x.excl for x in reads):
            writes = list(writes) + [x for x in reads if x.excl and x not in writes]
            reads = [x for x in reads if not x.excl]
        waits = self._waits(eng, self._deps(reads, writes))
        self.ecnt[eng] += 1
        tok = (self.esem[eng], self.ecnt[eng])
        self.q[eng].append((waits, emit, self.esem[eng]))
        self._commit(tok, reads, writes)
        return tok

    def dma(self, eng, parts, reads=(), writes=(), owner=None, **kw):
        if self.rec is not None:
            ph = [None]
            kw2 = dict(kw)
            kw2.update(reads=tuple(reads), writes=tuple(writes), owner=owner)
            self.rec.append(("dma", (eng, parts), kw2, ph))
            return ph
        if owner is None:
            owner = writes[0] if writes else reads[0]
        if owner.dsem is None:
            self.nd += 1
            owner.dsem = self.stack.enter_context(self.nc.semaphore("ds%d" % self.nd))
        waits = self._waits(eng, self._deps(reads, writes))
        sem = owner.dsem
        owner.dcnt += 16 * len(parts)
        tok = (sem, owner.dcnt)

        def emit(e, parts=parts, sem=sem, kw=kw):
            for (o, i) in parts:
                e.dma_start(out=o, in_=i, **kw).then_inc(sem, 16)
            return None
        self.q[eng].append((waits, emit, None))
        self._commit(tok, reads, writes)
        return tok

    def barrier(self, extra=()):
        toks = [(self.esem[x], self.ecnt[x]) for x in self.esem if self.ecnt[x] > 0] + [self._tok(t) for t in extra]
        for e in ENGS:
            w = self._waits(e, toks)
            if w:
                self.q[e].append((w, None, None))

    def finish(self, eng, toks):
        self.q[eng].append((self._waits(eng, [self._tok(t) for t in toks]), None, None))

    def run(self):
        nc, q = self.nc, self.q

        def play(e, items):
            for waits, emit, inc in items:
                for sem, val in waits:
                    e.wait_ge(sem, val)
                if emit is None:
                    continue
                ins = emit(e)
                if inc is not None:
                    ins.then_inc(inc, 1)

        with nc.Block() as block:
            @block.tensor
            def _(e):
                play(e, q["pe"])

            @block.scalar
            def _(e):
                play(e, q["act"])

            @block.vector
            def _(e):
                play(e, q["dve"])

            @block.gpsimd
            def _(e):
                play(e, q["pool"])

            @block.sync
            def _(e):
                play(e, q["sp"])


class _Stop(Exception):
    pass


DBG_STOP = None
DBG_REV = False
DBG_BARRIER = False
DBG_H = 0
DBG_TI = 0


_MARKS = {}


def mark(n):
    if DBG_STOP is None:
        return
    _MARKS[n] = _MARKS.get(n, 0) + 1
    if DBG_STOP == n or DBG_STOP == "%s:%d" % (n, _MARKS[n]):
        raise _Stop()


def build_program(NP, LP, NS, LS, T=256):
    NSEQ = NP + NS
    nc = bass.Bass("TRN2", target_bir_lowering=False)

    def din(name, shape):
        return nc.dram_tensor(name, list(shape), F32, kind="ExternalInput").ap()

    def dout(name, shape):
        return nc.dram_tensor(name, list(shape), F32, kind="ExternalOutput").ap()

    xp = din("xp", [NP, LP, D])
    xs_in = din("xs", [NS, LS, D])
    cc = din("cc", [NSEQ, D])
    st_lc = din("st_lc", [NS, 3, D])
    st_lh = din("st_lh", [NS, D])
    st_sc = din("st_sc", [NS, 3, D_XBC])
    st_ss = din("st_ss", [NS, D, P])
    norm_g = din("norm_g", [D])
    w_ada = din("w_ada", [D, 3 * D])
    b_ada = din("b_ada", [3 * D])
    w_in = din("w_in", [D, IN_COLS])
    lru_conv_w = din("lru_conv_w", [4, D])
    lru_conv_b = din("lru_conv_b", [D])
    lru_w_a = din("lru_w_a", [8, P, P])
    lru_b_a = din("lru_b_a", [D])
    lru_w_x = din("lru_w_x", [8, P, P])
    lru_b_x = din("lru_b_x", [D])
    lru_lambda = din("lru_lambda", [D])
    ssd_conv_w = din("ssd_conv_w", [4, D_XBC])
    ssd_conv_b = din("ssd_conv_b", [D_XBC])
    ssd_dt_bias = din("ssd_dt_bias", [16])
    ssd_a_log = din("ssd_a_log", [16])
    ssd_d = din("ssd_d", [16])
    ssd_norm_g = din("ssd_norm_g", [D])
    w_out = din("w_out", [2 * D, D])
    final_norm_g = din("final_norm_g", [D])
    consts = din("consts", [P, 4 * P])

    yp = dout("yp", [NP, LP, D])
    ys = dout("ys", [NS, LS, D])
    o_lc = [dout("o_lc_p", [NP, 3, D]), dout("o_lc_s", [NS, 3, D])]
    o_lh = [dout("o_lh_p", [NP, D]), dout("o_lh_s", [NS, D])]
    o_sc = [dout("o_sc_p", [NP, 3, D_XBC]), dout("o_sc_s", [NS, 3, D_XBC])]
    o_ss = [dout("o_ss_p", [NP, D, P]), dout("o_ss_s", [NS, D, P])]

    with ExitStack() as st:
        S = Sched(nc, st)
        stores = []

        def sb(name, shape, dt=F32):
            return st.enter_context(nc.sbuf_tensor(name, list(shape), dt))

        Win = sb("Win", [P, KC, IN_COLS], BF16)
        Wout = sb("Wout", [P, 16, D], BF16)
        Wa = sb("Wa", [P, 8, P], BF16)
        Wx = sb("Wx", [P, 8, P], BF16)
        cst = sb("cst", [P, 4 * P])
        identb = sb("identb", [P, P], BF16)
        NCOL = 8 + 32 + 8 + 8 + 8 + 8 + 48 + 12 + 8 + 16
        colp = sb("colp", [P, NCOL])
        dcol = sb("dcol", [P, 64])
        rows = sb("rows", [P, 64])
        fng = sb("fng", [P, D])
        gate = sb("gate", [P, D])
        modg = sb("modg", [8, D])
        cT = sb("cT", [P, KC, NSEQ])
        cTb = sb("cTb", [P, KC, NSEQ], BF16)
        modc = sb("modc", [P, 16, NSEQ])
        gsc = sb("gsc", [P, KC, NSEQ])

        xin = [sb("xin0", [P, D]), sb("xin1", [P, D])]
        junk = sb("junk", [P, D], BF16)
        ssq = sb("ssq", [P, 16])
        xn = sb("xn", [P, D], BF16)
        hnTs = [sb("hnT0", [P, KC, T], BF16), sb("hnT1", [P, KC, T], BF16)]
        lx = sb("lx", [P, T + 3])
        lx2 = sb("lx2", [P, T + 3])
        u2 = sb("u2", [P, T])
        ga2 = sb("ga2", [P, T])
        u = sb("u", [P, T])
        ub = sb("ub", [P, T], BF16)
        ga = sb("ga", [P, T])
        aa = sb("aa", [P, T])
        gm = sb("gm", [P, T])
        gx = sb("gx", [P, T])
        gb = sb("gb", [P, T])
        hh = sb("hh", [P, T])
        gg = sb("gg", [P, T])
        ylru = sb("ylru", [P, 8, T], BF16)
        xsb = sb("xsb", [P, 12, T], BF16)
        yssd = sb("yssd", [P, 8, T], BF16)
        lhalo = sb("lhalo", [P, 8, 3])
        shalo = sb("shalo", [P, 12, 3])
        hprev = sb("hprev", [P, 8])
        Sst = sb("Sst", [P, D])
        Sb = sb("Sb", [P, D], BF16)
        dtv = sb("dtv", [P, 16])
        dtt = sb("dtt", [P, 16])
        dA = sb("dA", [P, 16])
        sml = sb("sml", [P, 48])
        Ubig = sb("Ubig", [P, 8, P])
        Lm = sb("Lm", [P, 8, P])
        MT = sb("MT", [P, 8, P], BF16)
        CBm = sb("CBm", [P, P])
        xdt = sb("xdt", [P, D], BF16)
        xdd = sb("xdd", [P, D], BF16)
        xD = sb("xD", [P, 512])
        Btm = sb("Btm", [P, 256], BF16)
        yo = sb("yo", [P, 512])
        tz = sb("tz", [P, 512])
        yz = sb("yz", [P, 512])
        yn = sb("yn", [P, 512], BF16)
        gss = sb("gss", [P, 8])
        osb = sb("osb", [P, D])
        sto = sb("sto", [P, P])
        hT = sb("hT", [8, P])
        stg = xin[1]
        badag = gate
        rsel = osb
        wadab = osb.bitcast(BF16)

        ps = [st.enter_context(nc.psum_tensor("ps%d" % i, [P, 512], F32)) for i in range(8)]
        PA = [ps[0], ps[1]]
        PG, PT, PSG0, PSG1, PSM, PY = ps[2], ps[3], ps[4], ps[5], ps[6], ps[7]
        PTb = PT.bitcast(BF16)
        PT1b = ps[2].bitcast(BF16)

        B = {}

        PSUM_NAMES = ("PA0", "PA1", "PG", "PT", "PSG0", "PSG1", "PSM", "PY")

        def b(name):
            if name.startswith("PSM"):
                name = "PSM"
            if name not in B:
                B[name] = Buf(name, excl=name in PSUM_NAMES)
            return B[name]

        bPA = [b("PA0"), b("PA1")]

        ident = cst[:, 0:P]
        Umat = cst[:, P:2 * P]
        Vmat = cst[:, 2 * P:3 * P]
        ones = cst[:, 3 * P:4 * P]

        C_NG, C_LCW, C_LCB, C_LBA, C_LBX, C_LAM, C_SCW, C_SCB, C_GSSD, C_BADA = (
            0, 8, 40, 48, 56, 64, 72, 120, 132, 140)
        DC_NBA, DC_NBX, DC_M8, DC_M16 = 0, 8, 16, 24
        R_DTB, R_A, R_D = 0, 16, 32

        def e_ts(out, in0, s1, s2, op0, op1=None):
            if op1 is None:
                return lambda e: e.tensor_scalar(out, in0, s1, None, op0)
            return lambda e: e.tensor_scalar(out, in0, s1, s2, op0, op1)

        def e_tt(out, a, bb, op):
            return lambda e: e.tensor_tensor(out, a, bb, op)

        def e_stt(out, in0, sc, in1, op0, op1):
            return lambda e: e.scalar_tensor_tensor(out, in0, sc, in1, op0, op1)

        def e_cp(out, in_):
            return lambda e: e.tensor_copy(out, in_)

        def e_acp(out, in_):
            return lambda e: e.copy(out, in_)

        def e_act(out, in_, func, bias=0.0, scale=1.0, accum=None):
            if accum is None:
                return lambda e: e.activation(out, in_, func, bias=bias, scale=scale)
            return lambda e: e.activation(out, in_, func, bias=bias, scale=scale, accum_out=accum)

        def e_mm(items):
            def emit(e):
                ins = None
                for (o, l, r, s0, s1) in items:
                    ins = e.matmul(o, l, r, start=s0, stop=s1)
                return ins
            return emit

        def e_tr(items):
            def emit(e):
                ins = None
                for (o, i, idn) in items:
                    ins = e.transpose(o, i, idn)
                return ins
            return emit

        def e_memset(ap, v):
            return lambda e: e.memset(ap, v)

        try:
            S.dma("sp", [(cst[:], consts)], writes=[b("cst")])
            S.op("dve", e_cp(identb[:], ident), reads=[b("cst")], writes=[b("identb")])

            def colload(off, n, src):
                S.dma("sp", [(colp[:, off:off + n], src.rearrange("(c p) -> p c", p=P))],
                      writes=[b("colp")], owner=b("colp_d%d" % off), allow_slow_non_contiguous=True)

            colload(C_NG, 8, norm_g)
            colload(C_LCB, 8, lru_conv_b)
            colload(C_LBA, 8, lru_b_a)
            colload(C_LBX, 8, lru_b_x)
            colload(C_LAM, 8, lru_lambda)
            colload(C_SCB, 12, ssd_conv_b)
            colload(C_GSSD, 8, ssd_norm_g)
            colload(C_BADA, 16, b_ada[0:2 * D])
            for k in range(4):
                colload(C_LCW + 8 * k, 8, lru_conv_w[k])
                colload(C_SCW + 12 * k, 12, ssd_conv_w[k])
            S.dma("sp", [(rows[:, R_DTB:R_DTB + 16], ssd_dt_bias.partition_broadcast(P)),
                         (rows[:, R_A:R_A + 16], ssd_a_log.partition_broadcast(P)),
                         (rows[:, R_D:R_D + 16], ssd_d.partition_broadcast(P))], writes=[b("rows")])
            S.dma("sp", [(fng[:], final_norm_g.partition_broadcast(P))], writes=[b("fng")])
            S.dma("sp", [(badag[0:NSEQ, :], b_ada[2 * D:3 * D].partition_broadcast(NSEQ))], writes=[b("gate")])
            S.dma("sp", [(cT[:, :, r], cc[r].rearrange("(k p) -> p k", p=P)) for r in range(NSEQ)], writes=[b("cT")],
                  allow_slow_non_contiguous=True)

            S.op("dve", e_ts(dcol[:, DC_NBA:DC_NBA + 16], colp[:, C_LBA:C_LBA + 16], -1.0, None, ALU.mult),
                 reads=[b("colp")], writes=[b("dcol")])
            S.op("act", e_act(dcol[:, 32:40], colp[:, C_LAM:C_LAM + 8], AF.Exp, scale=-1.0),
                 reads=[b("colp")], writes=[b("dcol")])
            S.op("act", e_act(dcol[:, 32:40], dcol[:, 32:40], AF.Ln, bias=1.0), reads=[b("dcol")], writes=[b("dcol")])
            S.op("dve", e_ts(dcol[:, DC_M8:DC_M8 + 8], dcol[:, 32:40], -8.0, None, ALU.mult),
                 reads=[b("dcol")], writes=[b("dcol")])
            S.op("dve", e_ts(dcol[:, DC_M16:DC_M16 + 8], dcol[:, 32:40], -16.0, None, ALU.mult),
                 reads=[b("dcol")], writes=[b("dcol")])
            S.op("act", e_act(rows[:, R_A:R_A + 16], rows[:, R_A:R_A + 16], AF.Exp), reads=[b("rows")], writes=[b("rows")])
            S.op("dve", e_ts(rows[:, R_A:R_A + 16], rows[:, R_A:R_A + 16], -1.0, None, ALU.mult),
                 reads=[b("rows")], writes=[b("rows")])

            w_in_v = w_in.rearrange("(k p) n -> p k n", p=P)
            for k in range(KC):
                S.dma("pool", [(Win[:, k, :], w_in_v[:, k, :])], writes=[b("Win")], owner=b("Win_d%d" % k))
            S.dma("pool", [(Wa[:], lru_w_a.rearrange("h i j -> i h j")),
                           (Wx[:], lru_w_x.rearrange("h i j -> i h j"))], writes=[b("Wg")])
            w_out_v = w_out.rearrange("(k p) n -> p k n", p=P)
            for k in range(8):
                S.dma("pool", [(Wout[:, k, :], w_out_v[:, k, :])], writes=[b("Wout")], owner=b("Wout_d%d" % k))
            for k in range(8, 16):
                S.dma("sp", [(stg[:], w_out_v[:, k, :])], writes=[b("xin1")])
                S.op("dve", e_ts(Wout[:, k, :], stg[:], colp[:, C_GSSD + k - 8:C_GSSD + k - 7], None, ALU.mult),
                     reads=[b("xin1"), b("colp")], writes=[b("Wout")])

            cTf = cT[:].rearrange("p k r -> p (k r)")
            S.op("act", e_act(modc[:].rearrange("p a r -> p (a r)")[:, 0:KC * NSEQ], cTf, AF.Exp, scale=-1.0),
                 reads=[b("cT")], writes=[b("modc")])
            mtmp = modc[:].rearrange("p a r -> p (a r)")[:, 0:KC * NSEQ]
            S.op("act", e_act(mtmp, mtmp, AF.Ln, bias=1.0), reads=[b("modc")], writes=[b("modc")])
            S.op("act", e_act(mtmp, mtmp, AF.Exp, scale=-1.0), reads=[b("modc")], writes=[b("modc")])
            S.op("dve", e_tt(cTb[:].rearrange("p k r -> p (k r)"), cTf, mtmp, ALU.mult),
                 reads=[b("cT"), b("modc")], writes=[b("cTb")])
            w_ada_v = w_ada.rearrange("(k p) n -> p k n", p=P)
            wadab3 = wadab[:, 0:KC * 256].rearrange("p (k n) -> p k n", k=KC)
            PSMc = PSM[:, 0:16 * NSEQ].rearrange("p (a r) -> p a r", a=16)
            for blk in range(12):
                S.dma("pool", [(wadab3, w_ada_v[:, :, blk * 256:(blk + 1) * 256])], writes=[b("osb")])
                if blk < 8:
                    for half in range(2):
                        cb = blk * 2 + half
                        S.op("pe", e_mm([(PSMc[:, cb, :], wadab3[:, k, half * P:(half + 1) * P], cTb[:, k, :],
                                          k == 0, k == KC - 1) for k in range(KC)]),
                             reads=[b("osb"), b("cTb")], writes=[b("PSM")])
                else:
                    gb_ = blk - 8
                    bank = PA[gb_ // 2]
                    S.op("pe", e_mm([(bank[0:NSEQ, (gb_ % 2) * 256:(gb_ % 2) * 256 + 256], cTb[:, k, :],
                                      wadab3[:, k, :], k == 0, k == KC - 1) for k in range(KC)]),
                         reads=[b("osb"), b("cTb")], writes=[bPA[gb_ // 2]])
            S.op("dve", e_tt(modc[:], PSMc, colp[:, C_BADA:C_BADA + 16].unsqueeze(2).to_broadcast([P, 16, NSEQ]), ALU.add),
                 reads=[b("PSM"), b("colp")], writes=[b("modc")])
            S.op("dve", e_ts(gsc[:], modc[:, 8:16, :], 1.0, None, ALU.add), reads=[b("modc")], writes=[b("gsc")])
            S.op("dve", e_tt(gsc[:], gsc[:], colp[:, C_NG:C_NG + 8].unsqueeze(2).to_broadcast([P, KC, NSEQ]), ALU.mult),
                 reads=[b("gsc"), b("colp")], writes=[b("gsc")])
            for h2 in range(2):
                S.op("dve", e_tt(modg[0:NSEQ, h2 * 512:(h2 + 1) * 512], PA[h2][0:NSEQ, :],
                                 badag[0:NSEQ, h2 * 512:(h2 + 1) * 512], ALU.add),
                     reads=[bPA[h2], b("gate")], writes=[b("modg")])

            mark("setup")
            def seq_views(si):
                if si < NP:
                    return xp[si], yp[si], 0, si, LP
                return xs_in[si - NP], ys[si - NP], 1, si - NP, LS

            tile_ctr = [0]

            seq_order = list(range(NP, NSEQ)) + list(range(NP))
            for si in (seq_order if not DBG_REV else seq_order[::-1]):
                xsrc, ydst, grp, gi, L = seq_views(si)
                if DBG_BARRIER:
                    S.barrier(stores)
                is_prompt = grp == 0
                Tt = min(T, L)
                Q = min(P, Tt)
                NSUB = Tt // Q
                NT = L // Tt
                assert L % Tt == 0 and Tt % Q == 0

                if is_prompt:
                    S.op("pool", e_memset(hprev[:], 0.0), writes=[b("hprev")])
                    S.op("pool", e_memset(lhalo[:], 0.0), writes=[b("lhalo")])
                    S.op("pool", e_memset(shalo[:], 0.0), writes=[b("shalo")])
                    S.op("pool", e_memset(Sst[:], 0.0), writes=[b("Sst")])
                    S.op("pool", e_memset(Sb[:], 0.0), writes=[b("Sb")])
                else:
                    S.dma("sp", [(xin[0][0:3, :], st_lc[gi]), (xin[1][0:3, :], st_sc[gi][:, 0:1024]),
                                 (osb[0:3, 0:512], st_sc[gi][:, 1024:1536])],
                          writes=[b("xin0"), b("xin1"), b("osb")], owner=b("stld"))
                    S.dma("sp", [(osb[32:33, :], st_lh[gi].rearrange("(o d) -> o d", o=1))], writes=[b("osb")], owner=b("stld2"))
                    PSs = PSG1[:, 0:64]
                    S.op("pe", e_tr([(PSs[:, c * 3:(c + 1) * 3], xin[0][0:3, c * P:(c + 1) * P], ident[0:3, 0:3]) for c in range(8)]
                                    + [(PSs[:, 24 + c * 3:24 + (c + 1) * 3], (xin[1][0:3, c * P:(c + 1) * P] if c < 8 else
                                                                          osb[0:3, (c - 8) * P:(c - 7) * P]), ident[0:3, 0:3]) for c in range(12)]),
                         reads=[b("xin0"), b("xin1"), b("osb"), b("cst")], writes=[b("PSG1")])
                    S.op("act", e_acp(lhalo[:].rearrange("p c j -> p (c j)"), PSs[:, 0:24]), reads=[b("PSG1")], writes=[b("lhalo")])
                    S.op("act", e_acp(shalo[:].rearrange("p c j -> p (c j)"), PSs[:, 24:60]), reads=[b("PSG1")], writes=[b("shalo")])
                    S.op("pe", e_tr([(PSG1[:, 64 + c:65 + c], osb[32:33, c * P:(c + 1) * P], ident[32:33, 32:33]) for c in range(8)]),
                         reads=[b("osb"), b("cst")], writes=[b("PSG1")])
                    S.op("act", e_acp(hprev[:], PSG1[:, 64:72]), reads=[b("PSG1")], writes=[b("hprev")])
                    for c4 in range(2):
                        S.dma("sp", [(osb[:, c4 * 512:(c4 + 1) * 512].rearrange("p (c n) -> p c n", c=4),
                                      st_ss[gi, c4 * 512:(c4 + 1) * 512, :].rearrange("(c p) n -> p c n", p=P))],
                              writes=[b("osb")])
                        S.op("pe", e_tr([(PSG0[:, c * P:(c + 1) * P], osb[:, c4 * 512 + c * P:c4 * 512 + (c + 1) * P], ident)
                                         for c in range(4)]),
                             reads=[b("osb"), b("cst")], writes=[b("PSG0")])
                        S.op("act", e_acp(Sst[:, c4 * 512:(c4 + 1) * 512], PSG0[:, :]), reads=[b("PSG0")], writes=[b("Sst")])
                    S.op("pool", e_cp(Sb[:], Sst[:]), reads=[b("Sst")], writes=[b("Sb")])
                S.op("dve", e_ts(rsel[0:NSEQ, :], modg[0:NSEQ, :], ident[0:NSEQ, si:si + 1], None, ALU.mult),
                     reads=[b("modg"), b("cst")], writes=[b("osb")])
                for h2 in range(2):
                    bank, bk = (PSG0, b("PSG0")) if h2 == 0 else (PSG1, b("PSG1"))
                    S.op("pe", e_mm([(bank[:, :], ones[0:NSEQ, :], rsel[0:NSEQ, h2 * 512:(h2 + 1) * 512], True, True)]),
                         reads=[b("osb"), b("cst")], writes=[bk])
                    S.op("act", e_acp(gate[:, h2 * 512:(h2 + 1) * 512], bank[:, :]), reads=[bk], writes=[b("gate")])

                if DBG_STOP == "dump_gate":
                    dbg = dout("dbg", [P, D])
                    stores.append(S.dma("sp", [(dbg, gate[:])], reads=[b("gate")], owner=b("dbg0")))
                    raise _Stop()
                mark("seqinit")
                def proj_fm(col0, bank, bbank, hn_cur, bhn):
                    S.op("pe", e_mm([(bank[:, 0:Tt], Win[:, k, col0:col0 + P], hn_cur[:, k, 0:Tt],
                                      k == 0, k == KC - 1) for k in range(KC)]),
                         reads=[b("Win"), bhn], writes=[bbank])

                def sigmoid_act(dst, src, src_reads, nbias):
                    S.op("act", e_act(dst, src, AF.Exp, bias=nbias, scale=-1.0), reads=src_reads, writes=[b(dst.tensor.name)])
                    S.op("act", e_act(dst, dst, AF.Ln, bias=1.0), reads=[b(dst.tensor.name)], writes=[b(dst.tensor.name)])
                    S.op("act", e_act(dst, dst, AF.Exp, scale=-1.0), reads=[b(dst.tensor.name)], writes=[b(dst.tensor.name)])

                def conv4(dst, src, wcol, bcol, nchunks, c, nsrc="lx", ndst="u"):
                    S.op("dve", e_ts(dst[:, 0:Tt], src[:, 0:Tt], colp[:, wcol + c:wcol + c + 1],
                                     colp[:, bcol + c:bcol + c + 1], ALU.mult, ALU.add),
                         reads=[b(nsrc), b("colp")], writes=[b(ndst)])
                    for k in range(1, 4):
                        S.op("dve", e_stt(dst[:, 0:Tt], src[:, k:k + Tt],
                                          colp[:, wcol + k * nchunks + c:wcol + k * nchunks + c + 1],
                                          dst[:, 0:Tt], ALU.mult, ALU.add),
                             reads=[b(nsrc), b(ndst), b("colp")], writes=[b(ndst)])

                def sec_stage1(ti):
                    t0 = ti * Tt
                    hn_cur = hnTs[ti % 2]
                    bhn = b("hnT%d" % (ti % 2))
                    for j in range(NSUB):
                        xb_ = xin[j % 2]
                        bxin = b("xin%d" % (j % 2))
                        S.dma("sp", [(xb_[0:Q, :], xsrc[t0 + j * Q:t0 + (j + 1) * Q, :])], writes=[bxin])
                        S.op("act", e_act(junk[0:Q, :], xb_[0:Q, :], AF.Square, accum=ssq[0:Q, 0:1]),
                             reads=[bxin], writes=[b("junk"), b("ssq")])
                        S.op("act", e_act(ssq[0:Q, 1:2], ssq[0:Q, 0:1], AF.Ln, bias=EPS, scale=1.0 / D),
                             reads=[b("ssq")], writes=[b("ssq")])
                        S.op("act", e_act(ssq[0:Q, 2:3], ssq[0:Q, 1:2], AF.Exp, scale=-0.5),
                             reads=[b("ssq")], writes=[b("ssq")])
                        S.op("pool", e_ts(xn[0:Q, :], xb_[0:Q, :], ssq[0:Q, 2:3], None, ALU.mult),
                             reads=[bxin, b("ssq")], writes=[b("xn")])
                        PT3 = PT1b[:, 0:KC * P].rearrange("p (k q) -> p k q", k=KC)
                        S.op("pe", e_tr([(PT3[:, k, 0:Q], xn[0:Q, k * P:(k + 1) * P], identb[0:Q, 0:Q]) for k in range(KC)]),
                             reads=[b("xn"), b("identb")], writes=[b("PG")])
                        for k in range(KC):
                            if k % 2 == 0:
                                S.op("act", e_act(hn_cur[:, k, j * Q:(j + 1) * Q], PT3[:, k, 0:Q], AF.Identity,
                                                  bias=modc[:, k, si:si + 1], scale=gsc[:, k, si:si + 1]),
                                     reads=[b("PG"), b("modc"), b("gsc")], writes=[bhn])
                            else:
                                S.op("dve", e_ts(hn_cur[:, k, j * Q:(j + 1) * Q], PT3[:, k, 0:Q], gsc[:, k, si:si + 1],
                                                 modc[:, k, si:si + 1], ALU.mult, ALU.add),
                                     reads=[b("PG"), b("modc"), b("gsc")], writes=[bhn])

                def sec_lru(ti):
                    t0 = ti * Tt
                    hn_cur = hnTs[ti % 2]
                    bhn = b("hnT%d" % (ti % 2))
                    pa_i = 0
                    for h in range(8):
                        S.op("pool", e_cp(lx[:, 0:3], lhalo[:, h, :]), reads=[b("lhalo")], writes=[b("lx")])
                        proj_fm(OFF_LX + h * P, ps[0], b("PA0"), hn_cur, bhn)
                        S.op("act", e_acp(lx[:, 3:3 + Tt], ps[0][:, 0:Tt]), reads=[b("PA0")], writes=[b("lx")])
                        pa_i ^= 1
                        S.op("pool", e_cp(lhalo[:, h, :], lx[:, Tt:Tt + 3]), reads=[b("lx")], writes=[b("lhalo")])
                        conv4(u, lx, C_LCW, C_LCB, 8, h)
                        S.op("pool", e_cp(ub[:, 0:Tt], u[:, 0:Tt]), reads=[b("u")], writes=[b("ub")])
                        PG2 = ps[1][:, 0:2 * Tt].rearrange("p (a t) -> p a t", a=2)
                        S.op("pe", e_mm([(PG2[:, 0, :], Wa[:, h, :], ub[:, 0:Tt], True, True),
                                         (PG2[:, 1, :], Wx[:, h, :], ub[:, 0:Tt], True, True)]),
                             reads=[b("Wg"), b("ub")], writes=[b("PA1")])
                        sigmoid_act(ga[:, 0:Tt], PG2[:, 0, :], [b("PA1"), b("dcol")], dcol[:, DC_NBA + h:DC_NBA + h + 1])
                        S.op("act", e_act(aa[:, 0:Tt], ga[:, 0:Tt], AF.Exp, scale=dcol[:, DC_M8 + h:DC_M8 + h + 1]),
                             reads=[b("ga"), b("dcol")], writes=[b("aa")])
                        S.op("act", e_act(gm[:, 0:Tt], ga[:, 0:Tt], AF.Exp, scale=dcol[:, DC_M16 + h:DC_M16 + h + 1]),
                             reads=[b("ga"), b("dcol")], writes=[b("gm")])
                        S.op("act", e_act(gm[:, 0:Tt], gm[:, 0:Tt], AF.Ln, bias=1.0, scale=-1.0), reads=[b("gm")], writes=[b("gm")])
                        S.op("act", e_act(gm[:, 0:Tt], gm[:, 0:Tt], AF.Exp, scale=0.5), reads=[b("gm")], writes=[b("gm")])
                        sigmoid_act(gx[:, 0:Tt], PG2[:, 1, :], [b("PA1"), b("dcol")], dcol[:, DC_NBX + h:DC_NBX + h + 1])
                        S.op("dve", e_tt(gb[:, 0:Tt], gx[:, 0:Tt], u[:, 0:Tt], ALU.mult), reads=[b("gx"), b("u")], writes=[b("gb")])
                        if is_prompt and ti == 0:
                            S.op("pool", e_memset(gm[:, 0:1], 1.0), reads=[], writes=[b("gm")])
                        S.op("dve", e_tt(gb[:, 0:Tt], gb[:, 0:Tt], gm[:, 0:Tt], ALU.mult), reads=[b("gb"), b("gm")], writes=[b("gb")])
                        S.op("dve", (lambda h=h: (lambda e: e.tensor_tensor_scan(hh[:, 0:Tt], aa[:, 0:Tt], gb[:, 0:Tt],
                                                                                   hprev[:, h:h + 1], ALU.mult, ALU.add)))(),
                             reads=[b("aa"), b("gb"), b("hprev")], writes=[b("hh")])
                        S.op("pool", e_cp(hprev[:, h:h + 1], hh[:, Tt - 1:Tt]), reads=[b("hh")], writes=[b("hprev")])
                        if DBG_STOP == "dump_lru" and h == DBG_H and ti == DBG_TI:
                            dbg = dout("dbg", [P, 8 * 256])
                            for i_, (t_, nm_) in enumerate([(lx, "lx"), (u, "u"), (ga, "ga"), (aa, "aa"), (gm, "gm"), (gx, "gx"), (gb, "gb"), (hh, "hh")]):
                                stores.append(S.dma("sp", [(dbg[:, i_ * 256:(i_ + 1) * 256], t_[:, 0:256])], reads=[b(nm_)], owner=b("dbg%d" % i_)))
                            raise _Stop()
                        proj_fm(OFF_LG + h * P, ps[0], b("PA0"), hn_cur, bhn)
                        sigmoid_act(gg[:, 0:Tt], ps[0][:, 0:Tt], [b("PA0")], 0.0)
                        S.op("dve", e_tt(hh[:, 0:Tt], hh[:, 0:Tt], ps[0][:, 0:Tt], ALU.mult),
                             reads=[b("hh"), b("PA0")], writes=[b("hh")])
                        S.op("dve", e_tt(ylru[:, h, 0:Tt], hh[:, 0:Tt], gg[:, 0:Tt], ALU.mult),
                             reads=[b("hh"), b("gg")], writes=[b("ylru")])
                        pa_i ^= 1

                def sec_xbc(ti):
                    t0 = ti * Tt
                    hn_cur = hnTs[ti % 2]
                    bhn = b("hnT%d" % (ti % 2))
                    pa_i = 0
                    for c in range(12):
                        S.op("pool", e_cp(lx2[:, 0:3], shalo[:, c, :]), reads=[b("shalo")], writes=[b("lx2")])
                        proj_fm(OFF_XBC + c * P, ps[4 + pa_i], b("PSG%d" % pa_i), hn_cur, bhn)
                        S.op("act", e_acp(lx2[:, 3:3 + Tt], ps[4 + pa_i][:, 0:Tt]), reads=[b("PSG%d" % pa_i)], writes=[b("lx2")])
                        pa_i ^= 1
                        S.op("pool", e_cp(shalo[:, c, :], lx2[:, Tt:Tt + 3]), reads=[b("lx2")], writes=[b("shalo")])
                        conv4(u2, lx2, C_SCW, C_SCB, 12, c, "lx2", "u2")
                        sigmoid_act(ga2[:, 0:Tt], u2[:, 0:Tt], [b("u2")], 0.0)
                        S.op("dve", e_tt(xsb[:, c, 0:Tt], u2[:, 0:Tt], ga2[:, 0:Tt], ALU.mult),
                             reads=[b("u2"), b("ga2")], writes=[b("xsb")])

                def sec_ssd(ti):
                    t0 = ti * Tt
                    hn_cur = hnTs[ti % 2]
                    bhn = b("hnT%d" % (ti % 2))
                    for j in range(NSUB):
                        c0 = j * Q
                        PSM_dt = PSM[0:Q, 0:16]
                        S.op("pe", e_mm([(PSM_dt, hn_cur[:, k, c0:c0 + Q], Win[:, k, OFF_DT:OFF_DT + 16], k == 0, k == KC - 1)
                                         for k in range(KC)]),
                             reads=[bhn, b("Win")], writes=[b("PSM_dt")])
                        S.op("dve", e_tt(dtv[0:Q, :], PSM_dt, rows[0:Q, R_DTB:R_DTB + 16], ALU.add),
                             reads=[b("PSM_dt"), b("rows")], writes=[b("dtv")])
                        S.op("dve", e_ts(dtv[0:Q, :], dtv[0:Q, :], 30.0, None, ALU.min), reads=[b("dtv")], writes=[b("dtv")])
                        S.op("act", e_act(dtv[0:Q, :], dtv[0:Q, :], AF.Exp), reads=[b("dtv")], writes=[b("dtv")])
                        S.op("act", e_act(dtt[0:Q, :], dtv[0:Q, :], AF.Ln, bias=1.0), reads=[b("dtv")], writes=[b("dtt")])
                        S.op("dve", e_tt(dA[0:Q, :], dtt[0:Q, :], rows[0:Q, R_A:R_A + 16], ALU.mult),
                             reads=[b("dtt"), b("rows")], writes=[b("dA")])
                        mark("ssd_dt")
                        S.op("pe", e_mm([(PSM[0:Q, 64:80], Vmat[0:Q, 0:Q], dA[0:Q, :], True, True),
                                         (PSM[0:Q, 80:96], Umat[0:Q, 0:Q], dA[0:Q, :], True, True),
                                         (PSM[:, 96:112], ones[0:Q, :], dA[0:Q, :], True, True)]),
                             reads=[b("dA"), b("cst")], writes=[b("PSM_sm")])
                        S.op("act", e_act(sml[0:Q, 0:32], PSM[0:Q, 64:96], AF.Exp), reads=[b("PSM_sm")], writes=[b("sml")])
                        S.op("act", e_act(sml[:, 32:48], PSM[:, 96:112], AF.Exp), reads=[b("PSM_sm")], writes=[b("sml")])
                        mark("ssd_dec")
                        S.op("pe", e_tr([(PTb[0:Q, c * P:(c + 1) * P], xsb[:, c, c0:c0 + Q], identb[:, :]) for c in range(8)]),
                             reads=[b("xsb"), b("identb")], writes=[b("PT")])
                        PSMb = PSM.bitcast(BF16)
                        S.op("pe", e_tr([(PSMb[0:Q, 512 + g * P:512 + (g + 1) * P], xsb[:, 8 + g, c0:c0 + Q], identb[:, :])
                                         for g in range(2)]),
                             reads=[b("xsb"), b("identb")], writes=[b("PSM_bt")])
                        S.op("act", e_acp(Btm[0:Q, :], PSMb[0:Q, 512:768]), reads=[b("PSM_bt")], writes=[b("Btm")])
                        PT4 = PTb[0:Q, :].rearrange("p (k d) -> p k d", k=16)
                        dt_b = dtt[0:Q, :].unsqueeze(2).to_broadcast([Q, 16, 64])
                        S.op("dve", e_tt(xdt[0:Q, :].rearrange("p (k d) -> p k d", k=16), PT4, dt_b, ALU.mult),
                             reads=[b("PT"), b("dtt")], writes=[b("xdt")])
                        dec_b = sml[0:Q, 16:32].unsqueeze(2).to_broadcast([Q, 16, 64])
                        S.op("pool", e_tt(xdd[0:Q, :].rearrange("p (k d) -> p k d", k=16),
                                          xdt[0:Q, :].rearrange("p (k d) -> p k d", k=16), dec_b, ALU.mult),
                             reads=[b("xdt"), b("sml")], writes=[b("xdd")])

                        mark("ssd_tr")
                        for g in range(2):
                            S.op("pe", e_mm([(PSM[0:Q, 128:128 + Q], xsb[:, 8 + g, c0:c0 + Q], xsb[:, 10 + g, c0:c0 + Q], True, True)]),
                                 reads=[b("xsb")], writes=[b("PSM_cb")])
                            mark("ssd_cbmm")
                            if DBG_STOP == "exp1" and g == 1:
                                S.op("dve", e_memset(CBm[0:Q, 0:Q], 0.0), reads=[b("PSM_cb")], writes=[b("CBm")])
                                raise _Stop()
                            if DBG_STOP == "exp2" and g == 1:
                                S.op("dve", e_tt(CBm[0:Q, 0:Q], PSM[0:Q, 128:128 + Q], Vmat[0:Q, 0:Q], ALU.mult), reads=[b("cst")], writes=[b("CBm")])
                                raise _Stop()
                            S.op("dve", e_tt(CBm[0:Q, 0:Q], PSM[0:Q, 128:128 + Q], Vmat[0:Q, 0:Q], ALU.mult),
                                 reads=[b("PSM_cb"), b("cst")], writes=[b("CBm")])
                            mark("ssd_cb")
                            dA_b = dA[0:Q, g * 8:(g + 1) * 8].unsqueeze(2).to_broadcast([Q, 8, Q])
                            U_b = Umat[0:Q, 0:Q].unsqueeze(1).to_broadcast([Q, 8, Q])
                            S.op("dve", e_tt(Ubig[0:Q, :, 0:Q], U_b, dA_b, ALU.mult),
                                 reads=[b("dA"), b("cst")], writes=[b("Ubig")])
                            for hf in range(2):
                                bank, bk = (PSG0, b("PSG0")) if hf == 0 else (PSG1, b("PSG1"))
                                bank3 = bank[0:Q, :].rearrange("p (k l) -> p k l", k=4)
                                S.op("pe", e_mm([(bank3[:, k, 0:Q], Ubig[0:Q, hf * 4 + k, 0:Q], Vmat[0:Q, 0:Q], True, True)
                                                 for k in range(4)]),
                                     reads=[b("Ubig"), b("cst")], writes=[bk])
                                S.op("act", e_act(Lm[0:Q, hf * 4:(hf + 1) * 4, 0:Q], bank3[:, :, 0:Q], AF.Exp),
                                     reads=[bk], writes=[b("Lm")])
                            CB_b = CBm[0:Q, 0:Q].unsqueeze(1).to_broadcast([Q, 8, Q])
                            S.op("dve", e_tt(MT[0:Q, :, 0:Q], Lm[0:Q, :, 0:Q], CB_b, ALU.mult),
                                 reads=[b("Lm"), b("CBm")], writes=[b("MT")])
                            mark("ssd_L")
                            S.op("pe", e_mm([(PY[0:Q, k * 64:(k + 1) * 64], MT[0:Q, k, 0:Q],
                                              xdt[0:Q, g * 512 + k * 64:g * 512 + (k + 1) * 64], True, True) for k in range(8)]),
                                 reads=[b("MT"), b("xdt")], writes=[b("PY")])
                            S.op("pe", e_mm([(ps[4][0:Q, :], xsb[:, 10 + g, c0:c0 + Q], Sb[:, g * 512:(g + 1) * 512], True, True)]),
                                 reads=[b("xsb"), b("Sb")], writes=[b("PSG0")])
                            S.op("pe", e_mm([(ps[5][:, :], Btm[0:Q, g * P:(g + 1) * P], xdd[0:Q, g * 512:(g + 1) * 512], True, True)]),
                                 reads=[b("Btm"), b("xdd")], writes=[b("PSG1")])
                            mark("ssd_mm")
                            eA_b = sml[0:Q, g * 8:(g + 1) * 8].unsqueeze(2).to_broadcast([Q, 8, 64])
                            D_b = rows[0:Q, R_D + g * 8:R_D + (g + 1) * 8].unsqueeze(2).to_broadcast([Q, 8, 64])
                            S.op("dve", e_tt(yo[0:Q, :].rearrange("p (k d) -> p k d", k=8),
                                             ps[4][0:Q, :].rearrange("p (k d) -> p k d", k=8), eA_b, ALU.mult),
                                 reads=[b("PSG0"), b("sml")], writes=[b("yo")])
                            S.op("dve", e_tt(xD[0:Q, :].rearrange("p (k d) -> p k d", k=8),
                                             PT4[:, g * 8:(g + 1) * 8, :], D_b, ALU.mult),
                                 reads=[b("PT"), b("rows")], writes=[b("xD")])
                            S.op("pool", e_tt(yo[0:Q, :], yo[0:Q, :], xD[0:Q, :], ALU.add), reads=[b("yo"), b("xD")], writes=[b("yo")])
                            S.op("dve", e_tt(yo[0:Q, :], yo[0:Q, :], PY[0:Q, :], ALU.add), reads=[b("yo"), b("PY")], writes=[b("yo")])
                            mark("ssd_y")
                            ed_b = sml[:, 32 + g * 8:32 + (g + 1) * 8].unsqueeze(2).to_broadcast([P, 8, 64])
                            Sg = Sst[:, g * 512:(g + 1) * 512]
                            S.op("pool", e_tt(Sg.rearrange("p (k d) -> p k d", k=8), Sg.rearrange("p (k d) -> p k d", k=8),
                                              ed_b, ALU.mult), reads=[b("Sst"), b("sml")], writes=[b("Sst")])
                            S.op("dve", e_tt(Sg, Sg, ps[5][:, :], ALU.add), reads=[b("Sst"), b("PSG1")], writes=[b("Sst")])
                            S.op("act", e_acp(Sb[:, g * 512:(g + 1) * 512], Sg), reads=[b("Sst")], writes=[b("Sb")])
                            mark("ssd_st")
                            S.op("pe", e_mm([(ps[7][0:Q, :], hn_cur[:, k, c0:c0 + Q], Win[:, k, OFF_Z + g * 512:OFF_Z + (g + 1) * 512],
                                              k == 0, k == KC - 1) for k in range(KC)]),
                                 reads=[bhn, b("Win")], writes=[b("PY")])
                            sigmoid_act(tz[0:Q, :], ps[7][0:Q, :], [b("PY")], 0.0)
                            S.op("dve", e_tt(yz[0:Q, :], yo[0:Q, :], ps[7][0:Q, :], ALU.mult), reads=[b("yo"), b("PY")], writes=[b("yz")])
                            S.op("pool", e_tt(yz[0:Q, :], yz[0:Q, :], tz[0:Q, :], ALU.mult), reads=[b("yz"), b("tz")], writes=[b("yz")])
                            mark("ssd_z")
                            S.op("act", e_act(junk[0:Q, 0:512], yz[0:Q, :], AF.Square, accum=gss[0:Q, 0:1]),
                                 reads=[b("yz")], writes=[b("junk"), b("gss")])
                            S.op("act", e_act(gss[0:Q, 1:2], gss[0:Q, 0:1], AF.Ln, bias=EPS, scale=1.0 / 512),
                                 reads=[b("gss")], writes=[b("gss")])
                            S.op("act", e_act(gss[0:Q, 2:3], gss[0:Q, 1:2], AF.Exp, scale=-0.5), reads=[b("gss")], writes=[b("gss")])
                            S.op("pool", e_ts(yn[0:Q, :], yz[0:Q, :], gss[0:Q, 2:3], None, ALU.mult),
                                 reads=[b("yz"), b("gss")], writes=[b("yn")])
                            mark("ssd_n")
                            PSMy = PSMb[:, 512:512 + 4 * Q].rearrange("p (c q) -> p c q", c=4)
                            S.op("pe", e_tr([(PSMy[:, c, :], yn[0:Q, c * P:(c + 1) * P], identb[0:Q, 0:Q]) for c in range(4)]),
                                 reads=[b("yn"), b("identb")], writes=[b("PSM_bt")])
                            mark("ssd_ytr")
                            S.op("act", e_acp(yssd[:, g * 4:(g + 1) * 4, c0:c0 + Q], PSMy), reads=[b("PSM_bt")], writes=[b("yssd")])
                            mark("ssd_yev")

                def sec_out(ti):
                    t0 = ti * Tt
                    hn_cur = hnTs[ti % 2]
                    bhn = b("hnT%d" % (ti % 2))
                    for j in range(NSUB):
                        c0 = j * Q
                        xr = xin[j % 2]
                        bxr = b("xin%d" % (j % 2))
                        S.dma("sp", [(xr[0:Q, :], xsrc[t0 + c0:t0 + c0 + Q, :])], writes=[bxr])
                        for h2 in range(2):
                            items = []
                            for k in range(16):
                                lhs = ylru[:, k, c0:c0 + Q] if k < 8 else yssd[:, k - 8, c0:c0 + Q]
                                items.append((PA[h2][0:Q, :], lhs, Wout[:, k, h2 * 512:(h2 + 1) * 512], k == 0, k == 15))
                            S.op("pe", e_mm(items), reads=[b("ylru"), b("yssd"), b("Wout")], writes=[bPA[h2]])
                            S.op("dve", e_tt(osb[0:Q, h2 * 512:(h2 + 1) * 512], PA[h2][0:Q, :], gate[0:Q, h2 * 512:(h2 + 1) * 512], ALU.mult),
                                 reads=[bPA[h2], b("gate")], writes=[b("osb")])
                        if DBG_STOP == "dump_o1" and ti == DBG_TI and j == 0:
                            dbg = dout("dbg", [P, D])
                            stores.append(S.dma("sp", [(dbg, osb[:])], reads=[b("osb")], owner=b("dbg0")))
                            raise _Stop()
                        S.op("pool", e_tt(osb[0:Q, :], osb[0:Q, :], xr[0:Q, :], ALU.add), reads=[b("osb"), bxr], writes=[b("osb")])
                        if DBG_STOP == "dump_o2" and ti == DBG_TI and j == 0:
                            dbg = dout("dbg", [P, D])
                            stores.append(S.dma("sp", [(dbg, osb[:])], reads=[b("osb")], owner=b("dbg0")))
                            raise _Stop()
                        S.op("act", e_act(junk[0:Q, :], osb[0:Q, :], AF.Square, accum=ssq[0:Q, 4:5]),
                             reads=[b("osb")], writes=[b("junk"), b("ssq")])
                        S.op("act", e_act(ssq[0:Q, 5:6], ssq[0:Q, 4:5], AF.Ln, bias=EPS, scale=1.0 / D), reads=[b("ssq")], writes=[b("ssq")])
                        S.op("act", e_act(ssq[0:Q, 6:7], ssq[0:Q, 5:6], AF.Exp, scale=-0.5), reads=[b("ssq")], writes=[b("ssq")])
                        S.op("dve", e_stt(xr[0:Q, :], osb[0:Q, :], ssq[0:Q, 6:7], fng[0:Q, :], ALU.mult, ALU.mult),
                             reads=[b("osb"), b("ssq"), b("fng")], writes=[bxr])
                        if DBG_STOP == "dump_o3" and ti == DBG_TI and j == 0:
                            dbg = dout("dbg", [P, D + 16])
                            stores.append(S.dma("sp", [(dbg[:, 0:D], xr[:])], reads=[bxr], owner=b("dbg0")))
                            stores.append(S.dma("sp", [(dbg[:, D:D + 16], ssq[:])], reads=[b("ssq")], owner=b("dbg1")))
                            raise _Stop()
                        stores.append(S.dma("sp", [(ydst[t0 + c0:t0 + c0 + Q, :], xr[0:Q, :])], reads=[bxr],
                                            owner=b("xout%d" % (j % 2))))


                def rec(fn, ti):
                    S.begin()
                    fn(ti)
                    return S.end()

                S.replay([rec(sec_stage1, 0)])
                for ti in range(NT):
                    chA = rec(sec_lru, ti)
                    chB = rec(sec_xbc, ti) + rec(sec_ssd, ti)
                    chC = rec(sec_stage1, ti + 1) if ti + 1 < NT else []
                    S.replay([chA, chB, chC])
                    mark("ssd")
                    S.replay([rec(sec_out, ti)])
                Tlast = (NT - 1) % 2
                hnT = hnTs[Tlast]

                mark("out")
                for blk in range(5):
                    col0 = OFF_LX + blk * 512 if blk < 2 else OFF_XBC + (blk - 2) * 512
                    S.op("pe", e_mm([(PY[0:3, :], hnT[:, k, Tt - 3:Tt], Win[:, k, col0:col0 + 512], k == 0, k == KC - 1)
                                     for k in range(KC)]),
                         reads=[b("hnT%d" % Tlast), b("Win")], writes=[b("PY")])
                    stgt, stgb = (yo, b("yo")) if blk % 2 == 0 else (xD, b("xD"))
                    S.op("act", e_acp(stgt[0:3, :], PY[0:3, :]), reads=[b("PY")], writes=[stgb])
                    dst = o_lc[grp][gi][:, blk * 512:(blk + 1) * 512] if blk < 2 else o_sc[grp][gi][:, (blk - 2) * 512:(blk - 1) * 512]
                    stores.append(S.dma("sp", [(dst, stgt[0:3, :])], reads=[stgb], owner=b("cso%d" % (blk % 2))))
                S.op("pe", e_tr([(PSG1[0:8, 0:P], hprev[:, 0:8], ident)]), reads=[b("hprev"), b("cst")], writes=[b("PSG1")])
                S.op("act", e_acp(gss[0:8, :].bitcast(F32) if False else hT[0:8, :], PSG1[0:8, 0:P]), reads=[b("PSG1")], writes=[b("hT")])
                stores.append(S.dma("sp", [(o_lh[grp][gi].rearrange("(c p) -> c p", p=P), hT[0:8, :])], reads=[b("hT")]))
                for c in range(8):
                    S.op("pe", e_tr([(PSG0[:, 0:P], Sst[:, c * P:(c + 1) * P], ident)]), reads=[b("Sst"), b("cst")], writes=[b("PSG0")])
                    S.op("act", e_acp(sto[:], PSG0[:, 0:P]), reads=[b("PSG0")], writes=[b("sto")])
                    stores.append(S.dma("sp", [(o_ss[grp][gi, c * P:(c + 1) * P, :], sto[:])], reads=[b("sto")]))
                mark("seqend")
        except _Stop:
            pass
        S.finish("sp", stores)
        S.run()
    return nc


def make_consts():
    c = np.zeros((P, 4 * P), np.float32)
    j = np.arange(P)[:, None]
    s = np.arange(P)[None, :]
    c[:, 0:P] = (j == s)
    c[:, P:2 * P] = (j > s)
    c[:, 2 * P:3 * P] = (j <= s)
    c[:, 3 * P:4 * P] = 1.0
    return c


def core_inputs(inp, pidx, sidx):
    f = lambda a: np.ascontiguousarray(np.asarray(a, dtype=np.float32))
    m = {
        "xp": f(inp["x_prompt"][pidx]),
        "xs": f(inp["x_sample"][sidx]),
        "cc": f(np.concatenate([inp["c_prompt"][pidx], inp["c_sample"][sidx]], axis=0)),
        "st_lc": f(inp["state_lru_conv"][0][sidx]),
        "st_lh": f(inp["state_lru_h"][0][sidx]),
        "st_sc": f(inp["state_ssd_conv"][0][sidx]),
        "st_ss": f(np.asarray(inp["state_ssd"][0][sidx]).reshape(len(sidx), D, P)),
        "consts": make_consts(),
    }
    for k in ("norm_g", "w_ada", "b_ada", "w_in", "lru_conv_w", "lru_conv_b", "lru_w_a", "lru_b_a", "lru_w_x",
              "lru_b_x", "lru_lambda", "ssd_conv_w", "ssd_conv_b", "ssd_dt_bias", "ssd_a_log", "ssd_d",
              "ssd_norm_g", "w_out"):
        m[k] = f(np.asarray(inp[k])[0])
    m["final_norm_g"] = f(inp["final_norm_g"])
    return m


_NC_CACHE = {}


def kernel(**inputs):
    B_, L_ = inputs["x_prompt"].shape[:2]
    DB, DL = inputs["x_sample"].shape[:2]
    n = NCORES
    NP, NS = B_ // n, DB // n
    key = (NP, L_, NS, DL)
    if key not in _NC_CACHE:
        _NC_CACHE[key] = build_program(NP, L_, NS, DL)
    nc = _NC_CACHE[key]
    in_maps = []
    for i in range(n):
        pidx = list(range(i * NP, (i + 1) * NP))
        sidx = list(range(i * NS, (i + 1) * NS))
        in_maps.append(core_inputs(inputs, pidx, sidx))
    res = run_bass_kernel_spmd(nc, in_maps, core_ids=list(range(n)))
    R = res.results
    cat = lambda name: np.concatenate([np.asarray(r[name], dtype=np.float32) for r in R], axis=0)
    y_p = cat("yp")
    y_s = cat("ys")
    outs = [y_p, y_s]
    for sfx, nb in (("p", B_), ("s", DB)):
        outs.append(cat("o_lc_" + sfx)[None])
        outs.append(cat("o_lh_" + sfx)[None])
        outs.append(cat("o_sc_" + sfx)[None])
        outs.append(cat("o_ss_" + sfx).reshape(1, nb, 2, 8, 64, 128))
    return tuple(outs)
```

```python
import math
from contextlib import ExitStack

import numpy as np
import concourse.bass as bass
import concourse.mybir as mybir
from concourse.bass_utils import run_bass_kernel_spmd

F32 = mybir.dt.float32
BF16 = mybir.dt.bfloat16
ALU = mybir.AluOpType
AF = mybir.ActivationFunctionType

P = 128
D = 1024
KC = 8
NCORES = 8
D_XBC = 1536
IN_COLS = 4624
EPS = 1e-6
OFF_LX, OFF_LG, OFF_XBC, OFF_Z, OFF_DT = 0, 1024, 2048, 3584, 4608

ENGS = ("pe", "act", "dve", "pool", "sp")


class Buf:
    __slots__ = ("name", "w", "r", "dsem", "dcnt", "excl")

    def __init__(self, name, excl=False):
        self.name = name
        self.excl = excl
        self.w = None
        self.r = {}
        self.dsem = None
        self.dcnt = 0


class Sched:
    def __init__(self, nc, stack):
        self.nc = nc
        self.stack = stack
        self.q = {e: [] for e in ENGS}
        self.esem = {}
        for e in ENGS:
            if e != "sp":
                self.esem[e] = stack.enter_context(nc.semaphore("es_" + e))
        self.ecnt = {e: 0 for e in ENGS}
        self.known = {e: {} for e in ENGS}
        self.nd = 0
        self.rec = None

    def begin(self):
        self.rec = []

    def end(self):
        r, self.rec = self.rec, None
        return r

    def replay(self, chains):
        chains = [c for c in chains if c]
        pos = [0] * len(chains)
        while True:
            best, bf = None, None
            for i, c in enumerate(chains):
                if pos[i] < len(c):
                    f = pos[i] / len(c)
                    if bf is None or f < bf:
                        best, bf = i, f
            if best is None:
                break
            kind, args, kw, ph = chains[best][pos[best]]
            pos[best] += 1
            tok = (self.op if kind == "op" else self.dma)(*args, **kw)
            ph[0] = tok

    @staticmethod
    def _tok(t):
        return t[0] if isinstance(t, list) else t

    def _waits(self, eng, toks):
        need = {}
        for t in toks:
            if t is None:
                continue
            sem, val = t
            k = id(sem)
            if self.known[eng].get(k, 0) >= val:
                continue
            if k not in need or need[k][1] < val:
                need[k] = (sem, val)
        for k, (sem, val) in need.items():
            self.known[eng][k] = val
        return list(need.values())

    @staticmethod
    def _deps(reads, writes):
        toks = []
        for b in reads:
            toks.append(b.w)
        for b in writes:
            toks.append(b.w)
            toks.extend(b.r.values())
        return toks

    @staticmethod
    def _commit(tok, reads, writes):
        k = id(tok[0])
        for b in reads:
            b.r[k] = tok
        for b in writes:
            b.w = tok
            b.r = {}

    def op(self, eng, emit, reads=(), writes=()):
        if self.rec is not None:
            ph = [None]
            self.rec.append(("op", (eng, emit, tuple(reads), tuple(writes)), {}, ph))
            return ph
        if any(x.excl for x in reads):
            writes = list(writes) + [x for x in reads if x.excl and x not in writes]
            reads = [x for x in reads if not x.excl]
        waits = self._waits(eng, self._deps(reads, writes))
        self.ecnt[eng] += 1
        tok = (self.esem[eng], self.ecnt[eng])
        self.q[eng].append((waits, emit, self.esem[eng]))
        self._commit(tok, reads, writes)
        return tok

    def dma(self, eng, parts, reads=(), writes=(), owner=None, **kw):
        if self.rec is not None:
            ph = [None]
            kw2 = dict(kw)
            kw2.update(reads=tuple(reads), writes=tuple(writes), owner=owner)
            self.rec.append(("dma", (eng, parts), kw2, ph))
            return ph
        if owner is None:
            owner = writes[0] if writes else reads[0]
        if owner.dsem is None:
            self.nd += 1
            owner.dsem = self.stack.enter_context(self.nc.semaphore("ds%d" % self.nd))
        waits = self._waits(eng, self._deps(reads, writes))
        sem = owner.dsem
        owner.dcnt += 16 * len(parts)
        tok = (sem, owner.dcnt)

        def emit(e, parts=parts, sem=sem, kw=kw):
            for (o, i) in parts:
                e.dma_start(out=o, in_=i, **kw).then_inc(sem, 16)
            return None
        self.q[eng].append((waits, emit, None))
        self._commit(tok, reads, writes)
        return tok

    def barrier(self, extra=()):
        toks = [(self.esem[x], self.ecnt[x]) for x in self.esem if self.ecnt[x] > 0] + [self._tok(t) for t in extra]
        for e in ENGS:
            w = self._waits(e, toks)
            if w:
                self.q[e].append((w, None, None))

    def finish(self, eng, toks):
        self.q[eng].append((self._waits(eng, [self._tok(t) for t in toks]), None, None))

    def run(self):
        nc, q = self.nc, self.q

        def play(e, items):
            for waits, emit, inc in items:
                for sem, val in waits:
                    e.wait_ge(sem, val)
                if emit is None:
                    continue
                ins = emit(e)
                if inc is not None:
                    ins.then_inc(inc, 1)

        with nc.Block() as block:
            @block.tensor
            def _(e):
                play(e, q["pe"])

            @block.scalar
            def _(e):
                play(e, q["act"])

            @block.vector
            def _(e):
                play(e, q["dve"])

            @block.gpsimd
            def _(e):
                play(e, q["pool"])

            @block.sync
            def _(e):
                play(e, q["sp"])


class _Stop(Exception):
    pass


DBG_STOP = None
DBG_REV = False
DBG_BARRIER = False
DBG_H = 0
DBG_TI = 0


_MARKS = {}


def mark(n):
    if DBG_STOP is None:
        return
    _MARKS[n] = _MARKS.get(n, 0) + 1
    if DBG_STOP == n or DBG_STOP == "%s:%d" % (n, _MARKS[n]):
        raise _Stop()


def build_program(NP, LP, NS, LS, T=256):
    NSEQ = NP + NS
    nc = bass.Bass("TRN2", target_bir_lowering=False)

    def din(name, shape):
        return nc.dram_tensor(name, list(shape), F32, kind="ExternalInput").ap()

    def dout(name, shape):
        return nc.dram_tensor(name, list(shape), F32, kind="ExternalOutput").ap()

    xp = din("xp", [NP, LP, D])
    xs_in = din("xs", [NS, LS, D])
    cc = din("cc", [NSEQ, D])
    st_lc = din("st_lc", [NS, 3, D])
    st_lh = din("st_lh", [NS, D])
    st_sc = din("st_sc", [NS, 3, D_XBC])
    st_ss = din("st_ss", [NS, D, P])
    norm_g = din("norm_g", [D])
    w_ada = din("w_ada", [D, 3 * D])
    b_ada = din("b_ada", [3 * D])
    w_in = din("w_in", [D, IN_COLS])
    lru_conv_w = din("lru_conv_w", [4, D])
    lru_conv_b = din("lru_conv_b", [D])
    lru_w_a = din("lru_w_a", [8, P, P])
    lru_b_a = din("lru_b_a", [D])
    lru_w_x = din("lru_w_x", [8, P, P])
    lru_b_x = din("lru_b_x", [D])
    lru_lambda = din("lru_lambda", [D])
    ssd_conv_w = din("ssd_conv_w", [4, D_XBC])
    ssd_conv_b = din("ssd_conv_b", [D_XBC])
    ssd_dt_bias = din("ssd_dt_bias", [16])
    ssd_a_log = din("ssd_a_log", [16])
    ssd_d = din("ssd_d", [16])
    ssd_norm_g = din("ssd_norm_g", [D])
    w_out = din("w_out", [2 * D, D])
    final_norm_g = din("final_norm_g", [D])
    consts = din("consts", [P, 4 * P])

    yp = dout("yp", [NP, LP, D])
    ys = dout("ys", [NS, LS, D])
    o_lc = [dout("o_lc_p", [NP, 3, D]), dout("o_lc_s", [NS, 3, D])]
    o_lh = [dout("o_lh_p", [NP, D]), dout("o_lh_s", [NS, D])]
    o_sc = [dout("o_sc_p", [NP, 3, D_XBC]), dout("o_sc_s", [NS, 3, D_XBC])]
    o_ss = [dout("o_ss_p", [NP, D, P]), dout("o_ss_s", [NS, D, P])]

    with ExitStack() as st:
        S = Sched(nc, st)
        stores = []

        def sb(name, shape, dt=F32):
            return st.enter_context(nc.sbuf_tensor(name, list(shape), dt))

        Win = sb("Win", [P, KC, IN_COLS], BF16)
        Wout = sb("Wout", [P, 16, D], BF16)
        Wa = sb("Wa", [P, 8, P], BF16)
        Wx = sb("Wx", [P, 8, P], BF16)
        cst = sb("cst", [P, 4 * P])
        identb = sb("identb", [P, P], BF16)
        NCOL = 8 + 32 + 8 + 8 + 8 + 8 + 48 + 12 + 8 + 16
        colp = sb("colp", [P, NCOL])
        dcol = sb("dcol", [P, 64])
        rows = sb("rows", [P, 64])
        fng = sb("fng", [P, D])
        gate = sb("gate", [P, D])
        modg = sb("modg", [8, D])
        cT = sb("cT", [P, KC, NSEQ])
        cTb = sb("cTb", [P, KC, NSEQ], BF16)
        modc = sb("modc", [P, 16, NSEQ])
        gsc = sb("gsc", [P, KC, NSEQ])

        xin = [sb("xin0", [P, D]), sb("xin1", [P, D])]
        junk = sb("junk", [P, D], BF16)
        ssq = sb("ssq", [P, 16])
        xn = sb("xn", [P, D], BF16)
        hnTs = [sb("hnT0", [P, KC, T], BF16), sb("hnT1", [P, KC, T], BF16)]
        lx = sb("lx", [P, T + 3])
        lx2 = sb("lx2", [P, T + 3])
        u2 = sb("u2", [P, T])
        ga2 = sb("ga2", [P, T])
        u = sb("u", [P, T])
        ub = sb("ub", [P, T], BF16)
        ga = sb("ga", [P, T])
        aa = sb("aa", [P, T])
        gm = sb("gm", [P, T])
        gx = sb("gx", [P, T])
        gb = sb("gb", [P, T])
        hh = sb("hh", [P, T])
        gg = sb("gg", [P, T])
        ylru = sb("ylru", [P, 8, T], BF16)
        xsb = sb("xsb", [P, 12, T], BF16)
        yssd = sb("yssd", [P, 8, T], BF16)
        lhalo = sb("lhalo", [P, 8, 3])
        shalo = sb("shalo", [P, 12, 3])
        hprev = sb("hprev", [P, 8])
        Sst = sb("Sst", [P, D])
        Sb = sb("Sb", [P, D], BF16)
        dtv = sb("dtv", [P, 16])
        dtt = sb("dtt", [P, 16])
        dA = sb("dA", [P, 16])
        sml = sb("sml", [P, 48])
        Ubig = sb("Ubig", [P, 8, P])
        Lm = sb("Lm", [P, 8, P])
        MT = sb("MT", [P, 8, P], BF16)
        CBm = sb("CBm", [P, P])
        xdt = sb("xdt", [P, D], BF16)
        xdd = sb("xdd", [P, D], BF16)
        xD = sb("xD", [P, 512])
        Btm = sb("Btm", [P, 256], BF16)
        yo = sb("yo", [P, 512])
        tz = sb("tz", [P, 512])
        yz = sb("yz", [P, 512])
        yn = sb("yn", [P, 512], BF16)
        gss = sb("gss", [P, 8])
        osb = sb("osb", [P, D])
        sto = sb("sto", [P, P])
        hT = sb("hT", [8, P])
        stg = xin[1]
        badag = gate
        rsel = osb
        wadab = osb.bitcast(BF16)

        ps = [st.enter_context(nc.psum_tensor("ps%d" % i, [P, 512], F32)) for i in range(8)]
        PA = [ps[0], ps[1]]
        PG, PT, PSG0, PSG1, PSM, PY = ps[2], ps[3], ps[4], ps[5], ps[6], ps[7]
        PTb = PT.bitcast(BF16)
        PT1b = ps[2].bitcast(BF16)

        B = {}

        PSUM_NAMES = ("PA0", "PA1", "PG", "PT", "PSG0", "PSG1", "PSM", "PY")

        def b(name):
            if name.startswith("PSM"):
                name = "PSM"
            if name not in B:
                B[name] = Buf(name, excl=name in PSUM_NAMES)
            return B[name]

        bPA = [b("PA0"), b("PA1")]

        ident = cst[:, 0:P]
        Umat = cst[:, P:2 * P]
        Vmat = cst[:, 2 * P:3 * P]
        ones = cst[:, 3 * P:4 * P]

        C_NG, C_LCW, C_LCB, C_LBA, C_LBX, C_LAM, C_SCW, C_SCB, C_GSSD, C_BADA = (
            0, 8, 40, 48, 56, 64, 72, 120, 132, 140)
        DC_NBA, DC_NBX, DC_M8, DC_M16 = 0, 8, 16, 24
        R_DTB, R_A, R_D = 0, 16, 32

        def e_ts(out, in0, s1, s2, op0, op1=None):
            if op1 is None:
                return lambda e: e.tensor_scalar(out, in0, s1, None, op0)
            return lambda e: e.tensor_scalar(out, in0, s1, s2, op0, op1)

        def e_tt(out, a, bb, op):
            return lambda e: e.tensor_tensor(out, a, bb, op)

        def e_stt(out, in0, sc, in1, op0, op1):
            return lambda e: e.scalar_tensor_tensor(out, in0, sc, in1, op0, op1)

        def e_cp(out, in_):
            return lambda e: e.tensor_copy(out, in_)

        def e_acp(out, in_):
            return lambda e: e.copy(out, in_)

        def e_act(out, in_, func, bias=0.0, scale=1.0, accum=None):
            if accum is None:
                return lambda e: e.activation(out, in_, func, bias=bias, scale=scale)
            return lambda e: e.activation(out, in_, func, bias=bias, scale=scale, accum_out=accum)

        def e_mm(items):
            def emit(e):
                ins = None
                for (o, l, r, s0, s1) in items:
                    ins = e.matmul(o, l, r, start=s0, stop=s1)
                return ins
            return emit

        def e_tr(items):
            def emit(e):
                ins = None
                for (o, i, idn) in items:
                    ins = e.transpose(o, i, idn)
                return ins
            return emit

        def e_memset(ap, v):
            return lambda e: e.memset(ap, v)

        try:
            S.dma("sp", [(cst[:], consts)], writes=[b("cst")])
            S.op("dve", e_cp(identb[:], ident), reads=[b("cst")], writes=[b("identb")])

            def colload(off, n, src):
                S.dma("sp", [(colp[:, off:off + n], src.rearrange("(c p) -> p c", p=P))],
                      writes=[b("colp")], owner=b("colp_d%d" % off), allow_slow_non_contiguous=True)

            colload(C_NG, 8, norm_g)
            colload(C_LCB, 8, lru_conv_b)
            colload(C_LBA, 8, lru_b_a)
            colload(C_LBX, 8, lru_b_x)
            colload(C_LAM, 8, lru_lambda)
            colload(C_SCB, 12, ssd_conv_b)
            colload(C_GSSD, 8, ssd_norm_g)
            colload(C_BADA, 16, b_ada[0:2 * D])
            for k in range(4):
                colload(C_LCW + 8 * k, 8, lru_conv_w[k])
                colload(C_SCW + 12 * k, 12, ssd_conv_w[k])
            S.dma("sp", [(rows[:, R_DTB:R_DTB + 16], ssd_dt_bias.partition_broadcast(P)),
                         (rows[:, R_A:R_A + 16], ssd_a_log.partition_broadcast(P)),
                         (rows[:, R_D:R_D + 16], ssd_d.partition_broadcast(P))], writes=[b("rows")])
            S.dma("sp", [(fng[:], final_norm_g.partition_broadcast(P))], writes=[b("fng")])
            S.dma("sp", [(badag[0:NSEQ, :], b_ada[2 * D:3 * D].partition_broadcast(NSEQ))], writes=[b("gate")])
            S.dma("sp", [(cT[:, :, r], cc[r].rearrange("(k p) -> p k", p=P)) for r in range(NSEQ)], writes=[b("cT")],
                  allow_slow_non_contiguous=True)

            S.op("dve", e_ts(dcol[:, DC_NBA:DC_NBA + 16], colp[:, C_LBA:C_LBA + 16], -1.0, None, ALU.mult),
                 reads=[b("colp")], writes=[b("dcol")])
            S.op("act", e_act(dcol[:, 32:40], colp[:, C_LAM:C_LAM + 8], AF.Exp, scale=-1.0),
                 reads=[b("colp")], writes=[b("dcol")])
            S.op("act", e_act(dcol[:, 32:40], dcol[:, 32:40], AF.Ln, bias=1.0), reads=[b("dcol")], writes=[b("dcol")])
            S.op("dve", e_ts(dcol[:, DC_M8:DC_M8 + 8], dcol[:, 32:40], -8.0, None, ALU.mult),
                 reads=[b("dcol")], writes=[b("dcol")])
            S.op("dve", e_ts(dcol[:, DC_M16:DC_M16 + 8], dcol[:, 32:40], -16.0, None, ALU.mult),
                 reads=[b("dcol")], writes=[b("dcol")])
            S.op("act", e_act(rows[:, R_A:R_A + 16], rows[:, R_A:R_A + 16], AF.Exp), reads=[b("rows")], writes=[b("rows")])
            S.op("dve", e_ts(rows[:, R_A:R_A + 16], rows[:, R_A:R_A + 16], -1.0, None, ALU.mult),
                 reads=[b("rows")], writes=[b("rows")])

            w_in_v = w_in.rearrange("(k p) n -> p k n", p=P)
            for k in range(KC):
                S.dma("pool", [(Win[:, k, :], w_in_v[:, k, :])], writes=[b("Win")], owner=b("Win_d%d" % k))
            S.dma("pool", [(Wa[:], lru_w_a.rearrange("h i j -> i h j")),
                           (Wx[:], lru_w_x.rearrange("h i j -> i h j"))], writes=[b("Wg")])
            w_out_v = w_out.rearrange("(k p) n -> p k n", p=P)
            for k in range(8):
                S.dma("pool", [(Wout[:, k, :], w_out_v[:, k, :])], writes=[b("Wout")], owner=b("Wout_d%d" % k))
            for k in range(8, 16):
                S.dma("sp", [(stg[:], w_out_v[:, k, :])], writes=[b("xin1")])
                S.op("dve", e_ts(Wout[:, k, :], stg[:], colp[:, C_GSSD + k - 8:C_GSSD + k - 7], None, ALU.mult),
                     reads=[b("xin1"), b("colp")], writes=[b("Wout")])

            cTf = cT[:].rearrange("p k r -> p (k r)")
            S.op("act", e_act(modc[:].rearrange("p a r -> p (a r)")[:, 0:KC * NSEQ], cTf, AF.Exp, scale=-1.0),
                 reads=[b("cT")], writes=[b("modc")])
            mtmp = modc[:].rearrange("p a r -> p (a r)")[:, 0:KC * NSEQ]
            S.op("act", e_act(mtmp, mtmp, AF.Ln, bias=1.0), reads=[b("modc")], writes=[b("modc")])
            S.op("act", e_act(mtmp, mtmp, AF.Exp, scale=-1.0), reads=[b("modc")], writes=[b("modc")])
            S.op("dve", e_tt(cTb[:].rearrange("p k r -> p (k r)"), cTf, mtmp, ALU.mult),
                 reads=[b("cT"), b("modc")], writes=[b("cTb")])
            w_ada_v = w_ada.rearrange("(k p) n -> p k n", p=P)
            wadab3 = wadab[:, 0:KC * 256].rearrange("p (k n) -> p k n", k=KC)
            PSMc = PSM[:, 0:16 * NSEQ].rearrange("p (a r) -> p a r", a=16)
            for blk in range(12):
                S.dma("pool", [(wadab3, w_ada_v[:, :, blk * 256:(blk + 1) * 256])], writes=[b("osb")])
                if blk < 8:
                    for half in range(2):
                        cb = blk * 2 + half
                        S.op("pe", e_mm([(PSMc[:, cb, :], wadab3[:, k, half * P:(half + 1) * P], cTb[:, k, :],
                                          k == 0, k == KC - 1) for k in range(KC)]),
                             reads=[b("osb"), b("cTb")], writes=[b("PSM")])
                else:
                    gb_ = blk - 8
                    bank = PA[gb_ // 2]
                    S.op("pe", e_mm([(bank[0:NSEQ, (gb_ % 2) * 256:(gb_ % 2) * 256 + 256], cTb[:, k, :],
                                      wadab3[:, k, :], k == 0, k == KC - 1) for k in range(KC)]),
                         reads=[b("osb"), b("cTb")], writes=[bPA[gb_ // 2]])
            S.op("dve", e_tt(modc[:], PSMc, colp[:, C_BADA:C_BADA + 16].unsqueeze(2).to_broadcast([P, 16, NSEQ]), ALU.add),
                 reads=[b("PSM"), b("colp")], writes=[b("modc")])
            S.op("dve", e_ts(gsc[:], modc[:, 8:16, :], 1.0, None, ALU.add), reads=[b("modc")], writes=[b("gsc")])
            S.op("dve", e_tt(gsc[:], gsc[:], colp[:, C_NG:C_NG + 8].unsqueeze(2).to_broadcast([P, KC, NSEQ]), ALU.mult),
                 reads=[b("gsc"), b("colp")], writes=[b("gsc")])
            for h2 in range(2):
                S.op("dve", e_tt(modg[0:NSEQ, h2 * 512:(h2 + 1) * 512], PA[h2][0:NSEQ, :],
                                 badag[0:NSEQ, h2 * 512:(h2 + 1) * 512], ALU.add),
                     reads=[bPA[h2], b("gate")], writes=[b("modg")])

            mark("setup")
            def seq_views(si):
                if si < NP:
                    return xp[si], yp[si], 0, si, LP
                return xs_in[si - NP], ys[si - NP], 1, si - NP, LS

            tile_ctr = [0]

            seq_order = list(range(NP, NSEQ)) + list(range(NP))
            for si in (seq_order if not DBG_REV else seq_order[::-1]):
                xsrc, ydst, grp, gi, L = seq_views(si)
                if DBG_BARRIER:
                    S.barrier(stores)
                is_prompt = grp == 0
                Tt = min(T, L)
                Q = min(P, Tt)
                NSUB = Tt // Q
                NT = L // Tt
                assert L % Tt == 0 and Tt % Q == 0

                if is_prompt:
                    S.op("pool", e_memset(hprev[:], 0.0), writes=[b("hprev")])
                    S.op("pool", e_memset(lhalo[:], 0.0), writes=[b("lhalo")])
                    S.op("pool", e_memset(shalo[:], 0.0), writes=[b("shalo")])
                    S.op("pool", e_memset(Sst[:], 0.0), writes=[b("Sst")])
                    S.op("pool", e_memset(Sb[:], 0.0), writes=[b("Sb")])
                else:
                    S.dma("sp", [(xin[0][0:3, :], st_lc[gi]), (xin[1][0:3, :], st_sc[gi][:, 0:1024]),
                                 (osb[0:3, 0:512], st_sc[gi][:, 1024:1536])],
                          writes=[b("xin0"), b("xin1"), b("osb")], owner=b("stld"))
                    S.dma("sp", [(osb[32:33, :], st_lh[gi].rearrange("(o d) -> o d", o=1))], writes=[b("osb")], owner=b("stld2"))
                    PSs = PSG1[:, 0:64]
                    S.op("pe", e_tr([(PSs[:, c * 3:(c + 1) * 3], xin[0][0:3, c * P:(c + 1) * P], ident[0:3, 0:3]) for c in range(8)]
                                    + [(PSs[:, 24 + c * 3:24 + (c + 1) * 3], (xin[1][0:3, c * P:(c + 1) * P] if c < 8 else
                                                                          osb[0:3, (c - 8) * P:(c - 7) * P]), ident[0:3, 0:3]) for c in range(12)]),
                         reads=[b("xin0"), b("xin1"), b("osb"), b("cst")], writes=[b("PSG1")])
                    S.op("act", e_acp(lhalo[:].rearrange("p c j -> p (c j)"), PSs[:, 0:24]), reads=[b("PSG1")], writes=[b("lhalo")])
                    S.op("act", e_acp(shalo[:].rearrange("p c j -> p (c j)"), PSs[:, 24:60]), reads=[b("PSG1")], writes=[b("shalo")])
                    S.op("pe", e_tr([(PSG1[:, 64 + c:65 + c], osb[32:33, c * P:(c + 1) * P], ident[32:33, 32:33]) for c in range(8)]),
                         reads=[b("osb"), b("cst")], writes=[b("PSG1")])
                    S.op("act", e_acp(hprev[:], PSG1[:, 64:72]), reads=[b("PSG1")], writes=[b("hprev")])
                    for c4 in range(2):
                        S.dma("sp", [(osb[:, c4 * 512:(c4 + 1) * 512].rearrange("p (c n) -> p c n", c=4),
                                      st_ss[gi, c4 * 512:(c4 + 1) * 512, :].rearrange("(c p) n -> p c n", p=P))],
                              writes=[b("osb")])
                        S.op("pe", e_tr([(PSG0[:, c * P:(c + 1) * P], osb[:, c4 * 512 + c * P:c4 * 512 + (c + 1) * P], ident)
                                         for c in range(4)]),
                             reads=[b("osb"), b("cst")], writes=[b("PSG0")])
                        S.op("act", e_acp(Sst[:, c4 * 512:(c4 + 1) * 512], PSG0[:, :]), reads=[b("PSG0")], writes=[b("Sst")])
                    S.op("pool", e_cp(Sb[:], Sst[:]), reads=[b("Sst")], writes=[b("Sb")])
                S.op("dve", e_ts(rsel[0:NSEQ, :], modg[0:NSEQ, :], ident[0:NSEQ, si:si + 1], None, ALU.mult),
                     reads=[b("modg"), b("cst")], writes=[b("osb")])
                for h2 in range(2):
                    bank, bk = (PSG0, b("PSG0")) if h2 == 0 else (PSG1, b("PSG1"))
                    S.op("pe", e_mm([(bank[:, :], ones[0:NSEQ, :], rsel[0:NSEQ, h2 * 512:(h2 + 1) * 512], True, True)]),
                         reads=[b("osb"), b("cst")], writes=[bk])
                    S.op("act", e_acp(gate[:, h2 * 512:(h2 + 1) * 512], bank[:, :]), reads=[bk], writes=[b("gate")])

                if DBG_STOP == "dump_gate":
                    dbg = dout("dbg", [P, D])
                    stores.append(S.dma("sp", [(dbg, gate[:])], reads=[b("gate")], owner=b("dbg0")))
                    raise _Stop()
                mark("seqinit")
                def proj_fm(col0, bank, bbank, hn_cur, bhn):
                    S.op("pe", e_mm([(bank[:, 0:Tt], Win[:, k, col0:col0 + P], hn_cur[:, k, 0:Tt],
                                      k == 0, k == KC - 1) for k in range(KC)]),
                         reads=[b("Win"), bhn], writes=[bbank])

                def sigmoid_act(dst, src, src_reads, nbias):
                    S.op("act", e_act(dst, src, AF.Exp, bias=nbias, scale=-1.0), reads=src_reads, writes=[b(dst.tensor.name)])
                    S.op("act", e_act(dst, dst, AF.Ln, bias=1.0), reads=[b(dst.tensor.name)], writes=[b(dst.tensor.name)])
                    S.op("act", e_act(dst, dst, AF.Exp, scale=-1.0), reads=[b(dst.tensor.name)], writes=[b(dst.tensor.name)])

                def conv4(dst, src, wcol, bcol, nchunks, c, nsrc="lx", ndst="u"):
                    S.op("dve", e_ts(dst[:, 0:Tt], src[:, 0:Tt], colp[:, wcol + c:wcol + c + 1],
                                     colp[:, bcol + c:bcol + c + 1], ALU.mult, ALU.add),
                         reads=[b(nsrc), b("colp")], writes=[b(ndst)])
                    for k in range(1, 4):
                        S.op("dve", e_stt(dst[:, 0:Tt], src[:, k:k + Tt],
                                          colp[:, wcol + k * nchunks + c:wcol + k * nchunks + c + 1],
                                          dst[:, 0:Tt], ALU.mult, ALU.add),
                             reads=[b(nsrc), b(ndst), b("colp")], writes=[b(ndst)])

                def sec_stage1(ti):
                    t0 = ti * Tt
                    hn_cur = hnTs[ti % 2]
                    bhn = b("hnT%d" % (ti % 2))
                    for j in range(NSUB):
                        xb_ = xin[j % 2]
                        bxin = b("xin%d" % (j % 2))
                        S.dma("sp", [(xb_[0:Q, :], xsrc[t0 + j * Q:t0 + (j + 1) * Q, :])], writes=[bxin])
                        S.op("act", e_act(junk[0:Q, :], xb_[0:Q, :], AF.Square, accum=ssq[0:Q, 0:1]),
                             reads=[bxin], writes=[b("junk"), b("ssq")])
                        S.op("act", e_act(ssq[0:Q, 1:2], ssq[0:Q, 0:1], AF.Ln, bias=EPS, scale=1.0 / D),
                             reads=[b("ssq")], writes=[b("ssq")])
                        S.op("act", e_act(ssq[0:Q, 2:3], ssq[0:Q, 1:2], AF.Exp, scale=-0.5),
                             reads=[b("ssq")], writes=[b("ssq")])
                        S.op("act", e_act(xn[0:Q, :], xb_[0:Q, :], AF.Copy, scale=ssq[0:Q, 2:3]),
                             reads=[bxin, b("ssq")], writes=[b("xn")])
                        PT3 = PT1b[:, 0:KC * P].rearrange("p (k q) -> p k q", k=KC)
                        S.op("pe", e_tr([(PT3[:, k, 0:Q], xn[0:Q, k * P:(k + 1) * P], identb[0:Q, 0:Q]) for k in range(KC)]),
                             reads=[b("xn"), b("identb")], writes=[b("PG")])
                        for k in range(KC):
                            if k % 2 == 0:
                                S.op("act", e_act(hn_cur[:, k, j * Q:(j + 1) * Q], PT3[:, k, 0:Q], AF.Identity,
                                                  bias=modc[:, k, si:si + 1], scale=gsc[:, k, si:si + 1]),
                                     reads=[b("PG"), b("modc"), b("gsc")], writes=[bhn])
                            else:
                                S.op("dve", e_ts(hn_cur[:, k, j * Q:(j + 1) * Q], PT3[:, k, 0:Q], gsc[:, k, si:si + 1],
                                                 modc[:, k, si:si + 1], ALU.mult, ALU.add),
                                     reads=[b("PG"), b("modc"), b("gsc")], writes=[bhn])

                def sec_lru(ti):
                    t0 = ti * Tt
                    hn_cur = hnTs[ti % 2]
                    bhn = b("hnT%d" % (ti % 2))
                    pa_i = 0
                    for h in range(8):
                        S.op("pool", e_cp(lx[:, 0:3], lhalo[:, h, :]), reads=[b("lhalo")], writes=[b("lx")])
                        proj_fm(OFF_LX + h * P, ps[0], b("PA0"), hn_cur, bhn)
                        S.op("act", e_acp(lx[:, 3:3 + Tt], ps[0][:, 0:Tt]), reads=[b("PA0")], writes=[b("lx")])
                        pa_i ^= 1
                        S.op("pool", e_cp(lhalo[:, h, :], lx[:, Tt:Tt + 3]), reads=[b("lx")], writes=[b("lhalo")])
                        conv4(u, lx, C_LCW, C_LCB, 8, h)
                        S.op("act", e_acp(ub[:, 0:Tt], u[:, 0:Tt]), reads=[b("u")], writes=[b("ub")])
                        PG2 = ps[1][:, 0:2 * Tt].rearrange("p (a t) -> p a t", a=2)
                        S.op("pe", e_mm([(PG2[:, 0, :], Wa[:, h, :], ub[:, 0:Tt], True, True),
                                         (PG2[:, 1, :], Wx[:, h, :], ub[:, 0:Tt], True, True)]),
                             reads=[b("Wg"), b("ub")], writes=[b("PA1")])
                        sigmoid_act(ga[:, 0:Tt], PG2[:, 0, :], [b("PA1"), b("dcol")], dcol[:, DC_NBA + h:DC_NBA + h + 1])
                        S.op("act", e_act(aa[:, 0:Tt], ga[:, 0:Tt], AF.Exp, scale=dcol[:, DC_M8 + h:DC_M8 + h + 1]),
                             reads=[b("ga"), b("dcol")], writes=[b("aa")])
                        S.op("act", e_act(gm[:, 0:Tt], ga[:, 0:Tt], AF.Exp, scale=dcol[:, DC_M16 + h:DC_M16 + h + 1]),
                             reads=[b("ga"), b("dcol")], writes=[b("gm")])
                        S.op("act", e_act(gm[:, 0:Tt], gm[:, 0:Tt], AF.Ln, bias=1.0, scale=-1.0), reads=[b("gm")], writes=[b("gm")])
                        S.op("act", e_act(gm[:, 0:Tt], gm[:, 0:Tt], AF.Exp, scale=0.5), reads=[b("gm")], writes=[b("gm")])
                        sigmoid_act(gx[:, 0:Tt], PG2[:, 1, :], [b("PA1"), b("dcol")], dcol[:, DC_NBX + h:DC_NBX + h + 1])
                        S.op("dve", e_tt(gb[:, 0:Tt], gx[:, 0:Tt], u[:, 0:Tt], ALU.mult), reads=[b("gx"), b("u")], writes=[b("gb")])
                        if is_prompt and ti == 0:
                            S.op("pool", e_memset(gm[:, 0:1], 1.0), reads=[], writes=[b("gm")])
                        S.op("dve", e_tt(gb[:, 0:Tt], gb[:, 0:Tt], gm[:, 0:Tt], ALU.mult), reads=[b("gb"), b("gm")], writes=[b("gb")])
                        S.op("dve", (lambda h=h: (lambda e: e.tensor_tensor_scan(hh[:, 0:Tt], aa[:, 0:Tt], gb[:, 0:Tt],
                                                                                   hprev[:, h:h + 1], ALU.mult, ALU.add)))(),
                             reads=[b("aa"), b("gb"), b("hprev")], writes=[b("hh")])
                        S.op("pool", e_cp(hprev[:, h:h + 1], hh[:, Tt - 1:Tt]), reads=[b("hh")], writes=[b("hprev")])
                        if DBG_STOP == "dump_lru" and h == DBG_H and ti == DBG_TI:
                            dbg = dout("dbg", [P, 8 * 256])
                            for i_, (t_, nm_) in enumerate([(lx, "lx"), (u, "u"), (ga, "ga"), (aa, "aa"), (gm, "gm"), (gx, "gx"), (gb, "gb"), (hh, "hh")]):
                                stores.append(S.dma("sp", [(dbg[:, i_ * 256:(i_ + 1) * 256], t_[:, 0:256])], reads=[b(nm_)], owner=b("dbg%d" % i_)))
                            raise _Stop()
                        proj_fm(OFF_LG + h * P, ps[0], b("PA0"), hn_cur, bhn)
                        sigmoid_act(gg[:, 0:Tt], ps[0][:, 0:Tt], [b("PA0")], 0.0)
                        S.op("dve", e_tt(hh[:, 0:Tt], hh[:, 0:Tt], ps[0][:, 0:Tt], ALU.mult),
                             reads=[b("hh"), b("PA0")], writes=[b("hh")])
                        S.op("dve", e_tt(ylru[:, h, 0:Tt], hh[:, 0:Tt], gg[:, 0:Tt], ALU.mult),
                             reads=[b("hh"), b("gg")], writes=[b("ylru")])
                        pa_i ^= 1

                def sec_xbc(ti):
                    t0 = ti * Tt
                    hn_cur = hnTs[ti % 2]
                    bhn = b("hnT%d" % (ti % 2))
                    pa_i = 0
                    for c in range(12):
                        S.op("pool", e_cp(lx2[:, 0:3], shalo[:, c, :]), reads=[b("shalo")], writes=[b("lx2")])
                        proj_fm(OFF_XBC + c * P, ps[4 + pa_i], b("PSG%d" % pa_i), hn_cur, bhn)
                        S.op("act", e_acp(lx2[:, 3:3 + Tt], ps[4 + pa_i][:, 0:Tt]), reads=[b("PSG%d" % pa_i)], writes=[b("lx2")])
                        pa_i ^= 1
                        S.op("pool", e_cp(shalo[:, c, :], lx2[:, Tt:Tt + 3]), reads=[b("lx2")], writes=[b("shalo")])
                        conv4(u2, lx2, C_SCW, C_SCB, 12, c, "lx2", "u2")
                        sigmoid_act(ga2[:, 0:Tt], u2[:, 0:Tt], [b("u2")], 0.0)
                        S.op("dve", e_tt(xsb[:, c, 0:Tt], u2[:, 0:Tt], ga2[:, 0:Tt], ALU.mult),
                             reads=[b("u2"), b("ga2")], writes=[b("xsb")])

                def sec_ssd(ti):
                    t0 = ti * Tt
                    hn_cur = hnTs[ti % 2]
                    bhn = b("hnT%d" % (ti % 2))
                    for j in range(NSUB):
                        c0 = j * Q
                        PSM_dt = PSM[0:Q, 0:16]
                        S.op("pe", e_mm([(PSM_dt, hn_cur[:, k, c0:c0 + Q], Win[:, k, OFF_DT:OFF_DT + 16], k == 0, k == KC - 1)
                                         for k in range(KC)]),
                             reads=[bhn, b("Win")], writes=[b("PSM_dt")])
                        S.op("dve", e_tt(dtv[0:Q, :], PSM_dt, rows[0:Q, R_DTB:R_DTB + 16], ALU.add),
                             reads=[b("PSM_dt"), b("rows")], writes=[b("dtv")])
                        S.op("dve", e_ts(dtv[0:Q, :], dtv[0:Q, :], 30.0, None, ALU.min), reads=[b("dtv")], writes=[b("dtv")])
                        S.op("act", e_act(dtv[0:Q, :], dtv[0:Q, :], AF.Exp), reads=[b("dtv")], writes=[b("dtv")])
                        S.op("act", e_act(dtt[0:Q, :], dtv[0:Q, :], AF.Ln, bias=1.0), reads=[b("dtv")], writes=[b("dtt")])
                        S.op("dve", e_tt(dA[0:Q, :], dtt[0:Q, :], rows[0:Q, R_A:R_A + 16], ALU.mult),
                             reads=[b("dtt"), b("rows")], writes=[b("dA")])
                        mark("ssd_dt")
                        S.op("pe", e_mm([(PSM[0:Q, 64:80], Vmat[0:Q, 0:Q], dA[0:Q, :], True, True),
                                         (PSM[0:Q, 80:96], Umat[0:Q, 0:Q], dA[0:Q, :], True, True),
                                         (PSM[:, 96:112], ones[0:Q, :], dA[0:Q, :], True, True)]),
                             reads=[b("dA"), b("cst")], writes=[b("PSM_sm")])
                        S.op("act", e_act(sml[0:Q, 0:32], PSM[0:Q, 64:96], AF.Exp), reads=[b("PSM_sm")], writes=[b("sml")])
                        S.op("act", e_act(sml[:, 32:48], PSM[:, 96:112], AF.Exp), reads=[b("PSM_sm")], writes=[b("sml")])
                        mark("ssd_dec")
                        S.op("pe", e_tr([(PTb[0:Q, c * P:(c + 1) * P], xsb[:, c, c0:c0 + Q], identb[:, :]) for c in range(8)]),
                             reads=[b("xsb"), b("identb")], writes=[b("PT")])
                        PSMb = PSM.bitcast(BF16)
                        S.op("pe", e_tr([(PSMb[0:Q, 512 + g * P:512 + (g + 1) * P], xsb[:, 8 + g, c0:c0 + Q], identb[:, :])
                                         for g in range(2)]),
                             reads=[b("xsb"), b("identb")], writes=[b("PSM_bt")])
                        S.op("act", e_acp(Btm[0:Q, :], PSMb[0:Q, 512:768]), reads=[b("PSM_bt")], writes=[b("Btm")])
                        PT4 = PTb[0:Q, :].rearrange("p (k d) -> p k d", k=16)
                        dt_b = dtt[0:Q, :].unsqueeze(2).to_broadcast([Q, 16, 64])
                        S.op("dve", e_tt(xdt[0:Q, :].rearrange("p (k d) -> p k d", k=16), PT4, dt_b, ALU.mult),
                             reads=[b("PT"), b("dtt")], writes=[b("xdt")])
                        dec_b = sml[0:Q, 16:32].unsqueeze(2).to_broadcast([Q, 16, 64])
                        S.op("dve", e_tt(xdd[0:Q, :].rearrange("p (k d) -> p k d", k=16),
                                          xdt[0:Q, :].rearrange("p (k d) -> p k d", k=16), dec_b, ALU.mult),
                             reads=[b("xdt"), b("sml")], writes=[b("xdd")])

                        mark("ssd_tr")
                        for g in range(2):
                            S.op("pe", e_mm([(PSM[0:Q, 128:128 + Q], xsb[:, 8 + g, c0:c0 + Q], xsb[:, 10 + g, c0:c0 + Q], True, True)]),
                                 reads=[b("xsb")], writes=[b("PSM_cb")])
                            mark("ssd_cbmm")
                            if DBG_STOP == "exp1" and g == 1:
                                S.op("dve", e_memset(CBm[0:Q, 0:Q], 0.0), reads=[b("PSM_cb")], writes=[b("CBm")])
                                raise _Stop()
                            if DBG_STOP == "exp2" and g == 1:
                                S.op("dve", e_tt(CBm[0:Q, 0:Q], PSM[0:Q, 128:128 + Q], Vmat[0:Q, 0:Q], ALU.mult), reads=[b("cst")], writes=[b("CBm")])
                                raise _Stop()
                            S.op("dve", e_tt(CBm[0:Q, 0:Q], PSM[0:Q, 128:128 + Q], Vmat[0:Q, 0:Q], ALU.mult),
                                 reads=[b("PSM_cb"), b("cst")], writes=[b("CBm")])
                            mark("ssd_cb")
                            dA_b = dA[0:Q, g * 8:(g + 1) * 8].unsqueeze(2).to_broadcast([Q, 8, Q])
                            U_b = Umat[0:Q, 0:Q].unsqueeze(1).to_broadcast([Q, 8, Q])
                            S.op("dve", e_tt(Ubig[0:Q, :, 0:Q], U_b, dA_b, ALU.mult),
                                 reads=[b("dA"), b("cst")], writes=[b("Ubig")])
                            for hf in range(2):
                                bank, bk = (PSG0, b("PSG0")) if hf == 0 else (PSG1, b("PSG1"))
                                bank3 = bank[0:Q, :].rearrange("p (k l) -> p k l", k=4)
                                S.op("pe", e_mm([(bank3[:, k, 0:Q], Ubig[0:Q, hf * 4 + k, 0:Q], Vmat[0:Q, 0:Q], True, True)
                                                 for k in range(4)]),
                                     reads=[b("Ubig"), b("cst")], writes=[bk])
                                S.op("act", e_act(Lm[0:Q, hf * 4:(hf + 1) * 4, 0:Q], bank3[:, :, 0:Q], AF.Exp),
                                     reads=[bk], writes=[b("Lm")])
                            CB_b = CBm[0:Q, 0:Q].unsqueeze(1).to_broadcast([Q, 8, Q])
                            S.op("dve", e_tt(MT[0:Q, :, 0:Q], Lm[0:Q, :, 0:Q], CB_b, ALU.mult),
                                 reads=[b("Lm"), b("CBm")], writes=[b("MT")])
                            mark("ssd_L")
                            S.op("pe", e_mm([(PY[0:Q, k * 64:(k + 1) * 64], MT[0:Q, k, 0:Q],
                                              xdt[0:Q, g * 512 + k * 64:g * 512 + (k + 1) * 64], True, True) for k in range(8)]),
                                 reads=[b("MT"), b("xdt")], writes=[b("PY")])
                            S.op("pe", e_mm([(ps[4][0:Q, :], xsb[:, 10 + g, c0:c0 + Q], Sb[:, g * 512:(g + 1) * 512], True, True)]),
                                 reads=[b("xsb"), b("Sb")], writes=[b("PSG0")])
                            S.op("pe", e_mm([(ps[5][:, :], Btm[0:Q, g * P:(g + 1) * P], xdd[0:Q, g * 512:(g + 1) * 512], True, True)]),
                                 reads=[b("Btm"), b("xdd")], writes=[b("PSG1")])
                            mark("ssd_mm")
                            eA_b = sml[0:Q, g * 8:(g + 1) * 8].unsqueeze(2).to_broadcast([Q, 8, 64])
                            D_b = rows[0:Q, R_D + g * 8:R_D + (g + 1) * 8].unsqueeze(2).to_broadcast([Q, 8, 64])
                            S.op("dve", e_tt(yo[0:Q, :].rearrange("p (k d) -> p k d", k=8),
                                             ps[4][0:Q, :].rearrange("p (k d) -> p k d", k=8), eA_b, ALU.mult),
                                 reads=[b("PSG0"), b("sml")], writes=[b("yo")])
                            S.op("dve", e_tt(xD[0:Q, :].rearrange("p (k d) -> p k d", k=8),
                                             PT4[:, g * 8:(g + 1) * 8, :], D_b, ALU.mult),
                                 reads=[b("PT"), b("rows")], writes=[b("xD")])
                            S.op("dve", e_tt(yo[0:Q, :], yo[0:Q, :], xD[0:Q, :], ALU.add), reads=[b("yo"), b("xD")], writes=[b("yo")])
                            S.op("dve", e_tt(yo[0:Q, :], yo[0:Q, :], PY[0:Q, :], ALU.add), reads=[b("yo"), b("PY")], writes=[b("yo")])
                            mark("ssd_y")
                            ed_b = sml[:, 32 + g * 8:32 + (g + 1) * 8].unsqueeze(2).to_broadcast([P, 8, 64])
                            Sg = Sst[:, g * 512:(g + 1) * 512]
                            S.op("dve", e_tt(Sg.rearrange("p (k d) -> p k d", k=8), Sg.rearrange("p (k d) -> p k d", k=8),
                                              ed_b, ALU.mult), reads=[b("Sst"), b("sml")], writes=[b("Sst")])
                            S.op("dve", e_tt(Sg, Sg, ps[5][:, :], ALU.add), reads=[b("Sst"), b("PSG1")], writes=[b("Sst")])
                            S.op("act", e_acp(Sb[:, g * 512:(g + 1) * 512], Sg), reads=[b("Sst")], writes=[b("Sb")])
                            mark("ssd_st")
                            S.op("pe", e_mm([(ps[7][0:Q, :], hn_cur[:, k, c0:c0 + Q], Win[:, k, OFF_Z + g * 512:OFF_Z + (g + 1) * 512],
                                              k == 0, k == KC - 1) for k in range(KC)]),
                                 reads=[bhn, b("Win")], writes=[b("PY")])
                            sigmoid_act(tz[0:Q, :], ps[7][0:Q, :], [b("PY")], 0.0)
                            S.op("dve", e_tt(yz[0:Q, :], yo[0:Q, :], ps[7][0:Q, :], ALU.mult), reads=[b("yo"), b("PY")], writes=[b("yz")])
                            S.op("dve", e_tt(yz[0:Q, :], yz[0:Q, :], tz[0:Q, :], ALU.mult), reads=[b("yz"), b("tz")], writes=[b("yz")])
                            mark("ssd_z")
                            S.op("act", e_act(junk[0:Q, 0:512], yz[0:Q, :], AF.Square, accum=gss[0:Q, 0:1]),
                                 reads=[b("yz")], writes=[b("junk"), b("gss")])
                            S.op("act", e_act(gss[0:Q, 1:2], gss[0:Q, 0:1], AF.Ln, bias=EPS, scale=1.0 / 512),
                                 reads=[b("gss")], writes=[b("gss")])
                            S.op("act", e_act(gss[0:Q, 2:3], gss[0:Q, 1:2], AF.Exp, scale=-0.5), reads=[b("gss")], writes=[b("gss")])
                            S.op("act", e_act(yn[0:Q, :], yz[0:Q, :], AF.Copy, scale=gss[0:Q, 2:3]),
                                 reads=[b("yz"), b("gss")], writes=[b("yn")])
                            mark("ssd_n")
                            PSMy = PSMb[:, 512:512 + 4 * Q].rearrange("p (c q) -> p c q", c=4)
                            S.op("pe", e_tr([(PSMy[:, c, :], yn[0:Q, c * P:(c + 1) * P], identb[0:Q, 0:Q]) for c in range(4)]),
                                 reads=[b("yn"), b("identb")], writes=[b("PSM_bt")])
                            mark("ssd_ytr")
                            S.op("act", e_acp(yssd[:, g * 4:(g + 1) * 4, c0:c0 + Q], PSMy), reads=[b("PSM_bt")], writes=[b("yssd")])
                            mark("ssd_yev")

                def sec_out(ti):
                    t0 = ti * Tt
                    hn_cur = hnTs[ti % 2]
                    bhn = b("hnT%d" % (ti % 2))
                    for j in range(NSUB):
                        c0 = j * Q
                        xr = xin[j % 2]
                        bxr = b("xin%d" % (j % 2))
                        S.dma("sp", [(xr[0:Q, :], xsrc[t0 + c0:t0 + c0 + Q, :])], writes=[bxr])
                        for h2 in range(2):
                            items = []
                            for k in range(16):
                                lhs = ylru[:, k, c0:c0 + Q] if k < 8 else yssd[:, k - 8, c0:c0 + Q]
                                items.append((PA[h2][0:Q, :], lhs, Wout[:, k, h2 * 512:(h2 + 1) * 512], k == 0, k == 15))
                            S.op("pe", e_mm(items), reads=[b("ylru"), b("yssd"), b("Wout")], writes=[bPA[h2]])
                            S.op("dve", e_tt(osb[0:Q, h2 * 512:(h2 + 1) * 512], PA[h2][0:Q, :], gate[0:Q, h2 * 512:(h2 + 1) * 512], ALU.mult),
                                 reads=[bPA[h2], b("gate")], writes=[b("osb")])
                        if DBG_STOP == "dump_o1" and ti == DBG_TI and j == 0:
                            dbg = dout("dbg", [P, D])
                            stores.append(S.dma("sp", [(dbg, osb[:])], reads=[b("osb")], owner=b("dbg0")))
                            raise _Stop()
                        S.op("dve", e_tt(osb[0:Q, :], osb[0:Q, :], xr[0:Q, :], ALU.add), reads=[b("osb"), bxr], writes=[b("osb")])
                        if DBG_STOP == "dump_o2" and ti == DBG_TI and j == 0:
                            dbg = dout("dbg", [P, D])
                            stores.append(S.dma("sp", [(dbg, osb[:])], reads=[b("osb")], owner=b("dbg0")))
                            raise _Stop()
                        S.op("act", e_act(junk[0:Q, :], osb[0:Q, :], AF.Square, accum=ssq[0:Q, 4:5]),
                             reads=[b("osb")], writes=[b("junk"), b("ssq")])
                        S.op("act", e_act(ssq[0:Q, 5:6], ssq[0:Q, 4:5], AF.Ln, bias=EPS, scale=1.0 / D), reads=[b("ssq")], writes=[b("ssq")])
                        S.op("act", e_act(ssq[0:Q, 6:7], ssq[0:Q, 5:6], AF.Exp, scale=-0.5), reads=[b("ssq")], writes=[b("ssq")])
                        S.op("dve", e_stt(xr[0:Q, :], osb[0:Q, :], ssq[0:Q, 6:7], fng[0:Q, :], ALU.mult, ALU.mult),
                             reads=[b("osb"), b("ssq"), b("fng")], writes=[bxr])
                        if DBG_STOP == "dump_o3" and ti == DBG_TI and j == 0:
                            dbg = dout("dbg", [P, D + 16])
                            stores.append(S.dma("sp", [(dbg[:, 0:D], xr[:])], reads=[bxr], owner=b("dbg0")))
                            stores.append(S.dma("sp", [(dbg[:, D:D + 16], ssq[:])], reads=[b("ssq")], owner=b("dbg1")))
                            raise _Stop()
                        stores.append(S.dma("sp", [(ydst[t0 + c0:t0 + c0 + Q, :], xr[0:Q, :])], reads=[bxr],
                                            owner=b("xout%d" % (j % 2))))


                def rec(fn, ti):
                    S.begin()
                    fn(ti)
                    return S.end()

                S.replay([rec(sec_stage1, 0)])
                for ti in range(NT):
                    chA = rec(sec_lru, ti)
                    chB = rec(sec_xbc, ti) + rec(sec_ssd, ti)
                    chC = rec(sec_stage1, ti + 1) if ti + 1 < NT else []
                    S.replay([chA, chB, chC])
                    mark("ssd")
                    S.replay([rec(sec_out, ti)])
                Tlast = (NT - 1) % 2
                hnT = hnTs[Tlast]

                mark("out")
                for blk in range(5):
                    col0 = OFF_LX + blk * 512 if blk < 2 else OFF_XBC + (blk - 2) * 512
                    S.op("pe", e_mm([(PY[0:3, :], hnT[:, k, Tt - 3:Tt], Win[:, k, col0:col0 + 512], k == 0, k == KC - 1)
                                     for k in range(KC)]),
                         reads=[b("hnT%d" % Tlast), b("Win")], writes=[b("PY")])
                    stgt, stgb = (yo, b("yo")) if blk % 2 == 0 else (xD, b("xD"))
                    S.op("act", e_acp(stgt[0:3, :], PY[0:3, :]), reads=[b("PY")], writes=[stgb])
                    dst = o_lc[grp][gi][:, blk * 512:(blk + 1) * 512] if blk < 2 else o_sc[grp][gi][:, (blk - 2) * 512:(blk - 1) * 512]
                    stores.append(S.dma("sp", [(dst, stgt[0:3, :])], reads=[stgb], owner=b("cso%d" % (blk % 2))))
                S.op("pe", e_tr([(PSG1[0:8, 0:P], hprev[:, 0:8], ident)]), reads=[b("hprev"), b("cst")], writes=[b("PSG1")])
                S.op("act", e_acp(gss[0:8, :].bitcast(F32) if False else hT[0:8, :], PSG1[0:8, 0:P]), reads=[b("PSG1")], writes=[b("hT")])
                stores.append(S.dma("sp", [(o_lh[grp][gi].rearrange("(c p) -> c p", p=P), hT[0:8, :])], reads=[b("hT")]))
                for c in range(8):
                    S.op("pe", e_tr([(PSG0[:, 0:P], Sst[:, c * P:(c + 1) * P], ident)]), reads=[b("Sst"), b("cst")], writes=[b("PSG0")])
                    S.op("act", e_acp(sto[:], PSG0[:, 0:P]), reads=[b("PSG0")], writes=[b("sto")])
                    stores.append(S.dma("sp", [(o_ss[grp][gi, c * P:(c + 1) * P, :], sto[:])], reads=[b("sto")]))
                mark("seqend")
        except _Stop:
            pass
        S.finish("sp", stores)
        S.run()
    return nc


def make_consts():
    c = np.zeros((P, 4 * P), np.float32)
    j = np.arange(P)[:, None]
    s = np.arange(P)[None, :]
    c[:, 0:P] = (j == s)
    c[:, P:2 * P] = (j > s)
    c[:, 2 * P:3 * P] = (j <= s)
    c[:, 3 * P:4 * P] = 1.0
    return c


def core_inputs(inp, pidx, sidx):
    f = lambda a: np.ascontiguousarray(np.asarray(a, dtype=np.float32))
    m = {
        "xp": f(inp["x_prompt"][pidx]),
        "xs": f(inp["x_sample"][sidx]),
        "cc": f(np.concatenate([inp["c_prompt"][pidx], inp["c_sample"][sidx]], axis=0)),
        "st_lc": f(inp["state_lru_conv"][0][sidx]),
        "st_lh": f(inp["state_lru_h"][0][sidx]),
        "st_sc": f(inp["state_ssd_conv"][0][sidx]),
        "st_ss": f(np.asarray(inp["state_ssd"][0][sidx]).reshape(len(sidx), D, P)),
        "consts": make_consts(),
    }
    for k in ("norm_g", "w_ada", "b_ada", "w_in", "lru_conv_w", "lru_conv_b", "lru_w_a", "lru_b_a", "lru_w_x",
              "lru_b_x", "lru_lambda", "ssd_conv_w", "ssd_conv_b", "ssd_dt_bias", "ssd_a_log", "ssd_d",
              "ssd_norm_g", "w_out"):
        m[k] = f(np.asarray(inp[k])[0])
    m["final_norm_g"] = f(inp["final_norm_g"])
    return m


_NC_CACHE = {}


def kernel(**inputs):
    B_, L_ = inputs["x_prompt"].shape[:2]
    DB, DL = inputs["x_sample"].shape[:2]
    n = NCORES
    NP, NS = B_ // n, DB // n
    key = (NP, L_, NS, DL)
    if key not in _NC_CACHE:
        _NC_CACHE[key] = build_program(NP, L_, NS, DL)
    nc = _NC_CACHE[key]
    in_maps = []
    for i in range(n):
        pidx = list(range(i * NP, (i + 1) * NP))
        sidx = list(range(i * NS, (i + 1) * NS))
        in_maps.append(core_inputs(inputs, pidx, sidx))
    res = run_bass_kernel_spmd(nc, in_maps, core_ids=list(range(n)))
    R = res.results
    cat = lambda name: np.concatenate([np.asarray(r[name], dtype=np.float32) for r in R], axis=0)
    y_p = cat("yp")
    y_s = cat("ys")
    outs = [y_p, y_s]
    for sfx, nb in (("p", B_), ("s", DB)):
        outs.append(cat("o_lc_" + sfx)[None])
        outs.append(cat("o_lh_" + sfx)[None])
        outs.append(cat("o_sc_" + sfx)[None])
        outs.append(cat("o_ss_" + sfx).reshape(1, nb, 2, 8, 64, 128))
    return tuple(outs)
```

```python
import math
from contextlib import ExitStack

import numpy as np
import concourse.bass as bass
import concourse.mybir as mybir
from concourse.bass_utils import run_bass_kernel_spmd

F32 = mybir.dt.float32
BF16 = mybir.dt.bfloat16
ALU = mybir.AluOpType
AF = mybir.ActivationFunctionType

P = 128
D = 1024
KC = 8
NCORES = 8
D_XBC = 1536
IN_COLS = 4624
EPS = 1e-6
OFF_LX, OFF_LG, OFF_XBC, OFF_Z, OFF_DT = 0, 1024, 2048, 3584, 4608

ENGS = ("pe", "act", "dve", "pool", "sp")


class Buf:
    __slots__ = ("name", "w", "r", "dsem", "dcnt", "excl")

    def __init__(self, name, excl=False):
        self.name = name
        self.excl = excl
        self.w = None
        self.r = {}
        self.dsem = None
        self.dcnt = 0


class Sched:
    def __init__(self, nc, stack):
        self.nc = nc
        self.stack = stack
        self.q = {e: [] for e in ENGS}
        self.esem = {}
        for e in ENGS:
            if e != "sp":
                self.esem[e] = stack.enter_context(nc.semaphore("es_" + e))
        self.ecnt = {e: 0 for e in ENGS}
        self.known = {e: {} for e in ENGS}
        self.nd = 0
        self.rec = None

    def begin(self):
        self.rec = []

    def end(self):
        r, self.rec = self.rec, None
        return r

    def replay(self, chains):
        chains = [c for c in chains if c]
        pos = [0] * len(chains)
        while True:
            best, bf = None, None
            for i, c in enumerate(chains):
                if pos[i] < len(c):
                    f = pos[i] / len(c)
                    if bf is None or f < bf:
                        best, bf = i, f
            if best is None:
                break
            kind, args, kw, ph = chains[best][pos[best]]
            pos[best] += 1
            tok = (self.op if kind == "op" else self.dma)(*args, **kw)
            ph[0] = tok

    @staticmethod
    def _tok(t):
        return t[0] if isinstance(t, list) else t

    def _waits(self, eng, toks):
        need = {}
        for t in toks:
            if t is None:
                continue
            sem, val = t
            k = id(sem)
            if self.known[eng].get(k, 0) >= val:
                continue
            if k not in need or need[k][1] < val:
                need[k] = (sem, val)
        for k, (sem, val) in need.items():
            self.known[eng][k] = val
        return list(need.values())

    @staticmethod
    def _deps(reads, writes):
        toks = []
        for b in reads:
            toks.append(b.w)
        for b in writes:
            toks.append(b.w)
            toks.extend(b.r.values())
        return toks

    @staticmethod
    def _commit(tok, reads, writes):
        k = id(tok[0])
        for b in reads:
            b.r[k] = tok
        for b in writes:
            b.w = tok
            b.r = {}

    def op(self, eng, emit, reads=(), writes=()):
        if self.rec is not None:
            ph = [None]
            self.rec.append(("op", (eng, emit, tuple(reads), tuple(writes)), {}, ph))
            return ph
        if any(x.excl for x in reads):
            writes = list(writes) + [x for x in reads if x.excl and x not in writes]
            reads = [x for x in reads if not x.excl]
        waits = self._waits(eng, self._deps(reads, writes))
        self.ecnt[eng] += 1
        tok = (self.esem[eng], self.ecnt[eng])
        self.q[eng].append((waits, emit, self.esem[eng]))
        self._commit(tok, reads, writes)
        return tok

    def dma(self, eng, parts, reads=(), writes=(), owner=None, **kw):
        if self.rec is not None:
            ph = [None]
            kw2 = dict(kw)
            kw2.update(reads=tuple(reads), writes=tuple(writes), owner=owner)
            self.rec.append(("dma", (eng, parts), kw2, ph))
            return ph
        if owner is None:
            owner = writes[0] if writes else reads[0]
        if owner.dsem is None:
            self.nd += 1
            owner.dsem = self.stack.enter_context(self.nc.semaphore("ds%d" % self.nd))
        waits = self._waits(eng, self._deps(reads, writes))
        sem = owner.dsem
        owner.dcnt += 16 * len(parts)
        tok = (sem, owner.dcnt)

        def emit(e, parts=parts, sem=sem, kw=kw):
            for (o, i) in parts:
                e.dma_start(out=o, in_=i, **kw).then_inc(sem, 16)
            return None
        self.q[eng].append((waits, emit, None))
        self._commit(tok, reads, writes)
        return tok

    def barrier(self, extra=()):
        toks = [(self.esem[x], self.ecnt[x]) for x in self.esem if self.ecnt[x] > 0] + [self._tok(t) for t in extra]
        for e in ENGS:
            w = self._waits(e, toks)
            if w:
                self.q[e].append((w, None, None))

    def finish(self, eng, toks):
        self.q[eng].append((self._waits(eng, [self._tok(t) for t in toks]), None, None))

    def run(self):
        nc, q = self.nc, self.q

        def play(e, items):
            for waits, emit, inc in items:
                for sem, val in waits:
                    e.wait_ge(sem, val)
                if emit is None:
                    continue
                ins = emit(e)
                if inc is not None:
                    ins.then_inc(inc, 1)

        with nc.Block() as block:
            @block.tensor
            def _(e):
                play(e, q["pe"])

            @block.scalar
            def _(e):
                play(e, q["act"])

            @block.vector
            def _(e):
                play(e, q["dve"])

            @block.gpsimd
            def _(e):
                play(e, q["pool"])

            @block.sync
            def _(e):
                play(e, q["sp"])


class _Stop(Exception):
    pass


DBG_STOP = None
DBG_REV = False
DBG_BARRIER = False
DBG_H = 0
DBG_TI = 0


_MARKS = {}


def mark(n):
    if DBG_STOP is None:
        return
    _MARKS[n] = _MARKS.get(n, 0) + 1
    if DBG_STOP == n or DBG_STOP == "%s:%d" % (n, _MARKS[n]):
        raise _Stop()


def build_program(NP, LP, NS, LS, T=256):
    NSEQ = NP + NS
    nc = bass.Bass("TRN2", target_bir_lowering=False)

    def din(name, shape):
        return nc.dram_tensor(name, list(shape), F32, kind="ExternalInput").ap()

    def dout(name, shape):
        return nc.dram_tensor(name, list(shape), F32, kind="ExternalOutput").ap()

    xp = din("xp", [NP, LP, D])
    xs_in = din("xs", [NS, LS, D])
    cc = din("cc", [NSEQ, D])
    st_lc = din("st_lc", [NS, 3, D])
    st_lh = din("st_lh", [NS, D])
    st_sc = din("st_sc", [NS, 3, D_XBC])
    st_ss = din("st_ss", [NS, D, P])
    norm_g = din("norm_g", [D])
    w_ada = din("w_ada", [D, 3 * D])
    b_ada = din("b_ada", [3 * D])
    w_in = din("w_in", [D, IN_COLS])
    lru_conv_w = din("lru_conv_w", [4, D])
    lru_conv_b = din("lru_conv_b", [D])
    lru_w_a = din("lru_w_a", [8, P, P])
    lru_b_a = din("lru_b_a", [D])
    lru_w_x = din("lru_w_x", [8, P, P])
    lru_b_x = din("lru_b_x", [D])
    lru_lambda = din("lru_lambda", [D])
    ssd_conv_w = din("ssd_conv_w", [4, D_XBC])
    ssd_conv_b = din("ssd_conv_b", [D_XBC])
    ssd_dt_bias = din("ssd_dt_bias", [16])
    ssd_a_log = din("ssd_a_log", [16])
    ssd_d = din("ssd_d", [16])
    ssd_norm_g = din("ssd_norm_g", [D])
    w_out = din("w_out", [2 * D, D])
    final_norm_g = din("final_norm_g", [D])
    consts = din("consts", [P, 4 * P])

    yp = dout("yp", [NP, LP, D])
    ys = dout("ys", [NS, LS, D])
    o_lc = [dout("o_lc_p", [NP, 3, D]), dout("o_lc_s", [NS, 3, D])]
    o_lh = [dout("o_lh_p", [NP, D]), dout("o_lh_s", [NS, D])]
    o_sc = [dout("o_sc_p", [NP, 3, D_XBC]), dout("o_sc_s", [NS, 3, D_XBC])]
    o_ss = [dout("o_ss_p", [NP, D, P]), dout("o_ss_s", [NS, D, P])]

    with ExitStack() as st:
        S = Sched(nc, st)
        stores = []

        def sb(name, shape, dt=F32):
            return st.enter_context(nc.sbuf_tensor(name, list(shape), dt))

        Win = sb("Win", [P, KC, IN_COLS], BF16)
        Wout = sb("Wout", [P, 16, D], BF16)
        Wa = sb("Wa", [P, 8, P], BF16)
        Wx = sb("Wx", [P, 8, P], BF16)
        cst = sb("cst", [P, 4 * P])
        identb = sb("identb", [P, P], BF16)
        NCOL = 8 + 32 + 8 + 8 + 8 + 8 + 48 + 12 + 8 + 16
        colp = sb("colp", [P, NCOL])
        dcol = sb("dcol", [P, 64])
        rows = sb("rows", [P, 64])
        fng = sb("fng", [P, D])
        gate = sb("gate", [P, D])
        modg = sb("modg", [8, D])
        cT = sb("cT", [P, KC, NSEQ])
        cTb = sb("cTb", [P, KC, NSEQ], BF16)
        modc = sb("modc", [P, 16, NSEQ])
        gsc = sb("gsc", [P, KC, NSEQ])

        xin = [sb("xin0", [P, D]), sb("xin1", [P, D])]
        junk = sb("junk", [P, D], BF16)
        ssq = sb("ssq", [P, 16])
        xn = sb("xn", [P, D], BF16)
        hnTs = [sb("hnT0", [P, KC, T], BF16), sb("hnT1", [P, KC, T], BF16)]
        lx = sb("lx", [P, T + 3])
        lx2 = sb("lx2", [P, T + 3])
        u2 = sb("u2", [P, T])
        ga2 = sb("ga2", [P, T])
        u = sb("u", [P, T])
        ub = sb("ub", [P, T], BF16)
        ga = sb("ga", [P, T])
        aa = sb("aa", [P, T])
        gm = sb("gm", [P, T])
        gx = sb("gx", [P, T])
        gb = sb("gb", [P, T])
        hh = sb("hh", [P, T])
        gg = sb("gg", [P, T])
        ylru = sb("ylru", [P, 8, T], BF16)
        xsb = sb("xsb", [P, 12, T], BF16)
        yssd = sb("yssd", [P, 8, T], BF16)
        lhalo = sb("lhalo", [P, 8, 3])
        shalo = sb("shalo", [P, 12, 3])
        hprev = sb("hprev", [P, 8])
        Sst = sb("Sst", [P, D])
        Sb = sb("Sb", [P, D], BF16)
        dtv = sb("dtv", [P, 16])
        dtt = sb("dtt", [P, 16])
        dA = sb("dA", [P, 16])
        sml = sb("sml", [P, 48])
        Ubig = sb("Ubig", [P, 8, P])
        Lm = sb("Lm", [P, 8, P])
        MT = sb("MT", [P, 8, P], BF16)
        CBm = sb("CBm", [P, P])
        xdt = sb("xdt", [P, D], BF16)
        xdd = sb("xdd", [P, D], BF16)
        xD = sb("xD", [P, 512])
        Btm = sb("Btm", [P, 256], BF16)
        yo = sb("yo", [P, 512])
        tz = sb("tz", [P, 512])
        yz = sb("yz", [P, 512])
        yn = sb("yn", [P, 512], BF16)
        gss = sb("gss", [P, 8])
        osb = sb("osb", [P, D])
        sto = sb("sto", [P, P])
        hT = sb("hT", [8, P])
        stg = xin[1]
        badag = gate
        rsel = osb
        wadab = osb.bitcast(BF16)

        ps = [st.enter_context(nc.psum_tensor("ps%d" % i, [P, 512], F32)) for i in range(8)]
        PA = [ps[0], ps[1]]
        PG, PT, PSG0, PSG1, PSM, PY = ps[2], ps[3], ps[4], ps[5], ps[6], ps[7]
        PTb = PT.bitcast(BF16)
        PT1b = ps[2].bitcast(BF16)

        B = {}

        PSUM_NAMES = ("PA0", "PA1", "PG", "PT", "PSG0", "PSG1", "PSM", "PY")

        def b(name):
            if name.startswith("PSM"):
                name = "PSM"
            if name not in B:
                B[name] = Buf(name, excl=name in PSUM_NAMES)
            return B[name]

        bPA = [b("PA0"), b("PA1")]

        ident = cst[:, 0:P]
        Umat = cst[:, P:2 * P]
        Vmat = cst[:, 2 * P:3 * P]
        ones = cst[:, 3 * P:4 * P]

        C_NG, C_LCW, C_LCB, C_LBA, C_LBX, C_LAM, C_SCW, C_SCB, C_GSSD, C_BADA = (
            0, 8, 40, 48, 56, 64, 72, 120, 132, 140)
        DC_NBA, DC_NBX, DC_M8, DC_M16 = 0, 8, 16, 24
        R_DTB, R_A, R_D = 0, 16, 32

        def e_ts(out, in0, s1, s2, op0, op1=None):
            if op1 is None:
                return lambda e: e.tensor_scalar(out, in0, s1, None, op0)
            return lambda e: e.tensor_scalar(out, in0, s1, s2, op0, op1)

        def e_tt(out, a, bb, op):
            return lambda e: e.tensor_tensor(out, a, bb, op)

        def e_stt(out, in0, sc, in1, op0, op1):
            return lambda e: e.scalar_tensor_tensor(out, in0, sc, in1, op0, op1)

        def e_cp(out, in_):
            return lambda e: e.tensor_copy(out, in_)

        def e_acp(out, in_):
            return lambda e: e.copy(out, in_)

        def e_act(out, in_, func, bias=0.0, scale=1.0, accum=None):
            if accum is None:
                return lambda e: e.activation(out, in_, func, bias=bias, scale=scale)
            return lambda e: e.activation(out, in_, func, bias=bias, scale=scale, accum_out=accum)

        def e_mm(items):
            def emit(e):
                ins = None
                for (o, l, r, s0, s1) in items:
                    ins = e.matmul(o, l, r, start=s0, stop=s1)
                return ins
            return emit

        def e_tr(items):
            def emit(e):
                ins = None
                for (o, i, idn) in items:
                    ins = e.transpose(o, i, idn)
                return ins
            return emit

        def e_memset(ap, v):
            return lambda e: e.memset(ap, v)

        try:
            S.dma("sp", [(cst[:], consts)], writes=[b("cst")])
            S.op("dve", e_cp(identb[:], ident), reads=[b("cst")], writes=[b("identb")])

            def colload(off, n, src):
                S.dma("sp", [(colp[:, off:off + n], src.rearrange("(c p) -> p c", p=P))],
                      writes=[b("colp")], owner=b("colp_d%d" % off), allow_slow_non_contiguous=True)

            colload(C_NG, 8, norm_g)
            colload(C_LCB, 8, lru_conv_b)
            colload(C_LBA, 8, lru_b_a)
            colload(C_LBX, 8, lru_b_x)
            colload(C_LAM, 8, lru_lambda)
            colload(C_SCB, 12, ssd_conv_b)
            colload(C_GSSD, 8, ssd_norm_g)
            colload(C_BADA, 16, b_ada[0:2 * D])
            for k in range(4):
                colload(C_LCW + 8 * k, 8, lru_conv_w[k])
                colload(C_SCW + 12 * k, 12, ssd_conv_w[k])
            S.dma("sp", [(rows[:, R_DTB:R_DTB + 16], ssd_dt_bias.partition_broadcast(P)),
                         (rows[:, R_A:R_A + 16], ssd_a_log.partition_broadcast(P)),
                         (rows[:, R_D:R_D + 16], ssd_d.partition_broadcast(P))], writes=[b("rows")])
            S.dma("sp", [(fng[:], final_norm_g.partition_broadcast(P))], writes=[b("fng")])
            S.dma("sp", [(badag[0:NSEQ, :], b_ada[2 * D:3 * D].partition_broadcast(NSEQ))], writes=[b("gate")])
            S.dma("sp", [(cT[:, :, r], cc[r].rearrange("(k p) -> p k", p=P)) for r in range(NSEQ)], writes=[b("cT")],
                  allow_slow_non_contiguous=True)

            S.op("dve", e_ts(dcol[:, DC_NBA:DC_NBA + 16], colp[:, C_LBA:C_LBA + 16], -1.0, None, ALU.mult),
                 reads=[b("colp")], writes=[b("dcol")])
            S.op("act", e_act(dcol[:, 32:40], colp[:, C_LAM:C_LAM + 8], AF.Exp, scale=-1.0),
                 reads=[b("colp")], writes=[b("dcol")])
            S.op("act", e_act(dcol[:, 32:40], dcol[:, 32:40], AF.Ln, bias=1.0), reads=[b("dcol")], writes=[b("dcol")])
            S.op("dve", e_ts(dcol[:, DC_M8:DC_M8 + 8], dcol[:, 32:40], -8.0, None, ALU.mult),
                 reads=[b("dcol")], writes=[b("dcol")])
            S.op("dve", e_ts(dcol[:, DC_M16:DC_M16 + 8], dcol[:, 32:40], -16.0, None, ALU.mult),
                 reads=[b("dcol")], writes=[b("dcol")])
            S.op("act", e_act(rows[:, R_A:R_A + 16], rows[:, R_A:R_A + 16], AF.Exp), reads=[b("rows")], writes=[b("rows")])
            S.op("dve", e_ts(rows[:, R_A:R_A + 16], rows[:, R_A:R_A + 16], -1.0, None, ALU.mult),
                 reads=[b("rows")], writes=[b("rows")])

            w_in_v = w_in.rearrange("(k p) n -> p k n", p=P)
            for k in range(KC):
                S.dma("pool", [(Win[:, k, :], w_in_v[:, k, :])], writes=[b("Win")], owner=b("Win_d%d" % k))
            S.dma("pool", [(Wa[:], lru_w_a.rearrange("h i j -> i h j")),
                           (Wx[:], lru_w_x.rearrange("h i j -> i h j"))], writes=[b("Wg")])
            w_out_v = w_out.rearrange("(k p) n -> p k n", p=P)
            for k in range(8):
                S.dma("pool", [(Wout[:, k, :], w_out_v[:, k, :])], writes=[b("Wout")], owner=b("Wout_d%d" % k))
            for k in range(8, 16):
                S.dma("sp", [(stg[:], w_out_v[:, k, :])], writes=[b("xin1")])
                S.op("dve", e_ts(Wout[:, k, :], stg[:], colp[:, C_GSSD + k - 8:C_GSSD + k - 7], None, ALU.mult),
                     reads=[b("xin1"), b("colp")], writes=[b("Wout")])

            cTf = cT[:].rearrange("p k r -> p (k r)")
            S.op("act", e_act(modc[:].rearrange("p a r -> p (a r)")[:, 0:KC * NSEQ], cTf, AF.Exp, scale=-1.0),
                 reads=[b("cT")], writes=[b("modc")])
            mtmp = modc[:].rearrange("p a r -> p (a r)")[:, 0:KC * NSEQ]
            S.op("act", e_act(mtmp, mtmp, AF.Ln, bias=1.0), reads=[b("modc")], writes=[b("modc")])
            S.op("act", e_act(mtmp, mtmp, AF.Exp, scale=-1.0), reads=[b("modc")], writes=[b("modc")])
            S.op("dve", e_tt(cTb[:].rearrange("p k r -> p (k r)"), cTf, mtmp, ALU.mult),
                 reads=[b("cT"), b("modc")], writes=[b("cTb")])
            w_ada_v = w_ada.rearrange("(k p) n -> p k n", p=P)
            wadab3 = wadab[:, 0:KC * 256].rearrange("p (k n) -> p k n", k=KC)
            PSMc = PSM[:, 0:16 * NSEQ].rearrange("p (a r) -> p a r", a=16)
            for blk in range(12):
                S.dma("pool", [(wadab3, w_ada_v[:, :, blk * 256:(blk + 1) * 256])], writes=[b("osb")])
                if blk < 8:
                    for half in range(2):
                        cb = blk * 2 + half
                        S.op("pe", e_mm([(PSMc[:, cb, :], wadab3[:, k, half * P:(half + 1) * P], cTb[:, k, :],
                                          k == 0, k == KC - 1) for k in range(KC)]),
                             reads=[b("osb"), b("cTb")], writes=[b("PSM")])
                else:
                    gb_ = blk - 8
                    bank = PA[gb_ // 2]
                    S.op("pe", e_mm([(bank[0:NSEQ, (gb_ % 2) * 256:(gb_ % 2) * 256 + 256], cTb[:, k, :],
                                      wadab3[:, k, :], k == 0, k == KC - 1) for k in range(KC)]),
                         reads=[b("osb"), b("cTb")], writes=[bPA[gb_ // 2]])
            S.op("dve", e_tt(modc[:], PSMc, colp[:, C_BADA:C_BADA + 16].unsqueeze(2).to_broadcast([P, 16, NSEQ]), ALU.add),
                 reads=[b("PSM"), b("colp")], writes=[b("modc")])
            S.op("dve", e_ts(gsc[:], modc[:, 8:16, :], 1.0, None, ALU.add), reads=[b("modc")], writes=[b("gsc")])
            S.op("dve", e_tt(gsc[:], gsc[:], colp[:, C_NG:C_NG + 8].unsqueeze(2).to_broadcast([P, KC, NSEQ]), ALU.mult),
                 reads=[b("gsc"), b("colp")], writes=[b("gsc")])
            for h2 in range(2):
                S.op("dve", e_tt(modg[0:NSEQ, h2 * 512:(h2 + 1) * 512], PA[h2][0:NSEQ, :],
                                 badag[0:NSEQ, h2 * 512:(h2 + 1) * 512], ALU.add),
                     reads=[bPA[h2], b("gate")], writes=[b("modg")])

            mark("setup")
            def seq_views(si):
                if si < NP:
                    return xp[si], yp[si], 0, si, LP
                return xs_in[si - NP], ys[si - NP], 1, si - NP, LS

            tile_ctr = [0]

            seq_order = list(range(NP, NSEQ)) + list(range(NP))
            for si in (seq_order if not DBG_REV else seq_order[::-1]):
                xsrc, ydst, grp, gi, L = seq_views(si)
                if DBG_BARRIER:
                    S.barrier(stores)
                is_prompt = grp == 0
                Tt = min(T, L)
                Q = min(P, Tt)
                NSUB = Tt // Q
                NT = L // Tt
                assert L % Tt == 0 and Tt % Q == 0

                if is_prompt:
                    S.op("pool", e_memset(hprev[:], 0.0), writes=[b("hprev")])
                    S.op("pool", e_memset(lhalo[:], 0.0), writes=[b("lhalo")])
                    S.op("pool", e_memset(shalo[:], 0.0), writes=[b("shalo")])
                    S.op("pool", e_memset(Sst[:], 0.0), writes=[b("Sst")])
                    S.op("pool", e_memset(Sb[:], 0.0), writes=[b("Sb")])
                else:
                    S.dma("sp", [(xin[0][0:3, :], st_lc[gi]), (xin[1][0:3, :], st_sc[gi][:, 0:1024]),
                                 (osb[0:3, 0:512], st_sc[gi][:, 1024:1536])],
                          writes=[b("xin0"), b("xin1"), b("osb")], owner=b("stld"))
                    S.dma("sp", [(osb[32:33, :], st_lh[gi].rearrange("(o d) -> o d", o=1))], writes=[b("osb")], owner=b("stld2"))
                    PSs = PSG1[:, 0:64]
                    S.op("pe", e_tr([(PSs[:, c * 3:(c + 1) * 3], xin[0][0:3, c * P:(c + 1) * P], ident[0:3, 0:3]) for c in range(8)]
                                    + [(PSs[:, 24 + c * 3:24 + (c + 1) * 3], (xin[1][0:3, c * P:(c + 1) * P] if c < 8 else
                                                                          osb[0:3, (c - 8) * P:(c - 7) * P]), ident[0:3, 0:3]) for c in range(12)]),
                         reads=[b("xin0"), b("xin1"), b("osb"), b("cst")], writes=[b("PSG1")])
                    S.op("act", e_acp(lhalo[:].rearrange("p c j -> p (c j)"), PSs[:, 0:24]), reads=[b("PSG1")], writes=[b("lhalo")])
                    S.op("act", e_acp(shalo[:].rearrange("p c j -> p (c j)"), PSs[:, 24:60]), reads=[b("PSG1")], writes=[b("shalo")])
                    S.op("pe", e_tr([(PSG1[:, 64 + c:65 + c], osb[32:33, c * P:(c + 1) * P], ident[32:33, 32:33]) for c in range(8)]),
                         reads=[b("osb"), b("cst")], writes=[b("PSG1")])
                    S.op("act", e_acp(hprev[:], PSG1[:, 64:72]), reads=[b("PSG1")], writes=[b("hprev")])
                    for c4 in range(2):
                        S.dma("sp", [(osb[:, c4 * 512:(c4 + 1) * 512].rearrange("p (c n) -> p c n", c=4),
                                      st_ss[gi, c4 * 512:(c4 + 1) * 512, :].rearrange("(c p) n -> p c n", p=P))],
                              writes=[b("osb")])
                        S.op("pe", e_tr([(PSG0[:, c * P:(c + 1) * P], osb[:, c4 * 512 + c * P:c4 * 512 + (c + 1) * P], ident)
                                         for c in range(4)]),
                             reads=[b("osb"), b("cst")], writes=[b("PSG0")])
                        S.op("act", e_acp(Sst[:, c4 * 512:(c4 + 1) * 512], PSG0[:, :]), reads=[b("PSG0")], writes=[b("Sst")])
                    S.op("pool", e_cp(Sb[:], Sst[:]), reads=[b("Sst")], writes=[b("Sb")])
                S.op("dve", e_ts(rsel[0:NSEQ, :], modg[0:NSEQ, :], ident[0:NSEQ, si:si + 1], None, ALU.mult),
                     reads=[b("modg"), b("cst")], writes=[b("osb")])
                for h2 in range(2):
                    bank, bk = (PSG0, b("PSG0")) if h2 == 0 else (PSG1, b("PSG1"))
                    S.op("pe", e_mm([(bank[:, :], ones[0:NSEQ, :], rsel[0:NSEQ, h2 * 512:(h2 + 1) * 512], True, True)]),
                         reads=[b("osb"), b("cst")], writes=[bk])
                    S.op("act", e_acp(gate[:, h2 * 512:(h2 + 1) * 512], bank[:, :]), reads=[bk], writes=[b("gate")])

                if DBG_STOP == "dump_gate":
                    dbg = dout("dbg", [P, D])
                    stores.append(S.dma("sp", [(dbg, gate[:])], reads=[b("gate")], owner=b("dbg0")))
                    raise _Stop()
                mark("seqinit")
                def proj_fm(col0, bank, bbank, hn_cur, bhn):
                    S.op("pe", e_mm([(bank[:, 0:Tt], Win[:, k, col0:col0 + P], hn_cur[:, k, 0:Tt],
                                      k == 0, k == KC - 1) for k in range(KC)]),
                         reads=[b("Win"), bhn], writes=[bbank])

                def sigmoid_act(dst, src, src_reads, nbias):
                    S.op("act", e_act(dst, src, AF.Exp, bias=nbias, scale=-1.0), reads=src_reads, writes=[b(dst.tensor.name)])
                    S.op("act", e_act(dst, dst, AF.Ln, bias=1.0), reads=[b(dst.tensor.name)], writes=[b(dst.tensor.name)])
                    S.op("act", e_act(dst, dst, AF.Exp, scale=-1.0), reads=[b(dst.tensor.name)], writes=[b(dst.tensor.name)])

                def conv4(dst, src, wcol, bcol, nchunks, c, nsrc="lx", ndst="u"):
                    S.op("dve", e_ts(dst[:, 0:Tt], src[:, 0:Tt], colp[:, wcol + c:wcol + c + 1],
                                     colp[:, bcol + c:bcol + c + 1], ALU.mult, ALU.add),
                         reads=[b(nsrc), b("colp")], writes=[b(ndst)])
                    for k in range(1, 4):
                        S.op("dve", e_stt(dst[:, 0:Tt], src[:, k:k + Tt],
                                          colp[:, wcol + k * nchunks + c:wcol + k * nchunks + c + 1],
                                          dst[:, 0:Tt], ALU.mult, ALU.add),
                             reads=[b(nsrc), b(ndst), b("colp")], writes=[b(ndst)])

                def sec_stage1(ti):
                    t0 = ti * Tt
                    hn_cur = hnTs[ti % 2]
                    bhn = b("hnT%d" % (ti % 2))
                    for j in range(NSUB):
                        xb_ = xin[j % 2]
                        bxin = b("xin%d" % (j % 2))
                        S.dma("sp", [(xb_[0:Q, :], xsrc[t0 + j * Q:t0 + (j + 1) * Q, :])], writes=[bxin])
                        S.op("act", e_act(junk[0:Q, :], xb_[0:Q, :], AF.Square, accum=ssq[0:Q, 0:1]),
                             reads=[bxin], writes=[b("junk"), b("ssq")])
                        S.op("act", e_act(ssq[0:Q, 1:2], ssq[0:Q, 0:1], AF.Ln, bias=EPS, scale=1.0 / D),
                             reads=[b("ssq")], writes=[b("ssq")])
                        S.op("act", e_act(ssq[0:Q, 2:3], ssq[0:Q, 1:2], AF.Exp, scale=-0.5),
                             reads=[b("ssq")], writes=[b("ssq")])
                        S.op("act", e_act(xn[0:Q, :], xb_[0:Q, :], AF.Copy, scale=ssq[0:Q, 2:3]),
                             reads=[bxin, b("ssq")], writes=[b("xn")])
                        PT3 = PT1b[:, 0:KC * P].rearrange("p (k q) -> p k q", k=KC)
                        S.op("pe", e_tr([(PT3[:, k, 0:Q], xn[0:Q, k * P:(k + 1) * P], identb[0:Q, 0:Q]) for k in range(KC)]),
                             reads=[b("xn"), b("identb")], writes=[b("PG")])
                        for k in range(KC):
                            if k % 2 == 0:
                                S.op("act", e_act(hn_cur[:, k, j * Q:(j + 1) * Q], PT3[:, k, 0:Q], AF.Identity,
                                                  bias=modc[:, k, si:si + 1], scale=gsc[:, k, si:si + 1]),
                                     reads=[b("PG"), b("modc"), b("gsc")], writes=[bhn])
                            else:
                                S.op("dve", e_ts(hn_cur[:, k, j * Q:(j + 1) * Q], PT3[:, k, 0:Q], gsc[:, k, si:si + 1],
                                                 modc[:, k, si:si + 1], ALU.mult, ALU.add),
                                     reads=[b("PG"), b("modc"), b("gsc")], writes=[bhn])

                def sec_lru(ti):
                    t0 = ti * Tt
                    hn_cur = hnTs[ti % 2]
                    bhn = b("hnT%d" % (ti % 2))
                    pa_i = 0
                    for h in range(8):
                        S.op("pool", e_cp(lx[:, 0:3], lhalo[:, h, :]), reads=[b("lhalo")], writes=[b("lx")])
                        proj_fm(OFF_LX + h * P, ps[0], b("PA0"), hn_cur, bhn)
                        S.op("act", e_acp(lx[:, 3:3 + Tt], ps[0][:, 0:Tt]), reads=[b("PA0")], writes=[b("lx")])
                        pa_i ^= 1
                        S.op("pool", e_cp(lhalo[:, h, :], lx[:, Tt:Tt + 3]), reads=[b("lx")], writes=[b("lhalo")])
                        conv4(u, lx, C_LCW, C_LCB, 8, h)
                        S.op("act", e_acp(ub[:, 0:Tt], u[:, 0:Tt]), reads=[b("u")], writes=[b("ub")])
                        PG2 = ps[1][:, 0:2 * Tt].rearrange("p (a t) -> p a t", a=2)
                        S.op("pe", e_mm([(PG2[:, 0, :], Wa[:, h, :], ub[:, 0:Tt], True, True),
                                         (PG2[:, 1, :], Wx[:, h, :], ub[:, 0:Tt], True, True)]),
                             reads=[b("Wg"), b("ub")], writes=[b("PA1")])
                        sigmoid_act(ga[:, 0:Tt], PG2[:, 0, :], [b("PA1"), b("dcol")], dcol[:, DC_NBA + h:DC_NBA + h + 1])
                        S.op("act", e_act(aa[:, 0:Tt], ga[:, 0:Tt], AF.Exp, scale=dcol[:, DC_M8 + h:DC_M8 + h + 1]),
                             reads=[b("ga"), b("dcol")], writes=[b("aa")])
                        S.op("act", e_act(gm[:, 0:Tt], ga[:, 0:Tt], AF.Exp, scale=dcol[:, DC_M16 + h:DC_M16 + h + 1]),
                             reads=[b("ga"), b("dcol")], writes=[b("gm")])
                        S.op("act", e_act(gm[:, 0:Tt], gm[:, 0:Tt], AF.Ln, bias=1.0, scale=-1.0), reads=[b("gm")], writes=[b("gm")])
                        S.op("act", e_act(gm[:, 0:Tt], gm[:, 0:Tt], AF.Exp, scale=0.5), reads=[b("gm")], writes=[b("gm")])
                        sigmoid_act(gx[:, 0:Tt], PG2[:, 1, :], [b("PA1"), b("dcol")], dcol[:, DC_NBX + h:DC_NBX + h + 1])
                        S.op("dve", e_tt(gb[:, 0:Tt], gx[:, 0:Tt], u[:, 0:Tt], ALU.mult), reads=[b("gx"), b("u")], writes=[b("gb")])
                        if is_prompt and ti == 0:
                            S.op("pool", e_memset(gm[:, 0:1], 1.0), reads=[], writes=[b("gm")])
                        S.op("dve", e_tt(gb[:, 0:Tt], gb[:, 0:Tt], gm[:, 0:Tt], ALU.mult), reads=[b("gb"), b("gm")], writes=[b("gb")])
                        S.op("dve", (lambda h=h: (lambda e: e.tensor_tensor_scan(hh[:, 0:Tt], aa[:, 0:Tt], gb[:, 0:Tt],
                                                                                   hprev[:, h:h + 1], ALU.mult, ALU.add)))(),
                             reads=[b("aa"), b("gb"), b("hprev")], writes=[b("hh")])
                        S.op("pool", e_cp(hprev[:, h:h + 1], hh[:, Tt - 1:Tt]), reads=[b("hh")], writes=[b("hprev")])
                        if DBG_STOP == "dump_lru" and h == DBG_H and ti == DBG_TI:
                            dbg = dout("dbg", [P, 8 * 256])
                            for i_, (t_, nm_) in enumerate([(lx, "lx"), (u, "u"), (ga, "ga"), (aa, "aa"), (gm, "gm"), (gx, "gx"), (gb, "gb"), (hh, "hh")]):
                                stores.append(S.dma("sp", [(dbg[:, i_ * 256:(i_ + 1) * 256], t_[:, 0:256])], reads=[b(nm_)], owner=b("dbg%d" % i_)))
                            raise _Stop()
                        proj_fm(OFF_LG + h * P, ps[0], b("PA0"), hn_cur, bhn)
                        sigmoid_act(gg[:, 0:Tt], ps[0][:, 0:Tt], [b("PA0")], 0.0)
                        S.op("dve", e_tt(hh[:, 0:Tt], hh[:, 0:Tt], ps[0][:, 0:Tt], ALU.mult),
                             reads=[b("hh"), b("PA0")], writes=[b("hh")])
                        S.op("dve", e_tt(ylru[:, h, 0:Tt], hh[:, 0:Tt], gg[:, 0:Tt], ALU.mult),
                             reads=[b("hh"), b("gg")], writes=[b("ylru")])
                        pa_i ^= 1

                def sec_xbc(ti):
                    t0 = ti * Tt
                    hn_cur = hnTs[ti % 2]
                    bhn = b("hnT%d" % (ti % 2))
                    pa_i = 0
                    for c in range(12):
                        S.op("pool", e_cp(lx2[:, 0:3], shalo[:, c, :]), reads=[b("shalo")], writes=[b("lx2")])
                        proj_fm(OFF_XBC + c * P, ps[4 + pa_i], b("PSG%d" % pa_i), hn_cur, bhn)
                        S.op("act", e_acp(lx2[:, 3:3 + Tt], ps[4 + pa_i][:, 0:Tt]), reads=[b("PSG%d" % pa_i)], writes=[b("lx2")])
                        pa_i ^= 1
                        S.op("pool", e_cp(shalo[:, c, :], lx2[:, Tt:Tt + 3]), reads=[b("lx2")], writes=[b("shalo")])
                        conv4(u2, lx2, C_SCW, C_SCB, 12, c, "lx2", "u2")
                        sigmoid_act(ga2[:, 0:Tt], u2[:, 0:Tt], [b("u2")], 0.0)
                        S.op("dve", e_tt(xsb[:, c, 0:Tt], u2[:, 0:Tt], ga2[:, 0:Tt], ALU.mult),
                             reads=[b("u2"), b("ga2")], writes=[b("xsb")])

                def sec_ssd(ti):
                    t0 = ti * Tt
                    hn_cur = hnTs[ti % 2]
                    bhn = b("hnT%d" % (ti % 2))
                    for j in range(NSUB):
                        c0 = j * Q
                        PSM_dt = PSM[0:Q, 0:16]
                        S.op("pe", e_mm([(PSM_dt, hn_cur[:, k, c0:c0 + Q], Win[:, k, OFF_DT:OFF_DT + 16], k == 0, k == KC - 1)
                                         for k in range(KC)]),
                             reads=[bhn, b("Win")], writes=[b("PSM_dt")])
                        S.op("dve", e_tt(dtv[0:Q, :], PSM_dt, rows[0:Q, R_DTB:R_DTB + 16], ALU.add),
                             reads=[b("PSM_dt"), b("rows")], writes=[b("dtv")])
                        S.op("dve", e_ts(dtv[0:Q, :], dtv[0:Q, :], 30.0, None, ALU.min), reads=[b("dtv")], writes=[b("dtv")])
                        S.op("act", e_act(dtv[0:Q, :], dtv[0:Q, :], AF.Exp), reads=[b("dtv")], writes=[b("dtv")])
                        S.op("act", e_act(dtt[0:Q, :], dtv[0:Q, :], AF.Ln, bias=1.0), reads=[b("dtv")], writes=[b("dtt")])
                        S.op("dve", e_tt(dA[0:Q, :], dtt[0:Q, :], rows[0:Q, R_A:R_A + 16], ALU.mult),
                             reads=[b("dtt"), b("rows")], writes=[b("dA")])
                        mark("ssd_dt")
                        S.op("pe", e_mm([(PSM[0:Q, 64:80], Vmat[0:Q, 0:Q], dA[0:Q, :], True, True),
                                         (PSM[0:Q, 80:96], Umat[0:Q, 0:Q], dA[0:Q, :], True, True),
                                         (PSM[:, 96:112], ones[0:Q, :], dA[0:Q, :], True, True)]),
                             reads=[b("dA"), b("cst")], writes=[b("PSM_sm")])
                        S.op("act", e_act(sml[0:Q, 0:32], PSM[0:Q, 64:96], AF.Exp), reads=[b("PSM_sm")], writes=[b("sml")])
                        S.op("act", e_act(sml[:, 32:48], PSM[:, 96:112], AF.Exp), reads=[b("PSM_sm")], writes=[b("sml")])
                        mark("ssd_dec")
                        S.op("pe", e_tr([(PTb[0:Q, c * P:(c + 1) * P], xsb[:, c, c0:c0 + Q], identb[:, :]) for c in range(8)]),
                             reads=[b("xsb"), b("identb")], writes=[b("PT")])
                        PSMb = PSM.bitcast(BF16)
                        S.op("pe", e_tr([(PSMb[0:Q, 512 + g * P:512 + (g + 1) * P], xsb[:, 8 + g, c0:c0 + Q], identb[:, :])
                                         for g in range(2)]),
                             reads=[b("xsb"), b("identb")], writes=[b("PSM_bt")])
                        S.op("act", e_acp(Btm[0:Q, :], PSMb[0:Q, 512:768]), reads=[b("PSM_bt")], writes=[b("Btm")])
                        PT4 = PTb[0:Q, :].rearrange("p (k d) -> p k d", k=16)
                        dt_b = dtt[0:Q, :].unsqueeze(2).to_broadcast([Q, 16, 64])
                        S.op("dve", e_tt(xdt[0:Q, :].rearrange("p (k d) -> p k d", k=16), PT4, dt_b, ALU.mult),
                             reads=[b("PT"), b("dtt")], writes=[b("xdt")])
                        dec_b = sml[0:Q, 16:32].unsqueeze(2).to_broadcast([Q, 16, 64])
                        S.op("dve", e_tt(xdd[0:Q, :].rearrange("p (k d) -> p k d", k=16),
                                          xdt[0:Q, :].rearrange("p (k d) -> p k d", k=16), dec_b, ALU.mult),
                             reads=[b("xdt"), b("sml")], writes=[b("xdd")])

                        mark("ssd_tr")
                        for g in range(2):
                            S.op("pe", e_mm([(PSM[0:Q, 128:128 + Q], xsb[:, 8 + g, c0:c0 + Q], xsb[:, 10 + g, c0:c0 + Q], True, True)]),
                                 reads=[b("xsb")], writes=[b("PSM_cb")])
                            mark("ssd_cbmm")
                            if DBG_STOP == "exp1" and g == 1:
                                S.op("dve", e_memset(CBm[0:Q, 0:Q], 0.0), reads=[b("PSM_cb")], writes=[b("CBm")])
                                raise _Stop()
                            if DBG_STOP == "exp2" and g == 1:
                                S.op("dve", e_tt(CBm[0:Q, 0:Q], PSM[0:Q, 128:128 + Q], Vmat[0:Q, 0:Q], ALU.mult), reads=[b("cst")], writes=[b("CBm")])
                                raise _Stop()
                            S.op("dve", e_tt(CBm[0:Q, 0:Q], PSM[0:Q, 128:128 + Q], Vmat[0:Q, 0:Q], ALU.mult),
                                 reads=[b("PSM_cb"), b("cst")], writes=[b("CBm")])
                            mark("ssd_cb")
                            dA_b = dA[0:Q, g * 8:(g + 1) * 8].unsqueeze(2).to_broadcast([Q, 8, Q])
                            U_b = Umat[0:Q, 0:Q].unsqueeze(1).to_broadcast([Q, 8, Q])
                            S.op("dve", e_tt(Ubig[0:Q, :, 0:Q], U_b, dA_b, ALU.mult),
                                 reads=[b("dA"), b("cst")], writes=[b("Ubig")])
                            for hf in range(2):
                                bank, bk = (PSG0, b("PSG0")) if hf == 0 else (PSG1, b("PSG1"))
                                bank3 = bank[0:Q, :].rearrange("p (k l) -> p k l", k=4)
                                S.op("pe", e_mm([(bank3[:, k, 0:Q], Ubig[0:Q, hf * 4 + k, 0:Q], Vmat[0:Q, 0:Q], True, True)
                                                 for k in range(4)]),
                                     reads=[b("Ubig"), b("cst")], writes=[bk])
                                S.op("act", e_act(Lm[0:Q, hf * 4:(hf + 1) * 4, 0:Q], bank3[:, :, 0:Q], AF.Exp),
                                     reads=[bk], writes=[b("Lm")])
                            CB_b = CBm[0:Q, 0:Q].unsqueeze(1).to_broadcast([Q, 8, Q])
                            S.op("dve", e_tt(MT[0:Q, :, 0:Q], Lm[0:Q, :, 0:Q], CB_b, ALU.mult),
                                 reads=[b("Lm"), b("CBm")], writes=[b("MT")])
                            mark("ssd_L")
                            S.op("pe", e_mm([(PY[0:Q, k * 64:(k + 1) * 64], MT[0:Q, k, 0:Q],
                                              xdt[0:Q, g * 512 + k * 64:g * 512 + (k + 1) * 64], True, True) for k in range(8)]),
                                 reads=[b("MT"), b("xdt")], writes=[b("PY")])
                            S.op("pe", e_mm([(ps[4][0:Q, :], xsb[:, 10 + g, c0:c0 + Q], Sb[:, g * 512:(g + 1) * 512], True, True)]),
                                 reads=[b("xsb"), b("Sb")], writes=[b("PSG0")])
                            S.op("pe", e_mm([(ps[5][:, :], Btm[0:Q, g * P:(g + 1) * P], xdd[0:Q, g * 512:(g + 1) * 512], True, True)]),
                                 reads=[b("Btm"), b("xdd")], writes=[b("PSG1")])
                            mark("ssd_mm")
                            eA_b = sml[0:Q, g * 8:(g + 1) * 8].unsqueeze(2).to_broadcast([Q, 8, 64])
                            D_b = rows[0:Q, R_D + g * 8:R_D + (g + 1) * 8].unsqueeze(2).to_broadcast([Q, 8, 64])
                            S.op("dve", e_tt(yo[0:Q, :].rearrange("p (k d) -> p k d", k=8),
                                             ps[4][0:Q, :].rearrange("p (k d) -> p k d", k=8), eA_b, ALU.mult),
                                 reads=[b("PSG0"), b("sml")], writes=[b("yo")])
                            S.op("dve", e_tt(xD[0:Q, :].rearrange("p (k d) -> p k d", k=8),
                                             PT4[:, g * 8:(g + 1) * 8, :], D_b, ALU.mult),
                                 reads=[b("PT"), b("rows")], writes=[b("xD")])
                            S.op("dve", e_tt(yo[0:Q, :], yo[0:Q, :], xD[0:Q, :], ALU.add), reads=[b("yo"), b("xD")], writes=[b("yo")])
                            S.op("dve", e_tt(yo[0:Q, :], yo[0:Q, :], PY[0:Q, :], ALU.add), reads=[b("yo"), b("PY")], writes=[b("yo")])
                            mark("ssd_y")
                            ed_b = sml[:, 32 + g * 8:32 + (g + 1) * 8].unsqueeze(2).to_broadcast([P, 8, 64])
                            Sg = Sst[:, g * 512:(g + 1) * 512]
                            S.op("dve", e_tt(Sg.rearrange("p (k d) -> p k d", k=8), Sg.rearrange("p (k d) -> p k d", k=8),
                                              ed_b, ALU.mult), reads=[b("Sst"), b("sml")], writes=[b("Sst")])
                            S.op("dve", e_tt(Sg, Sg, ps[5][:, :], ALU.add), reads=[b("Sst"), b("PSG1")], writes=[b("Sst")])
                            S.op("act", e_acp(Sb[:, g * 512:(g + 1) * 512], Sg), reads=[b("Sst")], writes=[b("Sb")])
                            mark("ssd_st")
                            S.op("pe", e_mm([(ps[7][0:Q, :], hn_cur[:, k, c0:c0 + Q], Win[:, k, OFF_Z + g * 512:OFF_Z + (g + 1) * 512],
                                              k == 0, k == KC - 1) for k in range(KC)]),
                                 reads=[bhn, b("Win")], writes=[b("PY")])
                            sigmoid_act(tz[0:Q, :], ps[7][0:Q, :], [b("PY")], 0.0)
                            S.op("dve", e_tt(yz[0:Q, :], yo[0:Q, :], ps[7][0:Q, :], ALU.mult), reads=[b("yo"), b("PY")], writes=[b("yz")])
                            S.op("dve", e_tt(yz[0:Q, :], yz[0:Q, :], tz[0:Q, :], ALU.mult), reads=[b("yz"), b("tz")], writes=[b("yz")])
                            mark("ssd_z")
                            S.op("act", e_act(junk[0:Q, 0:512], yz[0:Q, :], AF.Square, accum=gss[0:Q, 0:1]),
                                 reads=[b("yz")], writes=[b("junk"), b("gss")])
                            S.op("act", e_act(gss[0:Q, 1:2], gss[0:Q, 0:1], AF.Ln, bias=EPS, scale=1.0 / 512),
                                 reads=[b("gss")], writes=[b("gss")])
                            S.op("act", e_act(gss[0:Q, 2:3], gss[0:Q, 1:2], AF.Exp, scale=-0.5), reads=[b("gss")], writes=[b("gss")])
                            S.op("act", e_act(yn[0:Q, :], yz[0:Q, :], AF.Copy, scale=gss[0:Q, 2:3]),
                                 reads=[b("yz"), b("gss")], writes=[b("yn")])
                            mark("ssd_n")
                            PSMy = PSMb[:, 512:512 + 4 * Q].rearrange("p (c q) -> p c q", c=4)
                            S.op("pe", e_tr([(PSMy[:, c, :], yn[0:Q, c * P:(c + 1) * P], identb[0:Q, 0:Q]) for c in range(4)]),
                                 reads=[b("yn"), b("identb")], writes=[b("PSM_bt")])
                            mark("ssd_ytr")
                            S.op("act", e_acp(yssd[:, g * 4:(g + 1) * 4, c0:c0 + Q], PSMy), reads=[b("PSM_bt")], writes=[b("yssd")])
                            mark("ssd_yev")

                def sec_out(ti):
                    t0 = ti * Tt
                    hn_cur = hnTs[ti % 2]
                    bhn = b("hnT%d" % (ti % 2))
                    for j in range(NSUB):
                        c0 = j * Q
                        xr = xin[j % 2]
                        bxr = b("xin%d" % (j % 2))
                        S.dma("sp", [(xr[0:Q, :], xsrc[t0 + c0:t0 + c0 + Q, :])], writes=[bxr])
                        for h2 in range(2):
                            items = []
                            for k in range(16):
                                lhs = ylru[:, k, c0:c0 + Q] if k < 8 else yssd[:, k - 8, c0:c0 + Q]
                                items.append((PA[h2][0:Q, :], lhs, Wout[:, k, h2 * 512:(h2 + 1) * 512], k == 0, k == 15))
                            S.op("pe", e_mm(items), reads=[b("ylru"), b("yssd"), b("Wout")], writes=[bPA[h2]])
                            S.op("dve", e_tt(osb[0:Q, h2 * 512:(h2 + 1) * 512], PA[h2][0:Q, :], gate[0:Q, h2 * 512:(h2 + 1) * 512], ALU.mult),
                                 reads=[bPA[h2], b("gate")], writes=[b("osb")])
                        if DBG_STOP == "dump_o1" and ti == DBG_TI and j == 0:
                            dbg = dout("dbg", [P, D])
                            stores.append(S.dma("sp", [(dbg, osb[:])], reads=[b("osb")], owner=b("dbg0")))
                            raise _Stop()
                        S.op("dve", e_tt(osb[0:Q, :], osb[0:Q, :], xr[0:Q, :], ALU.add), reads=[b("osb"), bxr], writes=[b("osb")])
                        if DBG_STOP == "dump_o2" and ti == DBG_TI and j == 0:
                            dbg = dout("dbg", [P, D])
                            stores.append(S.dma("sp", [(dbg, osb[:])], reads=[b("osb")], owner=b("dbg0")))
                            raise _Stop()
                        S.op("act", e_act(junk[0:Q, :], osb[0:Q, :], AF.Square, accum=ssq[0:Q, 4:5]),
                             reads=[b("osb")], writes=[b("junk"), b("ssq")])
                        S.op("act", e_act(ssq[0:Q, 5:6], ssq[0:Q, 4:5], AF.Ln, bias=EPS, scale=1.0 / D), reads=[b("ssq")], writes=[b("ssq")])
                        S.op("act", e_act(ssq[0:Q, 6:7], ssq[0:Q, 5:6], AF.Exp, scale=-0.5), reads=[b("ssq")], writes=[b("ssq")])
                        S.op("dve", e_stt(xr[0:Q, :], osb[0:Q, :], ssq[0:Q, 6:7], fng[0:Q, :], ALU.mult, ALU.mult),
                             reads=[b("osb"), b("ssq"), b("fng")], writes=[bxr])
                        if DBG_STOP == "dump_o3" and ti == DBG_TI and j == 0:
                            dbg = dout("dbg", [P, D + 16])
                            stores.append(S.dma("sp", [(dbg[:, 0:D], xr[:])], reads=[bxr], owner=b("dbg0")))
                            stores.append(S.dma("sp", [(dbg[:, D:D + 16], ssq[:])], reads=[b("ssq")], owner=b("dbg1")))
                            raise _Stop()
                        stores.append(S.dma("sp", [(ydst[t0 + c0:t0 + c0 + Q, :], xr[0:Q, :])], reads=[bxr],
                                            owner=b("xout%d" % (j % 2))))


                def rec(fn, ti):
                    S.begin()
                    fn(ti)
                    return S.end()

                S.replay([rec(sec_stage1, 0)])
                for ti in range(NT):
                    S.replay([rec(sec_out, ti - 1) if ti > 0 else [], rec(sec_xbc, ti)])
                    S.replay([rec(sec_lru, ti), rec(sec_ssd, ti), rec(sec_stage1, ti + 1) if ti + 1 < NT else []])
                mark("ssd")
                S.replay([rec(sec_out, NT - 1)])
                Tlast = (NT - 1) % 2
                hnT = hnTs[Tlast]

                mark("out")
                for blk in range(5):
                    col0 = OFF_LX + blk * 512 if blk < 2 else OFF_XBC + (blk - 2) * 512
                    S.op("pe", e_mm([(PY[0:3, :], hnT[:, k, Tt - 3:Tt], Win[:, k, col0:col0 + 512], k == 0, k == KC - 1)
                                     for k in range(KC)]),
                         reads=[b("hnT%d" % Tlast), b("Win")], writes=[b("PY")])
                    stgt, stgb = (yo, b("yo")) if blk % 2 == 0 else (xD, b("xD"))
                    S.op("act", e_acp(stgt[0:3, :], PY[0:3, :]), reads=[b("PY")], writes=[stgb])
                    dst = o_lc[grp][gi][:, blk * 512:(blk + 1) * 512] if blk < 2 else o_sc[grp][gi][:, (blk - 2) * 512:(blk - 1) * 512]
                    stores.append(S.dma("sp", [(dst, stgt[0:3, :])], reads=[stgb], owner=b("cso%d" % (blk % 2))))
                S.op("pe", e_tr([(PSG1[0:8, 0:P], hprev[:, 0:8], ident)]), reads=[b("hprev"), b("cst")], writes=[b("PSG1")])
                S.op("act", e_acp(gss[0:8, :].bitcast(F32) if False else hT[0:8, :], PSG1[0:8, 0:P]), reads=[b("PSG1")], writes=[b("hT")])
                stores.append(S.dma("sp", [(o_lh[grp][gi].rearrange("(c p) -> c p", p=P), hT[0:8, :])], reads=[b("hT")]))
                for c in range(8):
                    S.op("pe", e_tr([(PSG0[:, 0:P], Sst[:, c * P:(c + 1) * P], ident)]), reads=[b("Sst"), b("cst")], writes=[b("PSG0")])
                    S.op("act", e_acp(sto[:], PSG0[:, 0:P]), reads=[b("PSG0")], writes=[b("sto")])
                    stores.append(S.dma("sp", [(o_ss[grp][gi, c * P:(c + 1) * P, :], sto[:])], reads=[b("sto")]))
                mark("seqend")
        except _Stop:
            pass
        S.finish("sp", stores)
        S.run()
    return nc


def make_consts():
    c = np.zeros((P, 4 * P), np.float32)
    j = np.arange(P)[:, None]
    s = np.arange(P)[None, :]
    c[:, 0:P] = (j == s)
    c[:, P:2 * P] = (j > s)
    c[:, 2 * P:3 * P] = (j <= s)
    c[:, 3 * P:4 * P] = 1.0
    return c


def core_inputs(inp, pidx, sidx):
    f = lambda a: np.ascontiguousarray(np.asarray(a, dtype=np.float32))
    m = {
        "xp": f(inp["x_prompt"][pidx]),
        "xs": f(inp["x_sample"][sidx]),
        "cc": f(np.concatenate([inp["c_prompt"][pidx], inp["c_sample"][sidx]], axis=0)),
        "st_lc": f(inp["state_lru_conv"][0][sidx]),
        "st_lh": f(inp["state_lru_h"][0][sidx]),
        "st_sc": f(inp["state_ssd_conv"][0][sidx]),
        "st_ss": f(np.asarray(inp["state_ssd"][0][sidx]).reshape(len(sidx), D, P)),
        "consts": make_consts(),
    }
    for k in ("norm_g", "w_ada", "b_ada", "w_in", "lru_conv_w", "lru_conv_b", "lru_w_a", "lru_b_a", "lru_w_x",
              "lru_b_x", "lru_lambda", "ssd_conv_w", "ssd_conv_b", "ssd_dt_bias", "ssd_a_log", "ssd_d",
              "ssd_norm_g", "w_out"):
        m[k] = f(np.asarray(inp[k])[0])
    m["final_norm_g"] = f(inp["final_norm_g"])
    return m


_NC_CACHE = {}


def kernel(**inputs):
    B_, L_ = inputs["x_prompt"].shape[:2]
    DB, DL = inputs["x_sample"].shape[:2]
    n = NCORES
    NP, NS = B_ // n, DB // n
    key = (NP, L_, NS, DL)
    if key not in _NC_CACHE:
        _NC_CACHE[key] = build_program(NP, L_, NS, DL)
    nc = _NC_CACHE[key]
    in_maps = []
    for i in range(n):
        pidx = list(range(i * NP, (i + 1) * NP))
        sidx = list(range(i * NS, (i + 1) * NS))
        in_maps.append(core_inputs(inputs, pidx, sidx))
    res = run_bass_kernel_spmd(nc, in_maps, core_ids=list(range(n)))
    R = res.results
    cat = lambda name: np.concatenate([np.asarray(r[name], dtype=np.float32) for r in R], axis=0)
    y_p = cat("yp")
    y_s = cat("ys")
    outs = [y_p, y_s]
    for sfx, nb in (("p", B_), ("s", DB)):
        outs.append(cat("o_lc_" + sfx)[None])
        outs.append(cat("o_lh_" + sfx)[None])
        outs.append(cat("o_sc_" + sfx)[None])
        outs.append(cat("o_ss_" + sfx).reshape(1, nb, 2, 8, 64, 128))
    return tuple(outs)
```
